# Optimizing a Trainium2 kernel written in Bass

```python
import jax, jax.numpy as jnp
from jax import lax
import numpy as np

D_MODEL = 2048
BATCH = 16
SEQ = 2048
DEPTH = 2
DEC_BATCH = 16
DEC_SEQ = 64
PAST_LEN = 2048

CHUNK = 64
HEAD_DIM = 128
A_HEADS = 8
A_KV_HEADS = 2
A_GROUP = A_HEADS // A_KV_HEADS
WINDOW = 128
B_HEADS = 8
B_PAST_CHUNKS = 8
B_BAND_PAST = B_PAST_CHUNKS * CHUNK
REL_CLIP = 128
C_HEADS = 16
C_Q_RANK = 768
C_KV_RANK = 512
C_NOPE = 128
C_ROPE = 64
C_V = 128
Q_BLOCK = 128
D_FF = 4 * D_MODEL
PLE_DIM = 256
ROPE_THETA = 10000.0
LN_EPS = 1e-5
RMS_EPS = 1e-6
NEG_INF = -1e30
DEEPNORM_ALPHA = (2 * DEPTH) ** 0.25
DEEPNORM_BETA = (8 * DEPTH) ** -0.25
N_AB_LAYERS = (DEPTH + 1) // 2
N_C_LAYERS = DEPTH // 2
A_Q_W = A_HEADS * HEAD_DIM
A_KV_W = A_KV_HEADS * HEAD_DIM
B_W = B_HEADS * HEAD_DIM
AB_IN_W = A_Q_W + 2 * A_KV_W + 3 * B_W
AB_MIX_W = A_Q_W + B_W
C_IN_W = C_Q_RANK + C_KV_RANK + C_ROPE

kernel_name = 'chunk_streaming_hybrid_swa_band_mla'


def layer_norm(x, g, b):
    xf = x.astype(jnp.float32)
    mu = jnp.mean(xf, -1, keepdims=True)
    var = jnp.mean(jnp.square(xf - mu), -1, keepdims=True)
    return ((xf - mu) * lax.rsqrt(var + LN_EPS) * g.astype(jnp.float32) + b.astype(jnp.float32)).astype(x.dtype)


def rms_norm(x, g):
    xf = x.astype(jnp.float32)
    return (xf * lax.rsqrt(jnp.mean(jnp.square(xf), -1, keepdims=True) + RMS_EPS) * g.astype(jnp.float32)).astype(x.dtype)


def rope(x, pos0):
    t, d = x.shape[1], x.shape[-1]
    half = d // 2
    inv = ROPE_THETA ** (-jnp.arange(half, dtype=jnp.float32) * (2.0 / d))
    ang = (jnp.arange(t, dtype=jnp.float32) + pos0)[:, None] * inv[None, :]
    cos = jnp.cos(ang)[None, :, None, :]
    sin = jnp.sin(ang)[None, :, None, :]
    xf = x.astype(jnp.float32)
    x1, x2 = xf[..., :half], xf[..., half:]
    return jnp.concatenate([x1 * cos - x2 * sin, x2 * cos + x1 * sin], -1).astype(x.dtype)


def keep_newest(past, new, cap):
    full = jnp.concatenate([past, new], 1)
    return full[:, -min(cap, full.shape[1]):]


def rel_position_bias(table, band_past):
    r = jnp.arange(CHUNK)[:, None]
    m = jnp.arange(band_past + CHUNK)[None, :]
    idx = jnp.clip(band_past + r - m, -REL_CLIP, REL_CLIP) + REL_CLIP
    return table[:, idx][:, None]


def band_attention(q, k_past, v_past, k_new, v_new, band_past, bias=None, sinks=None):
    b, t, nkv, g, d = q.shape
    n_past = k_past.shape[1]
    nc = -(-t // CHUNK)
    tp = nc * CHUNK
    pad_l = band_past - n_past
    pad_p = ((0, 0), (pad_l, 0), (0, 0), (0, 0))
    pad_n = ((0, 0), (0, tp - t), (0, 0), (0, 0))
    k_all = jnp.concatenate([jnp.pad(k_past, pad_p), jnp.pad(k_new, pad_n)], 1)
    v_all = jnp.concatenate([jnp.pad(v_past, pad_p), jnp.pad(v_new, pad_n)], 1)
    j = jnp.arange(band_past + tp)
    valid = (j >= pad_l) & (j < band_past + t)
    q_blocks = jnp.pad(q, ((0, 0), (0, tp - t), (0, 0), (0, 0), (0, 0))).reshape(b, nc, CHUNK, nkv, g, d).swapaxes(0, 1)
    lb = band_past + CHUNK
    scale = d ** -0.5

    def one_chunk(args):
        c, qc = args
        start = c * CHUNK
        kb = lax.dynamic_slice_in_dim(k_all, start, lb, axis=1)
        vb = lax.dynamic_slice_in_dim(v_all, start, lb, axis=1)
        vm = lax.dynamic_slice_in_dim(valid, start, lb)
        s = jnp.einsum('bckgd,blkd->bkgcl', qc, kb).astype(jnp.float32) * scale
        if bias is not None:
            s = s + bias.astype(jnp.float32)
        s = jnp.where(vm, s, NEG_INF)
        m = jnp.max(s, -1, keepdims=True)
        if sinks is not None:
            sk = sinks.astype(jnp.float32)[None, :, :, None, None]
            m = jnp.maximum(m, sk)
        e = jnp.exp(s - m)
        den = jnp.sum(e, -1, keepdims=True)
        if sinks is not None:
            den = den + jnp.exp(sk - m)
        return jnp.einsum('bkgcl,blkd->bckgd', (e / den).astype(vb.dtype), vb)

    out = lax.map(one_chunk, (jnp.arange(nc), q_blocks))
    return out.swapaxes(0, 1).reshape(b, tp, nkv * g * d)[:, :t]


def chunk_causal_attention(q, k, v, q_pos0):
    b, t, h, dq = q.shape
    s_len = k.shape[1]
    blk = min(Q_BLOCK, t)
    nb = -(-t // blk)
    tp = nb * blk
    q_blocks = jnp.pad(q, ((0, 0), (0, tp - t), (0, 0), (0, 0))).reshape(b, nb, blk, h, dq).swapaxes(0, 1)
    q_chunk = ((q_pos0 + jnp.arange(tp)) // CHUNK).reshape(nb, blk)
    k_chunk = jnp.arange(s_len) // CHUNK
    scale = dq ** -0.5

    def one_block(args):
        qb, qc = args
        s = jnp.einsum('bqhd,bkhd->bhqk', qb, k).astype(jnp.float32) * scale
        s = jnp.where(k_chunk[None, :] <= qc[:, None], s, NEG_INF)
        p = jax.nn.softmax(s, axis=-1).astype(v.dtype)
        return jnp.einsum('bhqk,bkhd->bqhd', p, v)

    out = lax.map(one_block, (q_blocks, q_chunk))
    return out.swapaxes(0, 1).reshape(b, tp, h * v.shape[-1])[:, :t]


def mixer_ab(x, pos0, past_ak, past_av, past_bk, past_bv, w_in, sinks, rel_bias, w_out):
    b, t, _ = x.shape
    h = x @ w_in
    o1 = A_Q_W
    o2 = o1 + A_KV_W
    o3 = o2 + A_KV_W
    o4 = o3 + B_W
    o5 = o4 + B_W
    qa = rope(h[..., :o1].reshape(b, t, A_HEADS, HEAD_DIM), pos0).reshape(b, t, A_KV_HEADS, A_GROUP, HEAD_DIM)
    ka = rope(h[..., o1:o2].reshape(b, t, A_KV_HEADS, HEAD_DIM), pos0)
    va = h[..., o2:o3].reshape(b, t, A_KV_HEADS, HEAD_DIM)
    qb = h[..., o3:o4].reshape(b, t, B_HEADS, 1, HEAD_DIM)
    kb = h[..., o4:o5].reshape(b, t, B_HEADS, HEAD_DIM)
    vb = h[..., o5:].reshape(b, t, B_HEADS, HEAD_DIM)
    out_a = band_attention(qa, past_ak, past_av, ka, va, WINDOW, sinks=sinks.reshape(A_KV_HEADS, A_GROUP))
    out_b = band_attention(qb, past_bk, past_bv, kb, vb, B_BAND_PAST, bias=rel_position_bias(rel_bias, B_BAND_PAST))
    out = jnp.concatenate([out_a, out_b], -1) @ w_out
    states = (keep_newest(past_ak, ka, WINDOW), keep_newest(past_av, va, WINDOW),
              keep_newest(past_bk, kb, B_BAND_PAST), keep_newest(past_bv, vb, B_BAND_PAST))
    return out, states


def mixer_c(x, pos0, past_ckv, past_kr, w_in, g_q, w_q_b, g_kv, w_kv_b, w_out):
    b, t, _ = x.shape
    h = x @ w_in
    cq = rms_norm(h[..., :C_Q_RANK], g_q)
    ckv = rms_norm(h[..., C_Q_RANK:C_Q_RANK + C_KV_RANK], g_kv)
    kr = rope(h[..., C_Q_RANK + C_KV_RANK:][:, :, None, :], pos0)[:, :, 0]
    q = (cq @ w_q_b).reshape(b, t, C_HEADS, C_NOPE + C_ROPE)
    q = jnp.concatenate([q[..., :C_NOPE], rope(q[..., C_NOPE:], pos0)], -1)
    ckv_all = jnp.concatenate([past_ckv, ckv], 1)
    kr_all = jnp.concatenate([past_kr, kr], 1)
    s_len = ckv_all.shape[1]
    kv = (ckv_all @ w_kv_b).reshape(b, s_len, C_HEADS, C_NOPE + C_V)
    k = jnp.concatenate([kv[..., :C_NOPE], jnp.broadcast_to(kr_all[:, :, None, :], (b, s_len, C_HEADS, C_ROPE))], -1)
    o = chunk_causal_attention(q, k, kv[..., C_NOPE:], pos0)
    return o @ w_out, (ckv, kr)


def sq_relu_mlp(x, w_up, w_down):
    return jnp.square(jax.nn.relu(x @ w_up)) @ w_down


def run_trunk(x, p, pos0, past_a_k, past_a_v, past_b_k, past_b_v, past_c_kv, past_c_kr, w):
    a_k, a_v, b_k, b_v, c_kv, c_kr = [], [], [], [], [], []
    for i in range(DEPTH):
        j = i // 2
        if i % 2 == 0:
            mix, (ak, av, bk, bv) = mixer_ab(x, pos0, past_a_k[j], past_a_v[j], past_b_k[j], past_b_v[j],
                                             w['w_in_ab'][j], w['sinks_a'][j], w['rel_bias_b'][j], w['w_out_ab'][j])
            a_k.append(ak)
            a_v.append(av)
            b_k.append(bk)
            b_v.append(bv)
        else:
            mix, (ckv, ckr) = mixer_c(x, pos0, past_c_kv[j], past_c_kr[j], w['w_in_c'][j], w['g_q_c'][j],
                                      w['w_q_b_c'][j], w['g_kv_c'][j], w['w_kv_b_c'][j], w['w_out_c'][j])
            c_kv.append(ckv)
            c_kr.append(ckr)
        x = layer_norm(DEEPNORM_ALPHA * x + mix, w['ln1_g'][i], w['ln1_b'][i])
        x = layer_norm(DEEPNORM_ALPHA * x + sq_relu_mlp(x, w['w_mlp_up'][i], w['w_mlp_down'][i]),
                       w['ln2_g'][i], w['ln2_b'][i])
        gate = jax.nn.sigmoid((x @ w['w_ple_gate'][i] + w['b_ple_gate'][i]).astype(jnp.float32)).astype(x.dtype)
        x = x + gate * (p[i] @ w['w_ple'][i])
    return x, jnp.stack(a_k), jnp.stack(a_v), jnp.stack(b_k), jnp.stack(b_v), jnp.stack(c_kv), jnp.stack(c_kr)


def setup_inputs(seed: int = 0) -> dict:
    key = jax.random.key(seed)
    ks = jax.random.split(key, 32)

    def nrm(k, shape, scale=1.0):
        return jax.random.normal(k, shape, jnp.float32) * scale

    a_cache = min(WINDOW, PAST_LEN)
    b_cache = min(B_BAND_PAST, PAST_LEN)
    return {
        'x_prompt': nrm(ks[0], (BATCH, SEQ, D_MODEL)),
        'x_sample': nrm(ks[1], (DEC_BATCH, DEC_SEQ, D_MODEL)),
        'cache_a_k': nrm(ks[2], (N_AB_LAYERS, DEC_BATCH, a_cache, A_KV_HEADS, HEAD_DIM)),
        'cache_a_v': nrm(ks[3], (N_AB_LAYERS, DEC_BATCH, a_cache, A_KV_HEADS, HEAD_DIM)),
        'cache_b_k': nrm(ks[4], (N_AB_LAYERS, DEC_BATCH, b_cache, B_HEADS, HEAD_DIM)),
        'cache_b_v': nrm(ks[5], (N_AB_LAYERS, DEC_BATCH, b_cache, B_HEADS, HEAD_DIM)),
        'cache_c_kv': nrm(ks[6], (N_C_LAYERS, DEC_BATCH, PAST_LEN, C_KV_RANK)),
        'cache_c_krope': nrm(ks[7], (N_C_LAYERS, DEC_BATCH, PAST_LEN, C_ROPE)),
        'p_prompt': nrm(ks[8], (DEPTH, BATCH, SEQ, PLE_DIM)),
        'p_sample': nrm(ks[9], (DEPTH, DEC_BATCH, DEC_SEQ, PLE_DIM)),
        'w_in_ab': nrm(ks[10], (N_AB_LAYERS, D_MODEL, AB_IN_W), D_MODEL ** -0.5),
        'sinks_a': nrm(ks[11], (N_AB_LAYERS, A_HEADS), 0.5),
        'rel_bias_b': nrm(ks[12], (N_AB_LAYERS, B_HEADS, 2 * REL_CLIP + 1), 0.1),
        'w_out_ab': nrm(ks[13], (N_AB_LAYERS, AB_MIX_W, D_MODEL), DEEPNORM_BETA * AB_MIX_W ** -0.5),
        'w_in_c': nrm(ks[14], (N_C_LAYERS, D_MODEL, C_IN_W), D_MODEL ** -0.5),
        'g_q_c': 1.0 + nrm(ks[15], (N_C_LAYERS, C_Q_RANK), 0.02),
        'w_q_b_c': nrm(ks[16], (N_C_LAYERS, C_Q_RANK, C_HEADS * (C_NOPE + C_ROPE)), C_Q_RANK ** -0.5),
        'g_kv_c': 1.0 + nrm(ks[17], (N_C_LAYERS, C_KV_RANK), 0.02),
        'w_kv_b_c': nrm(ks[18], (N_C_LAYERS, C_KV_RANK, C_HEADS * (C_NOPE + C_V)), C_KV_RANK ** -0.5),
        'w_out_c': nrm(ks[19], (N_C_LAYERS, C_HEADS * C_V, D_MODEL), DEEPNORM_BETA * (C_HEADS * C_V) ** -0.5),
        'ln1_g': 1.0 + nrm(ks[20], (DEPTH, D_MODEL), 0.02),
        'ln1_b': nrm(ks[21], (DEPTH, D_MODEL), 0.02),
        'ln2_g': 1.0 + nrm(ks[22], (DEPTH, D_MODEL), 0.02),
        'ln2_b': nrm(ks[23], (DEPTH, D_MODEL), 0.02),
        'w_mlp_up': nrm(ks[24], (DEPTH, D_MODEL, D_FF), D_MODEL ** -0.5),
        'w_mlp_down': nrm(ks[25], (DEPTH, D_FF, D_MODEL), DEEPNORM_BETA * D_FF ** -0.5),
        'w_ple_gate': nrm(ks[26], (DEPTH, D_MODEL, D_MODEL), D_MODEL ** -0.5),
        'b_ple_gate': nrm(ks[27], (DEPTH, D_MODEL), 0.02),
        'w_ple': nrm(ks[28], (DEPTH, PLE_DIM, D_MODEL), PLE_DIM ** -0.5),
    }


def reference(x_prompt, x_sample, cache_a_k, cache_a_v, cache_b_k, cache_b_v, cache_c_kv, cache_c_krope,
              p_prompt, p_sample, w_in_ab, sinks_a, rel_bias_b, w_out_ab, w_in_c, g_q_c, w_q_b_c, g_kv_c,
              w_kv_b_c, w_out_c, ln1_g, ln1_b, ln2_g, ln2_b, w_mlp_up, w_mlp_down, w_ple_gate, b_ple_gate, w_ple):
    w = {
        'w_in_ab': w_in_ab, 'sinks_a': sinks_a, 'rel_bias_b': rel_bias_b, 'w_out_ab': w_out_ab,
        'w_in_c': w_in_c, 'g_q_c': g_q_c, 'w_q_b_c': w_q_b_c, 'g_kv_c': g_kv_c, 'w_kv_b_c': w_kv_b_c,
        'w_out_c': w_out_c, 'ln1_g': ln1_g, 'ln1_b': ln1_b, 'ln2_g': ln2_g, 'ln2_b': ln2_b,
        'w_mlp_up': w_mlp_up, 'w_mlp_down': w_mlp_down, 'w_ple_gate': w_ple_gate, 'b_ple_gate': b_ple_gate,
        'w_ple': w_ple,
    }
    bp = x_prompt.shape[0]
    dt = x_prompt.dtype
    empty_a = jnp.zeros((N_AB_LAYERS, bp, 0, A_KV_HEADS, HEAD_DIM), dt)
    empty_b = jnp.zeros((N_AB_LAYERS, bp, 0, B_HEADS, HEAD_DIM), dt)
    empty_c = jnp.zeros((N_C_LAYERS, bp, 0, C_KV_RANK), dt)
    empty_r = jnp.zeros((N_C_LAYERS, bp, 0, C_ROPE), dt)
    y_prompt, pa_k, pa_v, pb_k, pb_v, pc_kv, pc_kr = run_trunk(
        x_prompt, p_prompt, 0, empty_a, empty_a, empty_b, empty_b, empty_c, empty_r, w)
    y_sample, sa_k, sa_v, sb_k, sb_v, sc_kv, sc_kr = run_trunk(
        x_sample, p_sample, PAST_LEN, cache_a_k, cache_a_v, cache_b_k, cache_b_v, cache_c_kv, cache_c_krope, w)
    return (y_prompt, y_sample, pa_k, pa_v, pb_k, pb_v, pc_kv, pc_kr, sa_k, sa_v, sb_k, sb_v, sc_kv, sc_kr)
```

```python
import numpy as np
import ml_dtypes
from bisect import bisect_right
from contextlib import ExitStack
import concourse.bass as bass
import concourse.mybir as mybir
from concourse.bass_utils import run_bass_kernel_spmd

F32 = mybir.dt.float32
BF16 = mybir.dt.bfloat16
AF = mybir.ActivationFunctionType
ALU = mybir.AluOpType
AX = mybir.AxisListType

D = 2048
SEQ = 2048
DEC = 64
PAST = 2048
ALPHA = float(4 ** 0.25)
LN_EPS = 1e-5
RMS_EPS = 1e-6
SC_AB = float(128 ** -0.5)
SC_C = float(192 ** -0.5)
NCORES = 8
ESZ = {F32: 4, BF16: 2}
ENGS = ['pe', 'act', 'dve', 'pool', 'sp']
NDSEM = 24
NSLOT = 4
LOOK = 2
PE_ELEMS = 4096


def cap(ap, pattern):
    return bass.AP(ap.tensor, ap.offset, [list(ap.ap[0])] + [list(p) for p in pattern])


class Op:
    __slots__ = ('eng', 'fn', 'deps', 'ms', 'dsem', 'dval', 'isdma', 'has_dep')

    def __init__(self, eng, fn, isdma=False):
        self.eng = eng
        self.fn = fn
        self.deps = set()
        self.ms = None
        self.isdma = isdma
        self.dsem = None
        self.dval = None
        self.has_dep = False


class Tracker:
    def __init__(self):
        self.ops = {e: [] for e in ENGS}
        self.seg = {}
        self.rowb = {}
        self.ndma = 0

    def register(self, name, rowbytes):
        self.seg[name] = ([0], [[0, 1 << 62, None, {}]])
        self.rowb[name] = rowbytes

    def _region(self, ap):
        name = ap.tensor.name
        if name not in self.seg:
            return None
        es = ESZ[ap.dtype]
        pat = ap.ap
        rb = self.rowb[name]
        if rb is None:
            off = ap.offset
            ext = 1
            for st, cn in pat:
                ext += (cn - 1) * abs(st)
        else:
            re_ = rb // es
            off = ap.offset % re_
            ext = 1
            for st, cn in pat[1:]:
                ext += (cn - 1) * abs(st)
            if name == 'psum':
                lo = (off * es) // 2048 * 2048
                hi = ((off + ext) * es + 2047) // 2048 * 2048
                return name, lo, hi
        return name, off * es, (off + ext) * es

    def _access(self, reg, write, op):
        name, lo, hi = reg
        starts, segs = self.seg[name]
        i = bisect_right(starts, lo) - 1
        s = segs[i]
        if s[0] < lo:
            new = [lo, s[1], s[2], dict(s[3])]
            s[1] = lo
            segs.insert(i + 1, new)
            starts.insert(i + 1, lo)
            i += 1
        while i < len(segs) and segs[i][0] < hi:
            s = segs[i]
            if s[1] > hi:
                new = [hi, s[1], s[2], dict(s[3])]
                s[1] = hi
                segs.insert(i + 1, new)
                starts.insert(i + 1, hi)
            if s[2] is not None:
                op.deps.add(s[2])
            if write:
                for r in s[3].values():
                    op.deps.add(r)
                s[2] = op
                s[3] = {}
            else:
                key = ('d', id(op)) if op.isdma else op.eng
                s[3][key] = op
            i += 1

    def op(self, eng, fn, r=(), w=()):
        o = Op(eng, fn)
        for ap in r:
            reg = self._region(ap)
            if reg is not None:
                self._access(reg, reg[0] == 'psum', o)
        for ap in w:
            reg = self._region(ap)
            if reg is not None:
                self._access(reg, True, o)
        o.deps.discard(o)
        self.ops[eng].append(o)
        return o

    def dma(self, out, in_):
        o = Op('sp', (lambda e, out=out, in_=in_: e.dma_start(out=out, in_=in_)), isdma=True)
        o.dsem = self.ndma % NDSEM
        o.dval = 16 * (self.ndma // NDSEM + 1)
        self.ndma += 1
        reg = self._region(in_)
        if reg is not None:
            self._access(reg, False, o)
        reg = self._region(out)
        if reg is not None:
            self._access(reg, True, o)
        o.deps.discard(o)
        self.ops['sp'].append(o)
        return o

    def finalize(self):
        for e in ENGS:
            for o in self.ops[e]:
                for d in o.deps:
                    d.has_dep = True
        self.count = {}
        for e in ENGS:
            c = 0
            for o in self.ops[e]:
                if o.has_dep and not o.isdma:
                    c += 1
                    o.ms = c
            self.count[e] = c

    def emit(self, e, engobj, esem, dsem):
        known = {f: 0 for f in ENGS}
        dknown = {}
        for o in self.ops[e]:
            need = {}
            dneed = {}
            for d in o.deps:
                if d.isdma:
                    if dneed.get(d.dsem, 0) < d.dval:
                        dneed[d.dsem] = d.dval
                else:
                    if d.eng == e and e in ('pe', 'sp'):
                        continue
                    if need.get(d.eng, 0) < d.ms:
                        need[d.eng] = d.ms
            if o.isdma and o.dval > 16:
                if dneed.get(o.dsem, 0) < o.dval - 16:
                    dneed[o.dsem] = o.dval - 16
            for f, v in need.items():
                if known[f] < v:
                    engobj.wait_ge(esem[f], v)
                    known[f] = v
            for k, v in dneed.items():
                if dknown.get(k, 0) < v:
                    engobj.wait_ge(dsem[k], v)
                    dknown[k] = v
            ins = o.fn(engobj)
            if o.isdma:
                ins.then_inc(dsem[o.dsem], 16)
            elif o.ms is not None:
                ins.then_inc(esem[e], 1)


class Ring:
    def __init__(self, items):
        self.items = items
        self.i = 0

    def next(self):
        x = self.items[self.i % len(self.items)]
        self.i += 1
        return x


def panel_specs():
    sp = {
        'in_ab': (18, 4096), 'out_ab': (8, 4096), 'in_c': (6, 4096), 'q_b': (8, 3072), 'kv_b': (8, 2048),
        'out_c': (8, 4096),
    }
    for l in range(2):
        sp['up%d' % l] = (32, 4096)
        sp['dn%d' % l] = (32, 4096)
        sp['gate%d' % l] = (8, 4096)
        sp['ple%d' % l] = (1, 4096)
    return sp


def tile_pseq(kind):
    seq = []
    seq += [('in_ab', i) for i in range(18)] + [('out_ab', i) for i in range(8)]
    seq += [('up0', i) for i in range(32)] + [('dn0', i) for i in range(32)] + [('gate0', i) for i in range(8)]
    seq += [('in_c', i) for i in range(6)]
    for _ in range(2 if kind == 'sample' else 1):
        for hp in range(8):
            seq += [('q_b', hp), ('kv_b', hp)]
    seq += [('out_c', i) for i in range(8)]
    seq += [('up1', i) for i in range(32)] + [('dn1', i) for i in range(32)] + [('gate1', i) for i in range(8)]
    return seq


def build(tiles, TM=256, cfg=None):
    cfg = cfg or {}
    UPTO = cfg.get('upto', 'all')
    ORDER = ['setup', 'pre1', 'prepass', 'x', 'l0in', 'l0attn', 'l0out', 'l0ln1', 'l0mlp', 'l0gate', 'l1in', 'l1heads', 'l1out', 'all']
    def enabled(stage):
        return ORDER.index(stage) <= ORDER.index(UPTO)
    NTBM = TM // 128
    nc = bass.Bass("TRN2", target_bir_lowering=False, dynamic_dma_scratch_size=256)
    tr = Tracker()
    es = ExitStack()

    def din(name, shape, dt=F32):
        return nc.dram_tensor(name, list(shape), dt, kind="ExternalInput").ap()

    def dout(name, shape):
        return nc.dram_tensor(name, list(shape), F32, kind="ExternalOutput").ap()

    xp = din('xp', [2, SEQ, D])
    xs = din('xs', [128, D])
    cak = din('cak', [2, 128, 256])
    cav = din('cav', [2, 128, 256])
    cbk = din('cbk', [2, 512, 1024])
    cbv = din('cbv', [2, 512, 1024])
    cckv = din('cckv', [2, PAST, 512])
    ckr = din('ckr', [2, PAST, 64])
    pp = din('pp', [2, 2, SEQ, 256])
    psm = din('psm', [2, 128, 256])
    w_in_ab = din('w_in_ab', [D, 4608])
    sinks = din('sinks', [1, 8])
    relb = din('relb', [8, 257])
    w_out_ab = din('w_out_ab', [D, D])
    w_in_c = din('w_in_c', [D, 1344])
    g_q = din('g_q', [1, 768])
    w_q_b = din('w_q_b', [768, 3072])
    g_kv = din('g_kv', [1, 512])
    w_kv_b = din('w_kv_b', [512, 4096])
    w_out_c = din('w_out_c', [D, D])
    ln1_g = din('ln1_g', [2, D])
    ln1_b = din('ln1_b', [2, D])
    ln2_g = din('ln2_g', [2, D])
    ln2_b = din('ln2_b', [2, D])
    w_up = din('w_up', [2, D, 8192])
    w_down = din('w_down', [2, 8192, D])
    w_gate = din('w_gate', [2, D, D])
    b_gate = din('b_gate', [2, D])
    w_ple = din('w_ple', [2, 256, D])
    c_identb = din('c_identb', [128, 128], BF16)
    c_identf = din('c_identf', [128, 128])
    c_antiI = din('c_antiI', [128, 128])
    c_ones = din('c_ones', [128, 128])
    c_maskA = din('c_maskA', [128, 256])
    c_maskB = din('c_maskB', [128, 640])
    c_maskC = din('c_maskC', [128, 128])
    c_ropeA = din('c_ropeA', [2112, 128])
    c_ropeCt = din('c_ropeCt', [2112, 64])
    c_ropeCf = din('c_ropeCf', [64, 2, 2112])

    yp = dout('yp', [2, SEQ, D])
    ys = dout('ys', [128, D])
    pak = dout('pak', [2, 128, 256])
    pav = dout('pav', [2, 128, 256])
    pbk = dout('pbk', [2, 512, 1024])
    pbv = dout('pbv', [2, 512, 1024])
    pckv = dout('pckv', [2, SEQ, 512])
    pckr = dout('pckr', [2, SEQ, 64])
    sak = dout('sak', [2, 128, 256])
    sav = dout('sav', [2, 128, 256])
    sbk = dout('sbk', [2, 512, 1024])
    sbv = dout('sbv', [2, 512, 1024])
    sckv = dout('sckv', [128, 512])
    sckr = dout('sckr', [128, 64])

    PSPEC = panel_specs()
    wsc = {}
    for name, (npan, el) in PSPEC.items():
        t = nc.dram_tensor('wsc_' + name, [npan, 128, el], BF16, kind="Internal").ap()
        wsc[name] = t
        tr.register(t.tensor.name, None)
    hh = nc.dram_tensor('hh_scr', [8, 768], F32, kind="Internal").ap()
    tr.register(hh.tensor.name, None)

    def sb(name, n, dt):
        t = es.enter_context(nc.sbuf_tensor(name, [128, n], dt))
        tr.register(name, n * ESZ[dt])
        return t

    identb = sb('identb', 128, BF16)
    identf = sb('identf', 128, F32)
    antiI = sb('antiI', 128, F32)
    onesm = sb('onesm', 128, F32)
    maskA = sb('maskA', 256, F32)
    maskC = sb('maskC', 128, F32)
    BM = sb('BM', 8 * 640, F32)
    lnp = sb('lnp', 160, F32)
    gq_b = sb('gq_b', 768, F32)
    gkv_b = sb('gkv_b', 512, F32)
    sinks_b = sb('sinks_b', 8, F32)
    nsinks_b = sb('nsinks_b', 8, F32)
    ropeA = sb('ropeA', NTBM * 128, F32)
    ropeCt = sb('ropeCt', NTBM * 64, F32)
    ropeCf = sb('ropeCf', 2 * TM, F32)
    xT32 = sb('xT32', 16 * TM, F32)
    A0 = sb('A0', 16 * TM, BF16)
    KB_W = 512 + TM
    kbT = sb('kbT', 8 * KB_W, BF16)
    NVB = 4 + NTBM
    vb = sb('vb', NVB * 1024, BF16)
    KA_W = 128 + TM
    kaT = sb('kaT', 2 * KA_W, BF16)
    NVA = 1 + NTBM
    va = sb('va', NVA * 256, BF16)
    ckvT = sb('ckvT', 4 * 2112, BF16)
    krT = sb('krT', 2112, BF16)
    arena = sb('arena', 16384, BF16)
    wring = sb('wring', NSLOT * PE_ELEMS, BF16)
    Pbuf = [sb('Pbuf%d' % i, 2112, BF16) for i in range(2)]
    PTb = [sb('PTb%d' % i, 1024, BF16) for i in range(4)]
    Dbuf = [sb('Dbuf%d' % i, 512, BF16) for i in range(4)]
    stats = sb('stats', 256, F32)
    hst = [sb('hst%d' % i, 256, F32) for i in range(4)]
    tmpf = [sb('tmpf%d' % i, 256, F32) for i in range(3)]
    tmpb = [sb('tmpb%d' % i, 256, BF16) for i in range(5)]
    xstage_all = sb('xstage', 4096, F32)
    xstage = [xstage_all[:, 0:2048], xstage_all[:, 2048:4096]]
    hc = xstage_all[:, 0:NTBM * 1344]
    lnt = sb('lnt', 6 * TM, F32)
    Sbuf = lnt[:, 0:1024]
    maskB = xstage_all[:, 256:896]
    pT = sb('pT', 2 * TM, BF16)
    cnew = sb('cnew', 4 * 128 + 128, BF16)

    psum = es.enter_context(nc.psum_tensor('psum', [128, 4096], F32))
    tr.register('psum', 4096 * 4)

    esem = {e: es.enter_context(nc.semaphore('es_' + e)) for e in ENGS}
    dsem = [es.enter_context(nc.semaphore('ds%d' % i)) for i in range(NDSEM)]

    def OP(eng, fn, r=(), w=()):
        return tr.op(eng, fn, r, w)

    def DMA(out, in_):
        return tr.dma(out, in_)

    ps_cur = [0]

    def psum_alloc(nb=1):
        if ps_cur[0] + nb > 8:
            ps_cur[0] = 0
        b = ps_cur[0]
        ps_cur[0] += nb
        return psum[:, b * 512:(b + nb) * 512]

    def psum_at(b, nb=1):
        return psum[:, b * 512:(b + nb) * 512]

    pools = {}
    C_SINGLES = [[4, 5, 6]]

    def pool_next(name, choices):
        i = pools.get(name, 0)
        pools[name] = i + 1
        return choices[i % len(choices)]

    pe_cnt = [0]
    marks = []

    def phase(label):
        marks.append((label, pe_cnt[0]))

    def mm_group(out, pairs, start=True, stop=True):
        n = len(pairs)
        pe_cnt[0] += n

        def fn(e, out=out, pairs=pairs):
            ins = None
            for i, (l, r) in enumerate(pairs):
                ins = e.matmul(out, lhsT=l, rhs=r, start=(start and i == 0), stop=(stop and i == n - 1))
            return ins
        rr = []
        for l, r in pairs:
            rr.append(l)
            rr.append(r)
        return OP('pe', fn, r=rr, w=[out])

    def transposes(outs_ins, ident):
        pe_cnt[0] += len(outs_ins)
        def fn(e):
            ins = None
            for o, i in outs_ins:
                k = i.shape[0]
                ins = e.transpose(out=o, in_=i, identity=ident[0:k, 0:k])
            return ins
        return OP('pe', fn, r=[i for _, i in outs_ins] + [ident[:]], w=[o for o, _ in outs_ins])

    def act(out, in_, func, r_extra=(), **kw):
        return OP('act', lambda e: e.activation(out=out, in_=in_, func=func, **kw), r=[in_] + list(r_extra),
                  w=[out] + ([kw['accum_out']] if 'accum_out' in kw else []))

    def tt(eng, out, in0, in1, op):
        return OP(eng, lambda e: e.tensor_tensor(out=out, in0=in0, in1=in1, op=op), r=[in0, in1], w=[out])

    def ts(eng, out, in0, s1, s2, op0, op1=None, r_extra=()):
        if op1 is None:
            return OP(eng, lambda e: e.tensor_scalar(out=out, in0=in0, scalar1=s1, scalar2=None, op0=op0),
                      r=[in0] + list(r_extra), w=[out])
        return OP(eng, lambda e: e.tensor_scalar(out=out, in0=in0, scalar1=s1, scalar2=s2, op0=op0, op1=op1),
                  r=[in0] + list(r_extra), w=[out])

    def stt(out, in0, scalar, in1, op0, op1, r_extra=()):
        return OP('dve', lambda e: e.scalar_tensor_tensor(out=out, in0=in0, scalar=scalar, in1=in1, op0=op0, op1=op1),
                  r=[in0, in1] + list(r_extra), w=[out])

    def cp(eng, out, in_):
        if eng == 'act':
            return OP('act', lambda e: e.activation(out=out, in_=in_, func=AF.Copy), r=[in_], w=[out])
        return OP(eng, lambda e: e.tensor_copy(out=out, in_=in_), r=[in_], w=[out])

    evac_rr = [0]

    def evac(out, in_):
        evac_rr[0] += 1
        return cp('act' if evac_rr[0] % 2 else 'dve', out, in_)

    xT32v = xT32[:].rearrange("p (c t) -> p c t", t=TM)
    A0v = A0[:].rearrange("p (c t) -> p c t", t=TM)
    kbTv = kbT[:].rearrange("p (h k) -> p h k", k=KB_W)
    vbv = vb[:].rearrange("p (b c) -> p b c", c=1024)
    kaTv = kaT[:].rearrange("p (h k) -> p h k", k=KA_W)
    vav = va[:].rearrange("p (b c) -> p b c", c=256)
    ckvTv = ckvT[:].rearrange("p (c k) -> p c k", k=2112)
    BMv = BM[:].rearrange("p (h k) -> p h k", k=640)
    hcv = hc.rearrange("p (b c) -> p b c", c=1344)
    ropeAv = ropeA[:].rearrange("p (b c) -> p b c", c=128)
    ropeCtv = ropeCt[:].rearrange("p (b c) -> p b c", c=64)
    ropeCfv = ropeCf[:].rearrange("p (a t) -> p a t", t=TM)
    pTv = pT[:].rearrange("p (a t) -> p a t", t=TM)
    qaTv = arena[:, 0:8 * TM].rearrange("p (h t) -> p h t", t=TM)
    qbTv = arena[:, 8 * TM:16 * TM].rearrange("p (h t) -> p h t", t=TM)
    hidTv = arena[:, 0:64 * TM].rearrange("p (c t) -> p c t", t=TM)
    plew = arena[:, 12288:16384].rearrange("p (k n) -> p k n", n=2048)
    cqTv = arena[:, 0:6 * TM].rearrange("p (c t) -> p c t", t=TM)
    a0 = 6 * TM
    KhT = [arena[:, a0 + i * 2112: a0 + (i + 1) * 2112] for i in range(2)]
    a0 += 2 * 2112
    Vh = [arena[:, a0 + i * 2176: a0 + (i + 1) * 2176].rearrange("p (b d) -> p b d", d=128) for i in range(2)]
    a0 += 2 * 2176
    qnT = [arena[:, a0 + i * TM: a0 + (i + 1) * TM] for i in range(2)]
    a0 += 2 * TM
    qrT = [arena[:, a0 + i * TM: a0 + (i + 1) * TM] for i in range(2)]
    a0 += 2 * TM
    assert a0 <= 16384
    stage32 = [arena[:, i * 8192:(i + 1) * 8192].bitcast(F32) for i in range(2)]
    cnewv = cnew[:, 0:512].rearrange("p (c t) -> p c t", t=128)
    krnew = cnew[:, 512:640]

    hst_r = Ring(hst)
    tmpf_r = Ring(tmpf)
    tmpb_r = Ring(tmpb)
    P_r = Ring(Pbuf)
    PS_r = Ring([Pbuf[0][:, 0:1024], Pbuf[0][:, 1056:2080], Pbuf[1][:, 0:1024], Pbuf[1][:, 1056:2080]])
    PT_r = Ring(PTb)
    D_r = Ring(Dbuf)
    st_i = [0]

    def stat(n):
        if st_i[0] + n > 256:
            st_i[0] = 0
        a = stats[:, st_i[0]:st_i[0] + n]
        st_i[0] += n
        return a

    def psb(ps):
        return ps.bitcast(BF16)

    def dump_all():
        if UPTO == 'all' and not cfg.get('dump'):
            return
        for nm, t in [('xT32', xT32), ('A0', A0), ('arena', arena), ('kbT', kbT), ('vb', vb), ('kaT', kaT), ('va', va),
                      ('xstage', xstage_all), ('lnp', lnp), ('BM', BM), ('ropeA', ropeA), ('ckvT', ckvT), ('krT', krT),
                      ('lnt', lnt), ('ropeCf', ropeCf), ('ropeCt', ropeCt), ('sinks_b', sinks_b)]:
            ap = t[:]
            d = nc.dram_tensor('dbg_' + nm, list(ap.shape), ap.dtype, kind="ExternalOutput").ap()
            DMA(d[:, :], ap)

    DMA(identb[:], c_identb[:, :])
    DMA(identf[:], c_identf[:, :])
    DMA(antiI[:], c_antiI[:, :])
    DMA(onesm[:], c_ones[:, :])
    DMA(maskA[:], c_maskA[:, :])
    DMA(maskB, c_maskB[:, :])
    DMA(maskC[:], c_maskC[:, :])
    DMA(sinks_b[:], sinks[0:1, :].partition_broadcast(128))
    DMA(gq_b[:], g_q[0:1, :].partition_broadcast(128))
    DMA(gkv_b[:], g_kv[0:1, :].partition_broadcast(128))
    ts('dve', nsinks_b[:], sinks_b[:], -1.0, None, ALU.mult)
    for hb_ in (kbT, vb, kaT, va):
        OP('pool', (lambda e, hb_=hb_: e.memset(hb_[:], 0.0)), w=[hb_[:]])
    vst0 = xstage[0][:, 0:128]
    vst1 = xstage[0][0:32, 128:256]
    for v, src in enumerate([ln1_g, ln1_b, ln2_g, ln2_b]):
        DMA(xstage[0][v * 32:(v + 1) * 32, 0:128], src.rearrange("l (c p) -> (l c) p", p=128))
    DMA(vst1, b_gate.rearrange("l (c p) -> (l c) p", p=128))
    ps = psum_alloc(1)
    transposes([(ps[:, 0:128], vst0), (ps[:, 128:160], vst1)], identf)
    cp('act', lnp[:], ps[:, 0:160])

    def lnp_col(v, l, c):
        i = v * 32 + l * 16 + c
        return lnp[:, i:i + 1]

    tb8 = Sbuf[0:8, 0:257]
    hhs = xstage[1][0:8, 0:768]
    DMA(tb8, relb[:, :])
    cp('dve', hhs[:, 0:256], tb8[:, 1:257])
    cp('dve', hhs[:, 256:768], cap(tb8[:, 256:257], [[0, 512]]))
    DMA(hh[:, :], hhs)
    for h in range(8):
        hk = xstage[h % 2][:, 1024:1664].rearrange("p (c r) -> p c r", r=128)
        DMA(hk, bass.AP(hh.tensor, h * 768, [[1, 128], [128, 5], [1, 128]]))
        ps = psum_alloc(2)
        for c in range(5):
            mm_group(ps[:, (4 - c) * 128:(5 - c) * 128], [(hk[:, c, :], antiI[:])])
        tt('dve', BMv[:, h, :], ps[:, 0:640], maskB, ALU.add)

    cast_rr = [0]

    def cast(out, in_):
        cast_rr[0] += 1
        k = cast_rr[0] % 5
        eng = 'act' if k in (0, 2) else ('dve' if k in (1, 3) else 'pool')
        return cp(eng, out, in_)

    pre_i = [0]

    def cast2(out, in_):
        cast_rr[0] += 1
        return cp('act' if cast_rr[0] % 2 else 'dve', out, in_)

    def prepass_load(i, name, pi, src_ap, el, shape3, qb=False):
        stg = stage32[i % 2]
        if not qb:
            a, b = shape3
            DMA(stg[:, 0:el].rearrange("p (a b) -> p a b", b=b), src_ap)
        else:
            DMA(stg[:, 0:6 * 384].rearrange("p (a b) -> p a b", b=384), src_ap)

    def prepass_cast_store(i, name, pi, src_ap, el, shape3, qb=False):
        stg = stage32[i % 2]
        slot = wring[:, (i % NSLOT) * PE_ELEMS:(i % NSLOT) * PE_ELEMS + el]
        if not qb:
            cast2(slot, stg[:, 0:el])
        else:
            s4 = stg[:, 0:6 * 384].rearrange("p (a h c) -> p a h c", h=2, c=192)
            o4 = slot.rearrange("p (a h c) -> p a h c", h=2, c=256)
            cast2(o4[:, :, :, 0:192], s4)
            cast2(o4[:, :, :, 192:224], s4[:, :, :, 160:192])
            cast2(o4[:, :, :, 224:256], s4[:, :, :, 128:160])
        DMA(wsc[name][pi, :, 0:el], slot)

    def colpanel(w2d, pi, ncols=256, c0=None):
        c0 = pi * 256 if c0 is None else c0
        return w2d[:, c0:c0 + ncols].rearrange("(kc p) n -> p kc n", p=128)

    jobs = []
    for pi in range(18):
        jobs.append(('in_ab', pi, colpanel(w_in_ab, pi), 4096, (16, 256), False))
    for pi in range(8):
        jobs.append(('out_ab', pi, colpanel(w_out_ab, pi), 4096, (16, 256), False))
    for l in range(2):
        if l == 1:
            for pi in range(6):
                ncl = 256 if pi < 5 else 64
                jobs.append(('in_c', pi, colpanel(w_in_c, pi, ncl), 16 * ncl, (16, ncl), False))
            for pi in range(8):
                jobs.append(('q_b', pi, colpanel(w_q_b, pi, 384, pi * 384), 3072, None, True))
            for pi in range(8):
                jobs.append(('kv_b', pi, colpanel(w_kv_b, pi, 512, pi * 512), 2048, (4, 512), False))
            for pi in range(8):
                jobs.append(('out_c', pi, colpanel(w_out_c, pi), 4096, (16, 256), False))
        for pi in range(32):
            jobs.append(('up%d' % l, pi, colpanel(w_up[l], pi), 4096, (16, 256), False))
        for pi in range(32):
            ncx, jh = pi // 2, pi % 2
            src = w_down[l][jh * 4096:(jh + 1) * 4096, ncx * 128:(ncx + 1) * 128].rearrange("(j p) n -> p j n", p=128)
            jobs.append(('dn%d' % l, pi, src, 4096, (32, 128), False))
        for pi in range(8):
            jobs.append(('gate%d' % l, pi, colpanel(w_gate[l], pi), 4096, (16, 256), False))
        jobs.append(('ple%d' % l, 0, w_ple[l].rearrange("(kc p) n -> p kc n", p=128), 4096, (2, 2048), False))
    if 'prejobs' in cfg:
        jobs = [j for j in jobs if j[0] in cfg['prejobs']]
    if not enabled('pre1'):
        jobs = []
    elif not enabled('prepass'):
        jobs = jobs[0:2]
    for i, (nm, pi, src, el, shp, qb) in enumerate(jobs):
        if i == 0:
            prepass_load(0, nm, pi, src, el, shp, qb=qb)
        if i + 1 < len(jobs):
            nm2, pi2, src2, el2, shp2, qb2 = jobs[i + 1]
            prepass_load(i + 1, nm2, pi2, src2, el2, shp2, qb=qb2)
        prepass_cast_store(i, nm, pi, src, el, shp, qb=qb)
    pre_i[0] = len(jobs)

    PSEQ = []
    for t in tiles:
        PSEQ += tile_pseq(t[0])
    wst = {'pos': 0, 'issued': 0}

    def issue_load(j):
        name, pi = PSEQ[j]
        el = PSPEC[name][1]
        if name == 'in_c' and pi == 5:
            el = 1024
        s = (pre_i[0] + j) % NSLOT
        DMA(wring[:, s * PE_ELEMS:s * PE_ELEMS + el], wsc[name][pi, :, 0:el])

    def wget(name, pi):
        i = wst['pos']
        assert PSEQ[i] == (name, pi), (PSEQ[i], name, pi)
        wst['pos'] += 1
        while wst['issued'] <= min(i + LOOK, len(PSEQ) - 1):
            issue_load(wst['issued'])
            wst['issued'] += 1
        s = (pre_i[0] + i) % NSLOT
        return wring[:, s * PE_ELEMS:(s + 1) * PE_ELEMS]

    def rope_tok(src, nh, half, cos, sin, out1_of, out2_of):
        g = nh * 2
        s3 = src.rearrange("p (g c) -> p g c", c=half)
        tA = tmpf_r.next()[:, 0:g * half]
        tB = tmpf_r.next()[:, 0:g * half]
        tt('dve', tA.rearrange("p (g c) -> p g c", c=half), s3, cap(cos, [[0, g], [1, half]]), ALU.mult)
        tt('pool', tB.rearrange("p (g c) -> p g c", c=half), s3, cap(sin, [[0, g], [1, half]]), ALU.mult)
        a4 = tA.rearrange("p (h t c) -> p h t c", t=2, c=half)
        b4 = tB.rearrange("p (h t c) -> p h t c", t=2, c=half)
        tt('dve', out1_of, a4[:, :, 0, :], b4[:, :, 1, :], ALU.subtract)
        tt('pool', out2_of, a4[:, :, 1, :], b4[:, :, 0, :], ALU.add)

    def layer_norm(l, which, Tt):
        usum = lnt[:, 0:Tt]
        qsum = lnt[:, TM:TM + Tt]
        mean = lnt[:, 2 * TM:2 * TM + Tt]
        var = lnt[:, 3 * TM:3 * TM + Tt]
        rstd = lnt[:, 4 * TM:4 * TM + Tt]
        nmr = lnt[:, 5 * TM:5 * TM + Tt]
        ps = psum_alloc(1)
        mm_group(ps[:, 0:Tt], [(onesm[:], usum)])
        mm_group(ps[:, 256:256 + Tt], [(onesm[:], qsum)])
        cp('act', mean, ps[:, 0:Tt])
        tt('dve', usum, mean, mean, ALU.mult)
        stt(var, ps[:, 256:256 + Tt], LN_EPS, usum, ALU.add, ALU.subtract)
        act(var, var, AF.Sqrt)
        OP('dve', lambda e: e.reciprocal(out=rstd, in_=var), r=[var], w=[rstd])
        stt(nmr, mean, -1.0, rstd, ALU.mult, ALU.mult)
        gv, bv = (0, 1) if which == 0 else (2, 3)
        pend_cast = None
        for g4 in range(4):
            xs4 = xT32v[:, g4 * 4:(g4 + 1) * 4, 0:Tt]
            tt('dve', xs4, xs4, cap(rstd, [[0, 4], [1, Tt]]), ALU.mult)
            tt('dve', xs4, xs4, cap(nmr, [[0, 4], [1, Tt]]), ALU.add)
            if pend_cast is not None:
                pend_cast()
            for c in range(g4 * 4, g4 * 4 + 4):
                xc = xT32v[:, c, 0:Tt]
                act(xc, xc, AF.Identity, scale=lnp_col(gv, l, c), bias=lnp_col(bv, l, c),
                    r_extra=[lnp_col(gv, l, c), lnp_col(bv, l, c)])
            pend_cast = (lambda g4=g4, xs4=xs4: cp('dve', A0v[:, g4 * 4:(g4 + 1) * 4, 0:Tt], xs4))
        pend_cast()

    def resid_epilogue(ps_ap, c, Tt):
        xc = xT32v[:, c, 0:Tt]
        usum = lnt[:, 0:Tt]
        qsum = lnt[:, TM:TM + Tt]
        stt(xc, xc, ALPHA, ps_ap, ALU.mult, ALU.add)
        sq = tmpf_r.next()[:, 0:Tt]
        act(sq, xc, AF.Square)
        if c == 0:
            cp('dve', usum, xc)
            cp('pool', qsum, sq)
        else:
            tt('dve', usum, usum, xc, ALU.add)
            tt('pool', qsum, qsum, sq, ALU.add)

    def dense_fm(name, npan, Tt, epi):
        for pi in range(npan):
            slot = wget(name, pi)
            sv = slot.rearrange("p (k n) -> p k n", n=256)
            for j in range(2):
                ps = psum_alloc(1)
                mm_group(ps[:, 0:Tt], [(sv[:, kc, j * 128:(j + 1) * 128], A0v[:, kc, 0:Tt]) for kc in range(16)])
                epi(ps, 2 * pi + j)

    def mlp(l, Tt):
        for pi in range(32):
            slot = wget('up%d' % l, pi)
            sv = slot.rearrange("p (k n) -> p k n", n=256)
            ps = psum_alloc(1)
            for j in range(2):
                mm_group(ps[:, j * 256:j * 256 + Tt], [(sv[:, kc, j * 128:(j + 1) * 128], A0v[:, kc, 0:Tt]) for kc in range(16)])
            rl = lnt[:, (1 + pi % 2) * 2 * TM:(1 + pi % 2) * 2 * TM + 2 * TM].rearrange("p (j t) -> p j t", t=TM)[:, :, 0:Tt]
            p3 = ps.rearrange("p (j t) -> p j t", t=256)[:, :, 0:Tt]
            act(rl, p3, AF.Relu)
            tt('pool' if pi % 2 else 'dve', hidTv[:, 2 * pi:2 * pi + 2, 0:Tt], rl, rl, ALU.mult)
        for ncx in range(16):
            ps = psum_alloc(1)
            for jh in range(2):
                slot = wget('dn%d' % l, ncx * 2 + jh)
                sv = slot.rearrange("p (j n) -> p j n", n=128)
                mm_group(ps[:, 0:Tt], [(sv[:, j, :], hidTv[:, jh * 32 + j, 0:Tt]) for j in range(32)],
                         start=(jh == 0), stop=(jh == 1))
            resid_epilogue(ps[:, 0:Tt], ncx, Tt)
        layer_norm(l, 1, Tt)

    def p_prep(ntb, psrc_of_tb):
        for tb in range(ntb):
            pst_ = hst_r.next()
            pbf_ = tmpb_r.next()
            DMA(pst_[:], psrc_of_tb(tb))
            cp('pool', pbf_[:], pst_[:])
            ps = psum_alloc(1)
            pb_ = psb(ps).rearrange("p (j t) -> p j t", t=128)
            transposes([(pb_[:, j, :], pbf_[:, j * 128:(j + 1) * 128]) for j in range(2)], identb)
            evac(pTv[:, :, tb * 128:(tb + 1) * 128], pb_[:, 0:2, :])

    def gate_ple(l, Tt, ntb, psrc_of_tb, last):
        DMA(plew.rearrange("p k n -> p (k n)"), wsc['ple%d' % l][0, :, :])
        for c8 in range(8):
            slot = wget('gate%d' % l, c8)
            sv = slot.rearrange("p (k n) -> p k n", n=256)
            for j in range(2):
                c = 2 * c8 + j
                ps = psum_alloc(1)
                mm_group(ps[:, 0:Tt], [(sv[:, kc, j * 128:(j + 1) * 128], A0v[:, kc, 0:Tt]) for kc in range(16)])
                mm_group(ps[:, 256:256 + Tt], [(plew[:, kc, c * 128:(c + 1) * 128], pTv[:, kc, 0:Tt]) for kc in range(2)])
                gt = tmpf_r.next()[:, 0:Tt]
                act(gt, ps[:, 0:Tt], AF.Sigmoid, bias=lnp_col(4, l, c), r_extra=[lnp_col(4, l, c)])
                tt('dve', gt, gt, ps[:, 256:256 + Tt], ALU.mult)
                xc = xT32v[:, c, 0:Tt]
                tt('pool', xc, xc, gt, ALU.add)
        if not last:
            for g4 in range(4):
                cp('act' if g4 % 2 else 'dve', A0v[:, g4 * 4:(g4 + 1) * 4, 0:Tt], xT32v[:, g4 * 4:(g4 + 1) * 4, 0:Tt])

    def softmax_tail(nq, den, P, nheads=1):
        rden = stat(nheads)
        OP('dve', lambda e: e.reciprocal(out=rden[0:nq, :], in_=den), r=[den], w=[rden[0:nq, :]])
        Dt = D_r.next()
        if nheads == 1:
            Dv = Dt[0:nq, 0:nq]
            ts('dve', Dv, identb[0:nq, 0:nq], rden[0:nq, 0:1], None, ALU.mult, r_extra=[rden[0:nq, 0:1]])
        else:
            Dv = Dt[0:nq, 0:nheads * nq].rearrange("p (h q) -> p h q", q=nq)
            tt('dve', Dv, cap(identb[0:nq, 0:1], [[0, nheads], [1, nq]]), cap(rden[0:nq, 0:1], [[1, nheads], [0, nq]]), ALU.mult)
        return Dv

    def attn_A(g, qc0, nq, kc0, kbl, maskap):
        nk = sum(k for k, _ in kbl)
        st = {}

        def s1():
            ps = psum_at(pool_next('l0pair', [0, 2, 4]), 2)
            p3 = ps.rearrange("p (h k) -> p h k", k=256)
            for hh in range(4):
                mm_group(p3[0:nq, hh, 0:nk], [(qaTv[:, 4 * g + hh, qc0:qc0 + nq], kaTv[:, g, kc0:kc0 + nk])])
            S3 = p3[0:nq, :, 0:nk]
            tt('dve', S3, S3, cap(maskap, [[0, 4], [1, nk]]), ALU.add)
            mx = stat(4)[0:nq, :]
            OP('dve', lambda e: e.tensor_reduce(out=mx, in_=S3, axis=AX.X, op=ALU.max), r=[S3], w=[mx])
            negm = stat(4)[0:nq, :]
            stt(negm, mx, -1.0, nsinks_b[0:nq, 4 * g:4 * g + 4], ALU.mult, ALU.min)
            P = PS_r.next()
            den = stat(4)[0:nq, :]
            for hh in range(4):
                act(P[0:nq, hh * 256:hh * 256 + nk], S3[:, hh, :], AF.Exp, bias=negm[:, hh:hh + 1], scale=1.0,
                    accum_out=den[:, hh:hh + 1], r_extra=[negm[:, hh:hh + 1]])
            tmp = stat(4)[0:nq, :]
            tt('dve', tmp, sinks_b[0:nq, 4 * g:4 * g + 4], negm, ALU.add)
            es_ = stat(4)[0:nq, :]
            act(es_, tmp, AF.Exp)
            st['P'] = P
            st['den'] = den
            st['es'] = es_

        def s2a():
            den2 = stat(4)[0:nq, :]
            tt('dve', den2, st['den'], st['es'], ALU.add)
            P = st['P']
            Dv = softmax_tail(nq, den2, P, 4)
            ps = psum_at(pool_next('l0pair', [0, 2, 4]), 2)
            p4 = ps.rearrange("p (h b q) -> p h b q", b=2, q=128)
            off = 0
            for kb, (ks, _) in enumerate(kbl):
                for hh in range(4):
                    mm_group(p4[0:ks, hh, kb, 0:nq], [(P[0:nq, hh * 256 + off:hh * 256 + off + ks], Dv[:, hh, :])])
                off += ks
            PT = PT_r.next()
            PT4 = PT[:].rearrange("p (h b q) -> p h b q", b=2, q=128)
            nkb = len(kbl)
            if all(ks == 128 for ks, _ in kbl):
                cp('act', PT4[:, :, 0:nkb, 0:nq], p4[:, :, 0:nkb, 0:nq])
            else:
                for kb, (ks, _) in enumerate(kbl):
                    cp('act', PT4[0:ks, :, kb, 0:nq], p4[0:ks, :, kb, 0:nq])
            st['PT4'] = PT4

        def s2b():
            PT4 = st['PT4']
            po = psum_at(pool_next('l0po', [6, 7]))
            po3 = po.rearrange("p (h q) -> p h q", q=128)
            for hh in range(4):
                mm_group(po3[:, hh, 0:nq], [(vblk[0:ks, g * 128:(g + 1) * 128], PT4[0:ks, hh, kb, 0:nq])
                                           for kb, (ks, vblk) in enumerate(kbl)])
            cp('dve', A0v[:, 4 * g:4 * g + 4, qc0:qc0 + nq], po3[:, :, 0:nq])
        return s1, s2a, s2b

    def attn_B(h, qc0, nq, kc0, kbl, bias_ap):
        nk = sum(k for k, _ in kbl)
        st = {}

        def s1():
            ps = psum_at(pool_next('l0pair', [0, 2, 4]), 2)
            k0 = 0
            while k0 < nk:
                n = min(512, nk - k0)
                mm_group(ps[0:nq, k0:k0 + n], [(qbTv[:, h, qc0:qc0 + nq], kbTv[:, h, kc0 + k0:kc0 + k0 + n])])
                k0 += n
            S2 = ps[0:nq, 0:nk]
            tt('dve', S2, S2, bias_ap, ALU.add)
            negm = stat(1)[0:nq, :]
            OP('dve', lambda e: e.tensor_reduce(out=negm, in_=S2, axis=AX.X, op=ALU.max, negate=True), r=[S2], w=[negm])
            P = PS_r.next()
            den = stat(1)[0:nq, :]
            act(P[0:nq, 0:nk], S2, AF.Exp, bias=negm, scale=1.0, accum_out=den, r_extra=[negm])
            st['P'] = P
            st['den'] = den

        def s2a():
            P = st['P']
            Dv = softmax_tail(nq, st['den'], P, 1)
            ps = psum_at(pool_next('l0pair', [0, 2, 4]), 2)
            p3 = ps.rearrange("p (b q) -> p b q", q=128)
            off = 0
            for kb, (ks, _) in enumerate(kbl):
                mm_group(p3[0:ks, kb, 0:nq], [(P[0:nq, off:off + ks], Dv)])
                off += ks
            PT = PT_r.next()
            PT3 = PT[:].rearrange("p (b q) -> p b q", q=128)
            nkb = len(kbl)
            if all(ks == 128 for ks, _ in kbl):
                cp('act', PT3[:, 0:nkb, 0:nq], p3[:, 0:nkb, 0:nq])
            else:
                for kb, (ks, _) in enumerate(kbl):
                    cp('act', PT3[0:ks, kb, 0:nq], p3[0:ks, kb, 0:nq])
            st['PT3'] = PT3

        def s2b():
            PT3 = st['PT3']
            po = psum_at(pool_next('l0po', [6, 7]))
            mm_group(po[:, 0:nq], [(vblk[0:ks, h * 128:(h + 1) * 128], PT3[0:ks, kb, 0:nq])
                                   for kb, (ks, vblk) in enumerate(kbl)])
            cp('dve', A0v[:, 8 + h, qc0:qc0 + nq], po[:, 0:nq])
        return s1, s2a, s2b

    def attn_C(hslot, h, qc0, nq, kbl, mask_last):
        nk = sum(kbl)
        st = {}

        def s1():
            nb = (nk + 511) // 512
            ps = (psum_at(pool_next('l1sc', [0, 2]), nb) if nb <= 2 else psum_at(0, nb))
            k0 = 0
            while k0 < nk:
                n = min(512, nk - k0)
                mm_group(ps[0:nq, k0:k0 + n], [(qnT[hslot][:, qc0:qc0 + nq], KhT[hslot][:, k0:k0 + n]),
                                               (qrT[hslot][0:64, qc0:qc0 + nq], krT[0:64, k0:k0 + n])])
                k0 += n
            if mask_last:
                tt('dve', ps[0:nq, nk - 128:nk], ps[0:nq, nk - 128:nk], maskC[0:nq, :], ALU.add)
            negm = stat(1)[0:nq, :]
            pin = ps[0:nq, 0:nk]
            OP('dve', lambda e: e.tensor_reduce(out=negm, in_=pin, axis=AX.X, op=ALU.max, negate=True), r=[pin], w=[negm])
            P = P_r.next()
            den = stat(1)[0:nq, :]
            act(P[0:nq, 0:nk], pin, AF.Exp, bias=negm, scale=1.0, accum_out=den, r_extra=[negm])
            st['P'] = P
            st['den'] = den

        def s2():
            P = st['P']
            Dv = softmax_tail(nq, st['den'], P, 1)
            po = psum_at(7)
            nkb = len(kbl)
            groups = [list(range(g0, min(g0 + 4, nkb))) for g0 in range(0, nkb, 4)]
            pend = None
            for gi, grp in enumerate(groups):
                ps = psum_at(pool_next('l1s', C_SINGLES[0]))
                p3 = ps.rearrange("p (b q) -> p b q", q=128)
                for j, kb in enumerate(grp):
                    ks = kbl[kb]
                    mm_group(p3[0:ks, j, 0:nq], [(P[0:nq, kb * 128:kb * 128 + ks], Dv)])
                PT = PT_r.next()
                PT3 = PT[:, 0:512].rearrange("p (b q) -> p b q", q=128)
                if all(kbl[kb] == 128 for kb in grp):
                    evac(PT3[:, 0:len(grp), 0:nq], p3[:, 0:len(grp), 0:nq])
                else:
                    for j, kb in enumerate(grp):
                        evac(PT3[0:kbl[kb], j, 0:nq], p3[0:kbl[kb], j, 0:nq])
                if pend is not None:
                    pend()

                def pv(grp=grp, PT3=PT3):
                    mm_group(po[:, 0:nq], [(Vh[hslot][0:kbl[kb], kb, :], PT3[0:kbl[kb], j, 0:nq]) for j, kb in enumerate(grp)],
                             start=(grp[0] == 0), stop=(grp[-1] == nkb - 1))
                pend = pv
            pend()
            evac(A0v[:, h, qc0:qc0 + nq], po[:, 0:nq])
        return s1, s2

    def run_pipelined3(items):
        n = len(items)
        for k in range(n + 2):
            if k < n:
                items[k][0]()
            if 0 <= k - 1 < n:
                items[k - 1][1]()
            if 0 <= k - 2 < n:
                items[k - 2][2]()

    def run_pipelined(items, depth=1):
        pend = []
        for s1, s2 in items:
            s1()
            pend.append(s2)
            if len(pend) > depth:
                pend.pop(0)()
        while pend:
            pend.pop(0)()

    def run_tile(tile):
        kind = tile[0]
        if kind == 'prompt':
            _, s, pos0 = tile
            Tt = TM
        else:
            s, pos0, Tt = None, PAST, 128
        ntb = Tt // 128
        first = (kind == 'prompt' and pos0 == 0)
        if kind == 'prompt':
            psrc0 = lambda tb: pp[0, s, pos0 + tb * 128:pos0 + (tb + 1) * 128, :]
            psrc1 = lambda tb: pp[1, s, pos0 + tb * 128:pos0 + (tb + 1) * 128, :]
        else:
            psrc0 = lambda tb: psm[0, :, :]
            psrc1 = lambda tb: psm[1, :, :]

        for tb in range(ntb):
            if kind == 'prompt':
                p = pos0 + tb * 128
                DMA(ropeAv[:, tb, :], c_ropeA[p:p + 128, :])
                DMA(ropeCtv[:, tb, :], c_ropeCt[p:p + 128, :])
            else:
                for s2 in range(2):
                    DMA(ropeAv[s2 * 64:(s2 + 1) * 64, 0, :], c_ropeA[PAST:PAST + 64, :])
                    DMA(ropeCtv[s2 * 64:(s2 + 1) * 64, 0, :], c_ropeCt[PAST:PAST + 64, :])
        if kind == 'prompt':
            DMA(ropeCfv[0:64, :, 0:Tt], c_ropeCf[:, :, pos0:pos0 + Tt])
        else:
            for s2 in range(2):
                DMA(ropeCfv[0:64, :, s2 * 64:(s2 + 1) * 64], c_ropeCf[:, :, PAST:PAST + 64])

        if not enabled('x'):
            return
        phase('x')
        for tb in range(ntb):
            stg = xstage[tb % 2]
            src = xp[s, pos0 + tb * 128:pos0 + (tb + 1) * 128, :] if kind == 'prompt' else xs[:, :]
            DMA(stg[:], src)
            for g in range(4):
                ps = psum_alloc(1)
                transposes([(ps[:, j * 128:(j + 1) * 128], stg[:, (g * 4 + j) * 128:(g * 4 + j + 1) * 128]) for j in range(4)], identf)
                p3 = ps.rearrange("p (j t) -> p j t", t=128)
                cp('act', xT32v[:, g * 4:(g + 1) * 4, tb * 128:(tb + 1) * 128], p3)
                cp('dve', A0v[:, g * 4:(g + 1) * 4, tb * 128:(tb + 1) * 128], p3)

        if not enabled('l0in'):
            return
        phase('l0in')
        pendq = []

        def run_pend(keep):
            while len(pendq) > keep:
                pendq.pop(0)()

        def tr_post(src_bf, dst_fn, scale=None):
            def post():
                pt = psum_alloc(1)
                pb_ = psb(pt).rearrange("p (j t) -> p j t", t=128)
                transposes([(pb_[:, j, :], src_bf[:, j * 128:(j + 1) * 128]) for j in range(2)], identb)
                dst_fn(pb_[:, 0:2, :])
            pendq.append(post)

        for pi in range(18):
            slot = wget('in_ab', pi)
            sv = slot.rearrange("p (k n) -> p k n", n=256)
            for tb in range(ntb):
                tok0 = tb * 128
                ps = psum_alloc(1)
                mm_group(ps[:, 0:256], [(A0v[:, kc, tok0:tok0 + 128], sv[:, kc, :]) for kc in range(16)])
                run_pend(2)
                pin = ps[:, 0:256]
                cosA = ropeAv[:, tb, 0:64]
                sinA = ropeAv[:, tb, 64:128]
                if pi < 4:
                    h_ = hst_r.next()
                    cp('act', h_[:], pin)
                    qrot = tmpb_r.next()
                    q3 = qrot[:].rearrange("p (h c) -> p h c", c=128)
                    rope_tok(h_[:], 2, 64, cosA, sinA, q3[:, :, 0:64], q3[:, :, 64:128])
                    tr_post(qrot, lambda src, pi=pi, tok0=tok0: act(qaTv[:, 2 * pi:2 * pi + 2, tok0:tok0 + 128], src, AF.Copy, scale=SC_AB))
                elif pi == 4:
                    h_ = hst_r.next()
                    cp('act', h_[:], pin)
                    kro = hst_r.next()
                    k3 = kro[:].rearrange("p (h c) -> p h c", c=128)
                    rope_tok(h_[:], 2, 64, cosA, sinA, k3[:, :, 0:64], k3[:, :, 64:128])
                    if kind == 'prompt' and pos0 + tok0 == SEQ - 128:
                        DMA(pak[s, :, :], kro[:])
                    if kind == 'sample':
                        for s2 in range(2):
                            DMA(sak[s2, 64:128, :], kro[s2 * 64:(s2 + 1) * 64, :])
                    kbf = tmpb_r.next()
                    cp('act', kbf[:], kro[:])
                    tr_post(kbf, lambda src, tok0=tok0: evac(kaTv[:, 0:2, 128 + tok0:128 + tok0 + 128], src))
                elif pi == 5:
                    h_ = hst_r.next()
                    cp('act', h_[:], pin)
                    if kind == 'prompt' and pos0 + tok0 == SEQ - 128:
                        DMA(pav[s, :, :], h_[:])
                    if kind == 'sample':
                        for s2 in range(2):
                            DMA(sav[s2, 64:128, :], h_[s2 * 64:(s2 + 1) * 64, :])
                    cp('pool', vav[:, 1 + tb, :], h_[:])
                elif pi < 10:
                    hb = pi - 6
                    qt = tmpb_r.next()
                    act(qt[:], pin, AF.Copy, scale=SC_AB)
                    tr_post(qt, lambda src, hb=hb, tok0=tok0: cp('dve', qbTv[:, 2 * hb:2 * hb + 2, tok0:tok0 + 128], src))
                elif pi < 14:
                    hb = pi - 10
                    h_ = hst_r.next()
                    cp('act', h_[:], pin)
                    if kind == 'prompt' and pos0 + tok0 >= SEQ - 512:
                        r0 = pos0 + tok0 - (SEQ - 512)
                        DMA(pbk[s, r0:r0 + 128, hb * 256:(hb + 1) * 256], h_[:])
                    if kind == 'sample':
                        for s2 in range(2):
                            DMA(sbk[s2, 448:512, hb * 256:(hb + 1) * 256], h_[s2 * 64:(s2 + 1) * 64, :])
                    kt = tmpb_r.next()
                    cp('pool', kt[:], h_[:])
                    tr_post(kt, lambda src, hb=hb, tok0=tok0: cp('dve', kbTv[:, 2 * hb:2 * hb + 2, 512 + tok0:512 + tok0 + 128], src))
                else:
                    hb = pi - 14
                    h_ = hst_r.next()
                    cp('act', h_[:], pin)
                    if kind == 'prompt' and pos0 + tok0 >= SEQ - 512:
                        r0 = pos0 + tok0 - (SEQ - 512)
                        DMA(pbv[s, r0:r0 + 128, hb * 256:(hb + 1) * 256], h_[:])
                    if kind == 'sample':
                        for s2 in range(2):
                            DMA(sbv[s2, 448:512, hb * 256:(hb + 1) * 256], h_[s2 * 64:(s2 + 1) * 64, :])
                    cp('pool', vbv[:, 4 + tb, hb * 256:(hb + 1) * 256], h_[:])
        run_pend(0)

        if not enabled('l0attn'):
            return
        phase('l0attn')
        if kind == 'prompt':
            items = []
            for tb in range(ntb):
                qc0 = tb * 128
                if first and tb == 0:
                    kbl = [(128, vav[:, 1, :])]
                    kc0 = 128
                    mk = maskA[:, 128:256]
                else:
                    kbl = [(128, vav[:, tb, :]), (128, vav[:, tb + 1, :])]
                    kc0 = tb * 128
                    mk = maskA[:, 0:256]
                for g in range(2):
                    items.append(attn_A(g, qc0, 128, kc0, kbl, mk))
            run_pipelined3(items)
            for tb in range(ntb):
                qc0 = tb * 128
                nvalid = min(640, pos0 + tb * 128 + 128)
                nskip = (640 - nvalid) // 128
                kc0 = tb * 128 + nskip * 128
                kblB = [(128, vbv[:, tb + j, :]) for j in range(nskip, 5)]
                items = []
                for h in range(8):
                    items.append(attn_B(h, qc0, 128, kc0, kblB, BMv[:, h, nskip * 128:640]))
                run_pipelined3(items)
        else:
            for s2 in range(2):
                qc0 = s2 * 64
                stg = xstage[0]
                DMA(stg[:, 0:256], cak[s2, :, :])
                kbf = tmpb_r.next()
                cp('pool', kbf[:], stg[:, 0:256])
                pt = psum_alloc(1)
                pb_ = psb(pt).rearrange("p (j t) -> p j t", t=128)
                transposes([(pb_[:, j, :], kbf[:, j * 128:(j + 1) * 128]) for j in range(2)], identb)
                evac(kaTv[:, 0:2, 0:128], pb_[:, 0:2, :])
                DMA(stg[:, 256:512], cav[s2, :, :])
                cp('pool', vav[:, 0, :], stg[:, 256:512])
                for blk in range(4):
                    st2 = xstage[(blk + 1) % 2]
                    DMA(st2[:, 0:1024], cbk[s2, blk * 128:(blk + 1) * 128, :])
                    kt = PT_r.next()
                    cp('dve' if blk % 2 else 'pool', kt[:], st2[:, 0:1024])
                    pt = psum_alloc(1)
                    pb_ = psb(pt).rearrange("p (j t) -> p j t", t=128)
                    transposes([(pb_[:, j, :], kt[:, j * 128:(j + 1) * 128]) for j in range(8)], identb)
                    evac(kbTv[:, :, blk * 128:(blk + 1) * 128], pb_[:, 0:8, :])
                    DMA(st2[:, 1024:2048], cbv[s2, blk * 128:(blk + 1) * 128, :])
                    cp('pool' if blk % 2 else 'dve', vbv[:, blk, :], st2[:, 1024:2048])
                if s2 == 1:
                    cp('pool', kaTv[:, :, 128:192], kaTv[:, :, 192:256])
                    cp('pool', kbTv[:, :, 512:576], kbTv[:, :, 576:640])
                    DMA(vav[0:64, 2, :], vav[64:128, 1, :])
                    DMA(vbv[0:64, 5, :], vbv[64:128, 4, :])
                vnewA = vav[:, 1, :] if s2 == 0 else vav[:, 2, :]
                vnewB = vbv[:, 4, :] if s2 == 0 else vbv[:, 5, :]
                items = []
                kbl = [(128, vav[:, 0, :]), (64, vnewA)]
                for g in range(2):
                    items.append(attn_A(g, qc0, 64, 0, kbl, maskA[0:64, 0:192]))
                run_pipelined3(items)
                kblB = [(128, vbv[:, j, :]) for j in range(4)] + [(64, vnewB)]
                items = []
                for h in range(8):
                    items.append(attn_B(h, qc0, 64, 0, kblB, BMv[0:64, h, 0:576]))
                run_pipelined3(items)
                DMA(sak[s2, 0:64, :], cak[s2, 64:128, :])
                DMA(sav[s2, 0:64, :], cav[s2, 64:128, :])
                DMA(sbk[s2, 0:448, :], cbk[s2, 64:512, :])
                DMA(sbv[s2, 0:448, :], cbv[s2, 64:512, :])

        if kind == 'prompt' and pos0 + Tt < SEQ:
            for i in range(512 // Tt):
                cp('pool', kbTv[:, :, i * Tt:(i + 1) * Tt], kbTv[:, :, (i + 1) * Tt:(i + 2) * Tt])
                cp('pool', vbv[:, i * ntb:(i + 1) * ntb, :], vbv[:, (i + 1) * ntb:(i + 2) * ntb, :])
            cp('pool', kaTv[:, :, 0:128], kaTv[:, :, Tt:Tt + 128])
            cp('pool', vav[:, 0, :], vav[:, ntb, :])

        if not enabled('l0out'):
            return
        phase('l0out')
        dense_fm('out_ab', 8, Tt, lambda ps, c: resid_epilogue(ps[:, 0:Tt], c, Tt))
        if not enabled('l0ln1'):
            return
        phase('l0ln1')
        layer_norm(0, 0, Tt)
        phase('l0mlp')
        if not enabled('l0mlp'):
            return
        p_prep(ntb, psrc0)
        mlp(0, Tt)
        if not enabled('l0gate'):
            return
        phase('l0gate')
        gate_ple(0, Tt, ntb, psrc0, False)

        if not enabled('l1in'):
            return
        phase('l1in')
        for pi in range(6):
            ncl = 256 if pi < 5 else 64
            slot = wget('in_c', pi)
            sv = slot[:, 0:16 * ncl].rearrange("p (k n) -> p k n", n=ncl)
            for tb in range(ntb):
                tok0 = tb * 128
                ps = psum_alloc(1)
                mm_group(ps[:, 0:ncl], [(A0v[:, kc, tok0:tok0 + 128], sv[:, kc, :]) for kc in range(16)])
                evac(hcv[:, tb, pi * 256:pi * 256 + ncl], ps[:, 0:ncl])
        for tb in range(ntb):
            tok0 = tb * 128
            ssq = stat(2)
            act(Sbuf[:, 0:768], hcv[:, tb, 0:768], AF.Square, accum_out=ssq[:, 0:1])
            act(Sbuf[:, 0:512], hcv[:, tb, 768:1280], AF.Square, accum_out=ssq[:, 1:2])
            t2 = stat(2)
            ts('dve', t2[:, 0:1], ssq[:, 0:1], 1.0 / 768, RMS_EPS, ALU.mult, ALU.add)
            ts('dve', t2[:, 1:2], ssq[:, 1:2], 1.0 / 512, RMS_EPS, ALU.mult, ALU.add)
            sd = stat(2)
            act(sd, t2, AF.Sqrt)
            rs = stat(2)
            OP('dve', lambda e, rs=rs, sd=sd: e.reciprocal(out=rs, in_=sd), r=[sd], w=[rs])
            cqn = PT_r.next()
            stt(cqn[:, 0:768], hcv[:, tb, 0:768], rs[:, 0:1], gq_b[:], ALU.mult, ALU.mult, r_extra=[rs[:, 0:1]])
            ckv32 = hcv[:, tb, 768:1280]
            stt(ckv32, ckv32, rs[:, 1:2], gkv_b[:], ALU.mult, ALU.mult, r_extra=[rs[:, 1:2]])
            ckvb = PT_r.next()
            cp('pool', ckvb[:, 0:512], ckv32)
            kro = hst_r.next()
            rope_tok(hcv[:, tb, 1280:1344], 1, 32, ropeCtv[:, tb, 0:32], ropeCtv[:, tb, 32:64],
                     kro[:, 0:32].rearrange("p (h c) -> p h c", c=32), kro[:, 32:64].rearrange("p (h c) -> p h c", c=32))
            krb = tmpb_r.next()
            cp('act', krb[:, 0:64], kro[:, 0:64])
            if kind == 'prompt':
                p = pos0 + tok0
                DMA(pckv[s, p:p + 128, :], ckv32)
                DMA(pckr[s, p:p + 128, :], kro[:, 0:64])
            else:
                DMA(sckv[:, :], ckv32)
                DMA(sckr[:, :], kro[:, 0:64])
            pt = psum_alloc(1)
            pb_ = psb(pt).rearrange("p (j t) -> p j t", t=128)
            transposes([(pb_[:, j, :], cqn[:, j * 128:(j + 1) * 128]) for j in range(6)], identb)
            evac(cqTv[:, 0:6, tok0:tok0 + 128], pb_[:, 0:6, :])
            pt = psum_alloc(1)
            pb_ = psb(pt).rearrange("p (j t) -> p j t", t=128)
            transposes([(pb_[:, j, :], ckvb[:, j * 128:(j + 1) * 128]) for j in range(4)] +
                       [(pb_[0:64, 4, :], krb[:, 0:64])], identb)
            if kind == 'prompt':
                p = pos0 + tok0
                evac(ckvTv[:, 0:4, p:p + 128], pb_[:, 0:4, :])
                evac(krT[0:64, p:p + 128], pb_[0:64, 4, :])
            else:
                evac(cnewv[:, 0:4, :], pb_[:, 0:4, :])
                evac(krnew[0:64, :], pb_[0:64, 4, :])

        if not enabled('l1heads'):
            return
        phase('l1heads')
        def heads_loop(qsl_list, nk_total, kbl_full):
            prev_items = None
            C_SINGLES[0] = [5, 6] if nk_total > 2048 else [4, 5, 6]
            for hp in range(8):
                qslot = wget('q_b', hp)
                qv = qslot[:, 0:3072].rearrange("p (k h c) -> p k h c", h=2, c=256)
                kvslot = wget('kv_b', hp)
                kvv = kvslot[:, 0:2048].rearrange("p (k h c) -> p k h c", h=2, c=256)
                for hh in range(2):
                    h = 2 * hp + hh
                    hs = h % 2
                    ps = psum_at(C_SINGLES[0][0], 2)
                    mm_group(ps[:, 0:Tt], [(qv[:, kc, hh, 0:128], cqTv[:, kc, 0:Tt]) for kc in range(6)])
                    mm_group(ps[0:64, 256:256 + Tt], [(qv[:, kc, hh, 128:192], cqTv[:, kc, 0:Tt]) for kc in range(6)])
                    mm_group(ps[0:64, 512:512 + Tt], [(qv[:, kc, hh, 192:256], cqTv[:, kc, 0:Tt]) for kc in range(6)])
                    act(qnT[hs][:, 0:Tt], ps[:, 0:Tt], AF.Copy, scale=SC_C)
                    t1 = tmpf_r.next()[0:64, 0:Tt]
                    t2_ = tmpf_r.next()[0:64, 0:Tt]
                    tt('dve', t1, ps[0:64, 256:256 + Tt], ropeCfv[0:64, 0, 0:Tt], ALU.mult)
                    tt('dve', t2_, ps[0:64, 512:512 + Tt], ropeCfv[0:64, 1, 0:Tt], ALU.mult)
                    tt('pool', qrT[hs][0:64, 0:Tt], t1, t2_, ALU.add)
                    pit = list(prev_items) if prev_items is not None else []
                    coarse = (nk_total <= 1024 and len(pit) == 2)
                    if coarse:
                        pit[0][0]()
                        pit[1][0]()
                    elif pit:
                        pit[0][0]()
                    k0 = 0
                    while k0 < nk_total:
                        n = min(512, nk_total - k0)
                        ps = psum_at(pool_next('l1s', C_SINGLES[0]))
                        mm_group(ps[:, 0:n], [(kvv[:, kc, hh, 0:128], ckvTv[:, kc, k0:k0 + n]) for kc in range(4)])
                        evac(KhT[hs][:, k0:k0 + n], ps[:, 0:n])
                        k0 += n
                    if pit and not coarse:
                        pit[0][1]()
                        for it in pit[1:2]:
                            it[0]()
                    nkb = len(kbl_full)
                    for g0 in range(0, nkb, 4):
                        grp = list(range(g0, min(g0 + 4, nkb)))
                        ps = psum_at(pool_next('l1s', C_SINGLES[0]))
                        p3 = ps.rearrange("p (b d) -> p b d", d=128)
                        for j, kb in enumerate(grp):
                            ks = kbl_full[kb]
                            mm_group(p3[0:ks, j, :], [(ckvTv[:, kc, kb * 128:kb * 128 + ks], kvv[:, kc, hh, 128:256]) for kc in range(4)])
                        if all(kbl_full[kb] == 128 for kb in grp):
                            evac(Vh[hs][:, g0:g0 + len(grp), :], p3[:, 0:len(grp), :])
                        else:
                            for j, kb in enumerate(grp):
                                evac(Vh[hs][0:kbl_full[kb], kb, :], p3[0:kbl_full[kb], j, :])
                    if coarse:
                        pit[0][1]()
                        pit[1][1]()
                    elif len(pit) > 1:
                        pit[1][1]()
                        run_pipelined(pit[2:])
                    its = []
                    for (qc0, nq, nkv, ml) in qsl_list:
                        kbl = kbl_full[0:(nkv + 127) // 128]
                        its.append(attn_C(hs, h, qc0, nq, kbl, ml))
                    prev_items = its
            run_pipelined(prev_items)

        if kind == 'prompt':
            nk_total = pos0 + Tt
            qsl = [(tb * 128, 128, pos0 + tb * 128 + 128, True) for tb in range(ntb)]
            heads_loop(qsl, nk_total, [128] * (nk_total // 128))
        else:
            for s2 in range(2):
                for blk in range(16):
                    st2 = xstage[blk % 2]
                    DMA(st2[:, 0:512], cckv[s2, blk * 128:(blk + 1) * 128, :])
                    cb = PT_r.next()
                    cp('pool' if blk % 2 else 'dve', cb[:, 0:512], st2[:, 0:512])
                    pt = psum_alloc(1)
                    pb_ = psb(pt).rearrange("p (j t) -> p j t", t=128)
                    transposes([(pb_[:, j, :], cb[:, j * 128:(j + 1) * 128]) for j in range(4)], identb)
                    evac(ckvTv[:, 0:4, blk * 128:(blk + 1) * 128], pb_[:, 0:4, :])
                st2 = xstage[0]
                DMA(st2[:, 1024:2048].rearrange("p (b d) -> p b d", d=64), ckr[s2, :, :].rearrange("(b p) d -> p b d", p=128))
                kb_ = PT_r.next()
                cp('dve', kb_[:, 0:1024], st2[:, 1024:2048])
                for g0 in range(0, 16, 8):
                    pt = psum_alloc(1)
                    pb_ = psb(pt).rearrange("p (j t) -> p j t", t=128)
                    transposes([(pb_[0:64, j, :], kb_[:, (g0 + j) * 64:(g0 + j + 1) * 64]) for j in range(8)], identb)
                    evac(krT[0:64, g0 * 128:(g0 + 8) * 128].rearrange("p (j t) -> p j t", t=128), pb_[0:64, 0:8, :])
                cp('pool', ckvTv[:, :, PAST:PAST + 64], cnewv[:, :, s2 * 64:(s2 + 1) * 64])
                cp('pool', krT[0:64, PAST:PAST + 64], krnew[0:64, s2 * 64:(s2 + 1) * 64])
                heads_loop([(s2 * 64, 64, PAST + 64, False)], PAST + 64, [128] * 16 + [64])

        if not enabled('l1out'):
            return
        phase('l1out')
        dense_fm('out_c', 8, Tt, lambda ps, c: resid_epilogue(ps[:, 0:Tt], c, Tt))
        if not enabled('all'):
            return
        phase('l1ln1')
        layer_norm(1, 0, Tt)
        phase('l1mlp')
        p_prep(ntb, psrc1)
        mlp(1, Tt)
        phase('l1gate')
        gate_ple(1, Tt, ntb, psrc1, True)
        phase('yout')

        for tb in range(ntb):
            stg = xstage[tb % 2]
            for g in range(4):
                ps = psum_alloc(1)
                transposes([(ps[:, j * 128:(j + 1) * 128], xT32v[:, g * 4 + j, tb * 128:(tb + 1) * 128]) for j in range(4)], identf)
                evac(stg[:, g * 512:(g + 1) * 512], ps[:, 0:512])
            if kind == 'prompt':
                DMA(yp[s, pos0 + tb * 128:pos0 + (tb + 1) * 128, :], stg[:])
            else:
                DMA(ys[:, :], stg[:])


    if enabled('x'):
        for t in tiles:
            run_tile(t)
    dump_all()
    phase('end')
    if cfg.get('marks'):
        import json
        json.dump(marks, open(cfg['marks'], 'w'))
    assert UPTO != 'all' or wst['pos'] == len(PSEQ)

    tr.finalize()
    block = es.enter_context(nc.Block())

    @block.tensor
    def _(e):
        tr.emit('pe', e, esem, dsem)

    @block.scalar
    def _(e):
        tr.emit('act', e, esem, dsem)

    @block.vector
    def _(e):
        tr.emit('dve', e, esem, dsem)

    @block.gpsimd
    def _(e):
        tr.emit('pool', e, esem, dsem)

    @block.sync
    def _(e):
        tr.emit('sp', e, esem, dsem)
        nd = tr.ndma
        for k in range(min(NDSEM, nd)):
            cnt = (nd - 1 - k) // NDSEM + 1
            e.wait_ge(dsem[k], 16 * cnt)
        for f in ENGS:
            if f != 'sp' and tr.count[f] > 0:
                e.wait_ge(esem[f], tr.count[f])
    es.close()
    return nc


def _consts():
    c = {}
    c['c_identb'] = np.eye(128, dtype=np.float32).astype(ml_dtypes.bfloat16)
    c['c_identf'] = np.eye(128, dtype=np.float32)
    c['c_antiI'] = np.ascontiguousarray(np.eye(128, dtype=np.float32)[::-1])
    c['c_ones'] = np.full((128, 128), 1.0 / D, dtype=np.float32)
    NEG = np.float32(-1e30)
    mA = np.zeros((128, 256), np.float32)
    mA[0:64, 192:256] = NEG
    mA[64:128, 0:64] = NEG
    c['c_maskA'] = mA
    mB = np.zeros((128, 640), np.float32)
    mB[0:64, 576:640] = NEG
    mB[64:128, 0:64] = NEG
    c['c_maskB'] = mB
    mC = np.zeros((128, 128), np.float32)
    mC[0:64, 64:128] = NEG
    c['c_maskC'] = mC
    pos = np.arange(2112, dtype=np.float32)

    def tab(d):
        half = d // 2
        inv = (np.float32(10000.0) ** (-np.arange(half, dtype=np.float32) * np.float32(2.0 / d))).astype(np.float32)
        ang = (pos[:, None] * inv[None, :]).astype(np.float32)
        return np.cos(ang).astype(np.float32), np.sin(ang).astype(np.float32)
    ca, sa = tab(128)
    c['c_ropeA'] = np.ascontiguousarray(np.concatenate([ca, sa], 1))
    cc, sc = tab(64)
    c['c_ropeCt'] = np.ascontiguousarray(np.concatenate([cc, sc], 1))
    cf = np.zeros((64, 2, 2112), np.float32)
    cf[0:32, 0] = cc.T
    cf[32:64, 0] = cc.T
    cf[0:32, 1] = -sc.T
    cf[32:64, 1] = sc.T
    c['c_ropeCf'] = (cf * np.float32(SC_C)).astype(np.float32)
    return c


def all_tiles(TM=256):
    tiles = []
    for s in range(2):
        for p in range(0, SEQ, TM):
            tiles.append(('prompt', s, p))
    tiles.append(('sample',))
    return tiles


def core_inputs(inp, c, consts):
    b0, b1 = 2 * c, 2 * c + 2
    f = np.ascontiguousarray
    m = {
        'xp': f(inp['x_prompt'][b0:b1]),
        'xs': f(inp['x_sample'][b0:b1].reshape(128, D)),
        'cak': f(inp['cache_a_k'][0, b0:b1].reshape(2, 128, 256)),
        'cav': f(inp['cache_a_v'][0, b0:b1].reshape(2, 128, 256)),
        'cbk': f(inp['cache_b_k'][0, b0:b1].reshape(2, 512, 1024)),
        'cbv': f(inp['cache_b_v'][0, b0:b1].reshape(2, 512, 1024)),
        'cckv': f(inp['cache_c_kv'][0, b0:b1]),
        'ckr': f(inp['cache_c_krope'][0, b0:b1]),
        'pp': f(inp['p_prompt'][:, b0:b1]),
        'psm': f(inp['p_sample'][:, b0:b1].reshape(2, 128, 256)),
        'w_in_ab': inp['w_in_ab'][0], 'sinks': inp['sinks_a'], 'relb': inp['rel_bias_b'][0],
        'w_out_ab': inp['w_out_ab'][0], 'w_in_c': inp['w_in_c'][0], 'g_q': inp['g_q_c'],
        'w_q_b': inp['w_q_b_c'][0], 'g_kv': inp['g_kv_c'], 'w_kv_b': inp['w_kv_b_c'][0],
        'w_out_c': inp['w_out_c'][0], 'ln1_g': inp['ln1_g'], 'ln1_b': inp['ln1_b'],
        'ln2_g': inp['ln2_g'], 'ln2_b': inp['ln2_b'], 'w_up': inp['w_mlp_up'], 'w_down': inp['w_mlp_down'],
        'w_gate': inp['w_ple_gate'], 'b_gate': inp['b_ple_gate'], 'w_ple': inp['w_ple'],
    }
    m = {k: f(np.asarray(v, dtype=np.float32)) for k, v in m.items()}
    m.update(consts)
    return m


def kernel(**inputs):
    inp = {k: np.asarray(v) for k, v in inputs.items()}
    consts = _consts()
    nc = build(all_tiles())
    in_maps = [core_inputs(inp, c, consts) for c in range(NCORES)]
    res = run_bass_kernel_spmd(nc, in_maps, core_ids=list(range(NCORES)))
    R = res.results

    def cat(name, shape_tail):
        return np.concatenate([np.asarray(R[c][name], dtype=np.float32).reshape((2,) + shape_tail) for c in range(NCORES)], 0)
    y_prompt = cat('yp', (SEQ, D))
    y_sample = cat('ys', (DEC, D))
    pa_k = cat('pak', (128, 2, 128))[None]
    pa_v = cat('pav', (128, 2, 128))[None]
    pb_k = cat('pbk', (512, 8, 128))[None]
    pb_v = cat('pbv', (512, 8, 128))[None]
    pc_kv = cat('pckv', (SEQ, 512))[None]
    pc_kr = cat('pckr', (SEQ, 64))[None]
    sa_k = cat('sak', (128, 2, 128))[None]
    sa_v = cat('sav', (128, 2, 128))[None]
    sb_k = cat('sbk', (512, 8, 128))[None]
    sb_v = cat('sbv', (512, 8, 128))[None]
    sc_kv = cat('sckv', (DEC, 512))[None]
    sc_kr = cat('sckr', (DEC, 64))[None]
    return (y_prompt, y_sample, pa_k, pa_v, pb_k, pb_v, pc_kv, pc_kr, sa_k, sa_v, sb_k, sb_v, sc_kv, sc_kr)
```

```python
import numpy as np
import ml_dtypes
from bisect import bisect_right
from contextlib import ExitStack
import concourse.bass as bass
import concourse.mybir as mybir
from concourse.bass_utils import run_bass_kernel_spmd

F32 = mybir.dt.float32
BF16 = mybir.dt.bfloat16
AF = mybir.ActivationFunctionType
ALU = mybir.AluOpType
AX = mybir.AxisListType

D = 2048
SEQ = 2048
DEC = 64
PAST = 2048
ALPHA = float(4 ** 0.25)
LN_EPS = 1e-5
RMS_EPS = 1e-6
SC_AB = float(128 ** -0.5)
SC_C = float(192 ** -0.5)
NCORES = 8
ESZ = {F32: 4, BF16: 2}
ENGS = ['pe', 'act', 'dve', 'pool', 'sp']
NDSEM = 24
NSLOT = 4
LOOK = 2
PE_ELEMS = 4096


def cap(ap, pattern):
    return bass.AP(ap.tensor, ap.offset, [list(ap.ap[0])] + [list(p) for p in pattern])


class Op:
    __slots__ = ('eng', 'fn', 'deps', 'ms', 'dsem', 'dval', 'isdma', 'has_dep')

    def __init__(self, eng, fn, isdma=False):
        self.eng = eng
        self.fn = fn
        self.deps = set()
        self.ms = None
        self.isdma = isdma
        self.dsem = None
        self.dval = None
        self.has_dep = False


class Tracker:
    def __init__(self):
        self.ops = {e: [] for e in ENGS}
        self.seg = {}
        self.rowb = {}
        self.ndma = 0

    def register(self, name, rowbytes):
        self.seg[name] = ([0], [[0, 1 << 62, None, {}]])
        self.rowb[name] = rowbytes

    def _region(self, ap):
        name = ap.tensor.name
        if name not in self.seg:
            return None
        es = ESZ[ap.dtype]
        pat = ap.ap
        rb = self.rowb[name]
        if rb is None:
            off = ap.offset
            ext = 1
            for st, cn in pat:
                ext += (cn - 1) * abs(st)
        else:
            re_ = rb // es
            off = ap.offset % re_
            ext = 1
            for st, cn in pat[1:]:
                ext += (cn - 1) * abs(st)
            if name == 'psum':
                lo = (off * es) // 2048 * 2048
                hi = ((off + ext) * es + 2047) // 2048 * 2048
                return name, lo, hi
        return name, off * es, (off + ext) * es

    def _access(self, reg, write, op):
        name, lo, hi = reg
        starts, segs = self.seg[name]
        i = bisect_right(starts, lo) - 1
        s = segs[i]
        if s[0] < lo:
            new = [lo, s[1], s[2], dict(s[3])]
            s[1] = lo
            segs.insert(i + 1, new)
            starts.insert(i + 1, lo)
            i += 1
        while i < len(segs) and segs[i][0] < hi:
            s = segs[i]
            if s[1] > hi:
                new = [hi, s[1], s[2], dict(s[3])]
                s[1] = hi
                segs.insert(i + 1, new)
                starts.insert(i + 1, hi)
            if s[2] is not None:
                op.deps.add(s[2])
            if write:
                for r in s[3].values():
                    op.deps.add(r)
                s[2] = op
                s[3] = {}
            else:
                key = ('d', id(op)) if op.isdma else op.eng
                s[3][key] = op
            i += 1

    def op(self, eng, fn, r=(), w=()):
        o = Op(eng, fn)
        for ap in r:
            reg = self._region(ap)
            if reg is not None:
                self._access(reg, reg[0] == 'psum', o)
        for ap in w:
            reg = self._region(ap)
            if reg is not None:
                self._access(reg, True, o)
        o.deps.discard(o)
        self.ops[eng].append(o)
        return o

    def dma(self, out, in_):
        o = Op('sp', (lambda e, out=out, in_=in_: e.dma_start(out=out, in_=in_)), isdma=True)
        o.dsem = self.ndma % NDSEM
        o.dval = 16 * (self.ndma // NDSEM + 1)
        self.ndma += 1
        reg = self._region(in_)
        if reg is not None:
            self._access(reg, False, o)
        reg = self._region(out)
        if reg is not None:
            self._access(reg, True, o)
        o.deps.discard(o)
        self.ops['sp'].append(o)
        return o

    def finalize(self):
        for e in ENGS:
            for o in self.ops[e]:
                for d in o.deps:
                    d.has_dep = True
        self.count = {}
        for e in ENGS:
            c = 0
            for o in self.ops[e]:
                if o.has_dep and not o.isdma:
                    c += 1
                    o.ms = c
            self.count[e] = c

    def emit(self, e, engobj, esem, dsem):
        known = {f: 0 for f in ENGS}
        dknown = {}
        for o in self.ops[e]:
            need = {}
            dneed = {}
            for d in o.deps:
                if d.isdma:
                    if dneed.get(d.dsem, 0) < d.dval:
                        dneed[d.dsem] = d.dval
                else:
                    if d.eng == e and e in ('pe', 'sp'):
                        continue
                    if need.get(d.eng, 0) < d.ms:
                        need[d.eng] = d.ms
            if o.isdma and o.dval > 16:
                if dneed.get(o.dsem, 0) < o.dval - 16:
                    dneed[o.dsem] = o.dval - 16
            for f, v in need.items():
                if known[f] < v:
                    engobj.wait_ge(esem[f], v)
                    known[f] = v
            for k, v in dneed.items():
                if dknown.get(k, 0) < v:
                    engobj.wait_ge(dsem[k], v)
                    dknown[k] = v
            ins = o.fn(engobj)
            if o.isdma:
                ins.then_inc(dsem[o.dsem], 16)
            elif o.ms is not None:
                ins.then_inc(esem[e], 1)


class Ring:
    def __init__(self, items):
        self.items = items
        self.i = 0

    def next(self):
        x = self.items[self.i % len(self.items)]
        self.i += 1
        return x


def panel_specs():
    sp = {
        'in_ab': (18, 4096), 'out_ab': (8, 4096), 'in_c': (6, 4096), 'q_b': (8, 3072), 'kv_b': (8, 2048),
        'out_c': (8, 4096),
    }
    for l in range(2):
        sp['up%d' % l] = (32, 4096)
        sp['dn%d' % l] = (32, 4096)
        sp['gate%d' % l] = (8, 4096)
        sp['ple%d' % l] = (1, 4096)
    return sp


def tile_pseq(kind):
    seq = []
    seq += [('in_ab', i) for i in range(18)] + [('out_ab', i) for i in range(8)]
    seq += [('up0', i) for i in range(32)] + [('dn0', i) for i in range(32)] + [('gate0', i) for i in range(8)]
    seq += [('in_c', i) for i in range(6)]
    for _ in range(2 if kind == 'sample' else 1):
        for hp in range(8):
            seq += [('q_b', hp), ('kv_b', hp)]
    seq += [('out_c', i) for i in range(8)]
    seq += [('up1', i) for i in range(32)] + [('dn1', i) for i in range(32)] + [('gate1', i) for i in range(8)]
    return seq


def build(tiles, TM=256, cfg=None):
    cfg = cfg or {}
    UPTO = cfg.get('upto', 'all')
    ORDER = ['setup', 'pre1', 'prepass', 'x', 'l0in', 'l0attn', 'l0out', 'l0ln1', 'l0mlp', 'l0gate', 'l1in', 'l1heads', 'l1out', 'all']
    def enabled(stage):
        return ORDER.index(stage) <= ORDER.index(UPTO)
    NTBM = TM // 128
    nc = bass.Bass("TRN2", target_bir_lowering=False, dynamic_dma_scratch_size=256)
    tr = Tracker()
    es = ExitStack()

    def din(name, shape, dt=F32):
        return nc.dram_tensor(name, list(shape), dt, kind="ExternalInput").ap()

    def dout(name, shape):
        return nc.dram_tensor(name, list(shape), F32, kind="ExternalOutput").ap()

    xp = din('xp', [2, SEQ, D])
    xs = din('xs', [128, D])
    cak = din('cak', [2, 128, 256])
    cav = din('cav', [2, 128, 256])
    cbk = din('cbk', [2, 512, 1024])
    cbv = din('cbv', [2, 512, 1024])
    cckv = din('cckv', [2, PAST, 512])
    ckr = din('ckr', [2, PAST, 64])
    pp = din('pp', [2, 2, SEQ, 256])
    psm = din('psm', [2, 128, 256])
    w_in_ab = din('w_in_ab', [D, 4608])
    sinks = din('sinks', [1, 8])
    relb = din('relb', [8, 257])
    w_out_ab = din('w_out_ab', [D, D])
    w_in_c = din('w_in_c', [D, 1344])
    g_q = din('g_q', [1, 768])
    w_q_b = din('w_q_b', [768, 3072])
    g_kv = din('g_kv', [1, 512])
    w_kv_b = din('w_kv_b', [512, 4096])
    w_out_c = din('w_out_c', [D, D])
    ln1_g = din('ln1_g', [2, D])
    ln1_b = din('ln1_b', [2, D])
    ln2_g = din('ln2_g', [2, D])
    ln2_b = din('ln2_b', [2, D])
    w_up = din('w_up', [2, D, 8192])
    w_down = din('w_down', [2, 8192, D])
    w_gate = din('w_gate', [2, D, D])
    b_gate = din('b_gate', [2, D])
    w_ple = din('w_ple', [2, 256, D])
    c_identb = din('c_identb', [128, 128], BF16)
    c_identf = din('c_identf', [128, 128])
    c_antiI = din('c_antiI', [128, 128])
    c_ones = din('c_ones', [128, 128])
    c_maskA = din('c_maskA', [128, 256])
    c_maskB = din('c_maskB', [128, 640])
    c_maskC = din('c_maskC', [128, 128])
    c_ropeA = din('c_ropeA', [2112, 128])
    c_ropeCt = din('c_ropeCt', [2112, 64])
    c_ropeCf = din('c_ropeCf', [64, 2, 2112])

    yp = dout('yp', [2, SEQ, D])
    ys = dout('ys', [128, D])
    pak = dout('pak', [2, 128, 256])
    pav = dout('pav', [2, 128, 256])
    pbk = dout('pbk', [2, 512, 1024])
    pbv = dout('pbv', [2, 512, 1024])
    pckv = dout('pckv', [2, SEQ, 512])
    pckr = dout('pckr', [2, SEQ, 64])
    sak = dout('sak', [2, 128, 256])
    sav = dout('sav', [2, 128, 256])
    sbk = dout('sbk', [2, 512, 1024])
    sbv = dout('sbv', [2, 512, 1024])
    sckv = dout('sckv', [128, 512])
    sckr = dout('sckr', [128, 64])

    PSPEC = panel_specs()
    wsc = {}
    for name, (npan, el) in PSPEC.items():
        t = nc.dram_tensor('wsc_' + name, [npan, 128, el], BF16, kind="Internal").ap()
        wsc[name] = t
        tr.register(t.tensor.name, None)
    hh = nc.dram_tensor('hh_scr', [8, 768], F32, kind="Internal").ap()
    tr.register(hh.tensor.name, None)

    def sb(name, n, dt):
        t = es.enter_context(nc.sbuf_tensor(name, [128, n], dt))
        tr.register(name, n * ESZ[dt])
        return t

    identb = sb('identb', 128, BF16)
    identf = sb('identf', 128, F32)
    antiI = sb('antiI', 128, F32)
    onesm = sb('onesm', 128, F32)
    maskA = sb('maskA', 256, F32)
    maskC = sb('maskC', 128, F32)
    BM = sb('BM', 8 * 640, F32)
    lnp = sb('lnp', 160, F32)
    gq_b = sb('gq_b', 768, F32)
    gkv_b = sb('gkv_b', 512, F32)
    sinks_b = sb('sinks_b', 8, F32)
    nsinks_b = sb('nsinks_b', 8, F32)
    ropeA = sb('ropeA', NTBM * 128, F32)
    ropeCt = sb('ropeCt', NTBM * 64, F32)
    ropeCf = sb('ropeCf', 2 * TM, F32)
    xT32 = sb('xT32', 16 * TM, F32)
    A0 = sb('A0', 16 * TM, BF16)
    KB_W = 512 + TM
    kbT = sb('kbT', 8 * KB_W, BF16)
    NVB = 4 + NTBM
    vb = sb('vb', NVB * 1024, BF16)
    KA_W = 128 + TM
    kaT = sb('kaT', 2 * KA_W, BF16)
    NVA = 1 + NTBM
    va = sb('va', NVA * 256, BF16)
    ckvT = sb('ckvT', 4 * 2112, BF16)
    krT = sb('krT', 2112, BF16)
    arena = sb('arena', 16384, BF16)
    wring = sb('wring', NSLOT * PE_ELEMS, BF16)
    Pbuf = [sb('Pbuf%d' % i, 2112, BF16) for i in range(2)]
    PTb = [sb('PTb%d' % i, 1024, BF16) for i in range(4)]
    Dbuf = [sb('Dbuf%d' % i, 512, BF16) for i in range(4)]
    stats = sb('stats', 256, F32)
    hst = [sb('hst%d' % i, 256, F32) for i in range(4)]
    tmpf = [sb('tmpf%d' % i, 256, F32) for i in range(3)]
    tmpb = [sb('tmpb%d' % i, 256, BF16) for i in range(5)]
    xstage_all = sb('xstage', 4096, F32)
    xstage = [xstage_all[:, 0:2048], xstage_all[:, 2048:4096]]
    hc = xstage_all[:, 0:NTBM * 1344]
    lnt = sb('lnt', 6 * TM, F32)
    Sbuf = lnt[:, 0:1024]
    maskB = xstage_all[:, 256:896]
    pT = sb('pT', 2 * TM, BF16)
    cnew = sb('cnew', 4 * 128 + 128, BF16)

    psum = es.enter_context(nc.psum_tensor('psum', [128, 4096], F32))
    tr.register('psum', 4096 * 4)

    esem = {e: es.enter_context(nc.semaphore('es_' + e)) for e in ENGS}
    dsem = [es.enter_context(nc.semaphore('ds%d' % i)) for i in range(NDSEM)]

    def OP(eng, fn, r=(), w=()):
        return tr.op(eng, fn, r, w)

    def DMA(out, in_):
        return tr.dma(out, in_)

    ps_cur = [0]

    def psum_alloc(nb=1):
        if ps_cur[0] + nb > 8:
            ps_cur[0] = 0
        b = ps_cur[0]
        ps_cur[0] += nb
        return psum[:, b * 512:(b + nb) * 512]

    def psum_at(b, nb=1):
        return psum[:, b * 512:(b + nb) * 512]

    pools = {}
    C_SINGLES = [[4, 5, 6]]

    def pool_next(name, choices):
        i = pools.get(name, 0)
        pools[name] = i + 1
        return choices[i % len(choices)]

    pe_cnt = [0]
    marks = []

    def phase(label):
        marks.append((label, pe_cnt[0]))

    def mm_group(out, pairs, start=True, stop=True):
        n = len(pairs)
        pe_cnt[0] += n

        def fn(e, out=out, pairs=pairs):
            ins = None
            for i, (l, r) in enumerate(pairs):
                ins = e.matmul(out, lhsT=l, rhs=r, start=(start and i == 0), stop=(stop and i == n - 1))
            return ins
        rr = []
        for l, r in pairs:
            rr.append(l)
            rr.append(r)
        return OP('pe', fn, r=rr, w=[out])

    def transposes(outs_ins, ident):
        pe_cnt[0] += len(outs_ins)
        def fn(e):
            ins = None
            for o, i in outs_ins:
                k = i.shape[0]
                ins = e.transpose(out=o, in_=i, identity=ident[0:k, 0:k])
            return ins
        return OP('pe', fn, r=[i for _, i in outs_ins] + [ident[:]], w=[o for o, _ in outs_ins])

    def act(out, in_, func, r_extra=(), **kw):
        return OP('act', lambda e: e.activation(out=out, in_=in_, func=func, **kw), r=[in_] + list(r_extra),
                  w=[out] + ([kw['accum_out']] if 'accum_out' in kw else []))

    def tt(eng, out, in0, in1, op):
        return OP(eng, lambda e: e.tensor_tensor(out=out, in0=in0, in1=in1, op=op), r=[in0, in1], w=[out])

    def ts(eng, out, in0, s1, s2, op0, op1=None, r_extra=()):
        if op1 is None:
            return OP(eng, lambda e: e.tensor_scalar(out=out, in0=in0, scalar1=s1, scalar2=None, op0=op0),
                      r=[in0] + list(r_extra), w=[out])
        return OP(eng, lambda e: e.tensor_scalar(out=out, in0=in0, scalar1=s1, scalar2=s2, op0=op0, op1=op1),
                  r=[in0] + list(r_extra), w=[out])

    def stt(out, in0, scalar, in1, op0, op1, r_extra=()):
        return OP('dve', lambda e: e.scalar_tensor_tensor(out=out, in0=in0, scalar=scalar, in1=in1, op0=op0, op1=op1),
                  r=[in0, in1] + list(r_extra), w=[out])

    def cp(eng, out, in_):
        if eng == 'act':
            return OP('act', lambda e: e.activation(out=out, in_=in_, func=AF.Copy), r=[in_], w=[out])
        return OP(eng, lambda e: e.tensor_copy(out=out, in_=in_), r=[in_], w=[out])

    evac_rr = [0]

    def evac(out, in_):
        evac_rr[0] += 1
        return cp('act' if evac_rr[0] % 2 else 'dve', out, in_)

    xT32v = xT32[:].rearrange("p (c t) -> p c t", t=TM)
    A0v = A0[:].rearrange("p (c t) -> p c t", t=TM)
    kbTv = kbT[:].rearrange("p (h k) -> p h k", k=KB_W)
    vbv = vb[:].rearrange("p (b c) -> p b c", c=1024)
    kaTv = kaT[:].rearrange("p (h k) -> p h k", k=KA_W)
    vav = va[:].rearrange("p (b c) -> p b c", c=256)
    ckvTv = ckvT[:].rearrange("p (c k) -> p c k", k=2112)
    BMv = BM[:].rearrange("p (h k) -> p h k", k=640)
    hcv = hc.rearrange("p (b c) -> p b c", c=1344)
    ropeAv = ropeA[:].rearrange("p (b c) -> p b c", c=128)
    ropeCtv = ropeCt[:].rearrange("p (b c) -> p b c", c=64)
    ropeCfv = ropeCf[:].rearrange("p (a t) -> p a t", t=TM)
    pTv = pT[:].rearrange("p (a t) -> p a t", t=TM)
    qaTv = arena[:, 0:8 * TM].rearrange("p (h t) -> p h t", t=TM)
    qbTv = arena[:, 8 * TM:16 * TM].rearrange("p (h t) -> p h t", t=TM)
    hidTv = arena[:, 0:64 * TM].rearrange("p (c t) -> p c t", t=TM)
    plew = arena[:, 12288:16384].rearrange("p (k n) -> p k n", n=2048)
    cqTv = arena[:, 0:6 * TM].rearrange("p (c t) -> p c t", t=TM)
    a0 = 6 * TM
    KhT = [arena[:, a0 + i * 2112: a0 + (i + 1) * 2112] for i in range(2)]
    a0 += 2 * 2112
    Vh = [arena[:, a0 + i * 2176: a0 + (i + 1) * 2176].rearrange("p (b d) -> p b d", d=128) for i in range(2)]
    a0 += 2 * 2176
    qnT = [arena[:, a0 + i * TM: a0 + (i + 1) * TM] for i in range(2)]
    a0 += 2 * TM
    qrT = [arena[:, a0 + i * TM: a0 + (i + 1) * TM] for i in range(2)]
    a0 += 2 * TM
    assert a0 <= 16384
    stage32 = [arena[:, i * 8192:(i + 1) * 8192].bitcast(F32) for i in range(2)]
    cnewv = cnew[:, 0:512].rearrange("p (c t) -> p c t", t=128)
    krnew = cnew[:, 512:640]

    hst_r = Ring(hst)
    tmpf_r = Ring(tmpf)
    tmpb_r = Ring(tmpb)
    P_r = Ring(Pbuf)
    PS_r = Ring([Pbuf[0][:, 0:1024], Pbuf[0][:, 1056:2080], Pbuf[1][:, 0:1024], Pbuf[1][:, 1056:2080]])
    PT_r = Ring(PTb)
    D_r = Ring(Dbuf)
    st_i = [0]

    def stat(n):
        if st_i[0] + n > 256:
            st_i[0] = 0
        a = stats[:, st_i[0]:st_i[0] + n]
        st_i[0] += n
        return a

    def psb(ps):
        return ps.bitcast(BF16)

    def dump_all():
        if UPTO == 'all' and not cfg.get('dump'):
            return
        for nm, t in [('xT32', xT32), ('A0', A0), ('arena', arena), ('kbT', kbT), ('vb', vb), ('kaT', kaT), ('va', va),
                      ('xstage', xstage_all), ('lnp', lnp), ('BM', BM), ('ropeA', ropeA), ('ckvT', ckvT), ('krT', krT),
                      ('lnt', lnt), ('ropeCf', ropeCf), ('ropeCt', ropeCt), ('sinks_b', sinks_b)]:
            ap = t[:]
            d = nc.dram_tensor('dbg_' + nm, list(ap.shape), ap.dtype, kind="ExternalOutput").ap()
            DMA(d[:, :], ap)

    DMA(identb[:], c_identb[:, :])
    DMA(identf[:], c_identf[:, :])
    DMA(antiI[:], c_antiI[:, :])
    DMA(onesm[:], c_ones[:, :])
    DMA(maskA[:], c_maskA[:, :])
    DMA(maskB, c_maskB[:, :])
    DMA(maskC[:], c_maskC[:, :])
    DMA(sinks_b[:], sinks[0:1, :].partition_broadcast(128))
    DMA(gq_b[:], g_q[0:1, :].partition_broadcast(128))
    DMA(gkv_b[:], g_kv[0:1, :].partition_broadcast(128))
    ts('dve', nsinks_b[:], sinks_b[:], -1.0, None, ALU.mult)
    for hb_ in (kbT, vb, kaT, va):
        OP('pool', (lambda e, hb_=hb_: e.memset(hb_[:], 0.0)), w=[hb_[:]])
    vst0 = xstage[0][:, 0:128]
    vst1 = xstage[0][0:32, 128:256]
    for v, src in enumerate([ln1_g, ln1_b, ln2_g, ln2_b]):
        DMA(xstage[0][v * 32:(v + 1) * 32, 0:128], src.rearrange("l (c p) -> (l c) p", p=128))
    DMA(vst1, b_gate.rearrange("l (c p) -> (l c) p", p=128))
    ps = psum_alloc(1)
    transposes([(ps[:, 0:128], vst0), (ps[:, 128:160], vst1)], identf)
    cp('act', lnp[:], ps[:, 0:160])

    def lnp_col(v, l, c):
        i = v * 32 + l * 16 + c
        return lnp[:, i:i + 1]

    tb8 = Sbuf[0:8, 0:257]
    hhs = xstage[1][0:8, 0:768]
    DMA(tb8, relb[:, :])
    cp('dve', hhs[:, 0:256], tb8[:, 1:257])
    cp('dve', hhs[:, 256:768], cap(tb8[:, 256:257], [[0, 512]]))
    DMA(hh[:, :], hhs)
    for h in range(8):
        hk = xstage[h % 2][:, 1024:1664].rearrange("p (c r) -> p c r", r=128)
        DMA(hk, bass.AP(hh.tensor, h * 768, [[1, 128], [128, 5], [1, 128]]))
        ps = psum_alloc(2)
        for c in range(5):
            mm_group(ps[:, (4 - c) * 128:(5 - c) * 128], [(hk[:, c, :], antiI[:])])
        tt('dve', BMv[:, h, :], ps[:, 0:640], maskB, ALU.add)

    cast_rr = [0]

    def cast(out, in_):
        cast_rr[0] += 1
        k = cast_rr[0] % 5
        eng = 'act' if k in (0, 2) else ('dve' if k in (1, 3) else 'pool')
        return cp(eng, out, in_)

    pre_i = [0]

    def cast2(out, in_):
        cast_rr[0] += 1
        return cp('act' if cast_rr[0] % 2 else 'dve', out, in_)

    def prepass_load(i, name, pi, src_ap, el, shape3, qb=False):
        stg = stage32[i % 2]
        if not qb:
            a, b = shape3
            DMA(stg[:, 0:el].rearrange("p (a b) -> p a b", b=b), src_ap)
        else:
            DMA(stg[:, 0:6 * 384].rearrange("p (a b) -> p a b", b=384), src_ap)

    def prepass_cast_store(i, name, pi, src_ap, el, shape3, qb=False):
        stg = stage32[i % 2]
        slot = wring[:, (i % NSLOT) * PE_ELEMS:(i % NSLOT) * PE_ELEMS + el]
        if not qb:
            cast2(slot, stg[:, 0:el])
        else:
            s4 = stg[:, 0:6 * 384].rearrange("p (a h c) -> p a h c", h=2, c=192)
            o4 = slot.rearrange("p (a h c) -> p a h c", h=2, c=256)
            cast2(o4[:, :, :, 0:192], s4)
            cast2(o4[:, :, :, 192:224], s4[:, :, :, 160:192])
            cast2(o4[:, :, :, 224:256], s4[:, :, :, 128:160])
        DMA(wsc[name][pi, :, 0:el], slot)

    def colpanel(w2d, pi, ncols=256, c0=None):
        c0 = pi * 256 if c0 is None else c0
        return w2d[:, c0:c0 + ncols].rearrange("(kc p) n -> p kc n", p=128)

    jobs = []
    for pi in range(18):
        jobs.append(('in_ab', pi, colpanel(w_in_ab, pi), 4096, (16, 256), False))
    for pi in range(8):
        jobs.append(('out_ab', pi, colpanel(w_out_ab, pi), 4096, (16, 256), False))
    for l in range(2):
        if l == 1:
            for pi in range(6):
                ncl = 256 if pi < 5 else 64
                jobs.append(('in_c', pi, colpanel(w_in_c, pi, ncl), 16 * ncl, (16, ncl), False))
            for pi in range(8):
                jobs.append(('q_b', pi, colpanel(w_q_b, pi, 384, pi * 384), 3072, None, True))
            for pi in range(8):
                jobs.append(('kv_b', pi, colpanel(w_kv_b, pi, 512, pi * 512), 2048, (4, 512), False))
            for pi in range(8):
                jobs.append(('out_c', pi, colpanel(w_out_c, pi), 4096, (16, 256), False))
        for pi in range(32):
            jobs.append(('up%d' % l, pi, colpanel(w_up[l], pi), 4096, (16, 256), False))
        for pi in range(32):
            ncx, jh = pi // 2, pi % 2
            src = w_down[l][jh * 4096:(jh + 1) * 4096, ncx * 128:(ncx + 1) * 128].rearrange("(j p) n -> p j n", p=128)
            jobs.append(('dn%d' % l, pi, src, 4096, (32, 128), False))
        for pi in range(8):
            jobs.append(('gate%d' % l, pi, colpanel(w_gate[l], pi), 4096, (16, 256), False))
        jobs.append(('ple%d' % l, 0, w_ple[l].rearrange("(kc p) n -> p kc n", p=128), 4096, (2, 2048), False))
    if 'prejobs' in cfg:
        jobs = [j for j in jobs if j[0] in cfg['prejobs']]
    if not enabled('pre1'):
        jobs = []
    elif not enabled('prepass'):
        jobs = jobs[0:2]
    for i, (nm, pi, src, el, shp, qb) in enumerate(jobs):
        if i == 0:
            prepass_load(0, nm, pi, src, el, shp, qb=qb)
        if i + 1 < len(jobs):
            nm2, pi2, src2, el2, shp2, qb2 = jobs[i + 1]
            prepass_load(i + 1, nm2, pi2, src2, el2, shp2, qb=qb2)
        prepass_cast_store(i, nm, pi, src, el, shp, qb=qb)
    pre_i[0] = len(jobs)

    PSEQ = []
    for t in tiles:
        PSEQ += tile_pseq(t[0])
    wst = {'pos': 0, 'issued': 0}

    def issue_load(j):
        name, pi = PSEQ[j]
        el = PSPEC[name][1]
        if name == 'in_c' and pi == 5:
            el = 1024
        s = (pre_i[0] + j) % NSLOT
        DMA(wring[:, s * PE_ELEMS:s * PE_ELEMS + el], wsc[name][pi, :, 0:el])

    def wget(name, pi):
        i = wst['pos']
        assert PSEQ[i] == (name, pi), (PSEQ[i], name, pi)
        wst['pos'] += 1
        while wst['issued'] <= min(i + LOOK, len(PSEQ) - 1):
            issue_load(wst['issued'])
            wst['issued'] += 1
        s = (pre_i[0] + i) % NSLOT
        return wring[:, s * PE_ELEMS:(s + 1) * PE_ELEMS]

    def rope_tok(src, nh, half, cos, sin, out1_of, out2_of):
        g = nh * 2
        s3 = src.rearrange("p (g c) -> p g c", c=half)
        tA = tmpf_r.next()[:, 0:g * half]
        tB = tmpf_r.next()[:, 0:g * half]
        tt('dve', tA.rearrange("p (g c) -> p g c", c=half), s3, cap(cos, [[0, g], [1, half]]), ALU.mult)
        tt('pool', tB.rearrange("p (g c) -> p g c", c=half), s3, cap(sin, [[0, g], [1, half]]), ALU.mult)
        a4 = tA.rearrange("p (h t c) -> p h t c", t=2, c=half)
        b4 = tB.rearrange("p (h t c) -> p h t c", t=2, c=half)
        tt('dve', out1_of, a4[:, :, 0, :], b4[:, :, 1, :], ALU.subtract)
        tt('pool', out2_of, a4[:, :, 1, :], b4[:, :, 0, :], ALU.add)

    def layer_norm(l, which, Tt):
        usum = lnt[:, 0:Tt]
        qsum = lnt[:, TM:TM + Tt]
        mean = lnt[:, 2 * TM:2 * TM + Tt]
        var = lnt[:, 3 * TM:3 * TM + Tt]
        rstd = lnt[:, 4 * TM:4 * TM + Tt]
        nmr = lnt[:, 5 * TM:5 * TM + Tt]
        ps = psum_alloc(1)
        mm_group(ps[:, 0:Tt], [(onesm[:], usum)])
        mm_group(ps[:, 256:256 + Tt], [(onesm[:], qsum)])
        cp('act', mean, ps[:, 0:Tt])
        tt('dve', usum, mean, mean, ALU.mult)
        stt(var, ps[:, 256:256 + Tt], LN_EPS, usum, ALU.add, ALU.subtract)
        act(var, var, AF.Sqrt)
        OP('dve', lambda e: e.reciprocal(out=rstd, in_=var), r=[var], w=[rstd])
        stt(nmr, mean, -1.0, rstd, ALU.mult, ALU.mult)
        gv, bv = (0, 1) if which == 0 else (2, 3)
        pend_cast = None
        for g4 in range(4):
            xs4 = xT32v[:, g4 * 4:(g4 + 1) * 4, 0:Tt]
            tt('dve', xs4, xs4, cap(rstd, [[0, 4], [1, Tt]]), ALU.mult)
            tt('dve', xs4, xs4, cap(nmr, [[0, 4], [1, Tt]]), ALU.add)
            if pend_cast is not None:
                pend_cast()
            for c in range(g4 * 4, g4 * 4 + 4):
                xc = xT32v[:, c, 0:Tt]
                act(xc, xc, AF.Identity, scale=lnp_col(gv, l, c), bias=lnp_col(bv, l, c),
                    r_extra=[lnp_col(gv, l, c), lnp_col(bv, l, c)])
            pend_cast = (lambda g4=g4, xs4=xs4: cp('dve', A0v[:, g4 * 4:(g4 + 1) * 4, 0:Tt], xs4))
        pend_cast()

    def resid_epilogue(ps_ap, c, Tt):
        xc = xT32v[:, c, 0:Tt]
        usum = lnt[:, 0:Tt]
        qsum = lnt[:, TM:TM + Tt]
        stt(xc, xc, ALPHA, ps_ap, ALU.mult, ALU.add)
        sq = tmpf_r.next()[:, 0:Tt]
        act(sq, xc, AF.Square)
        if c == 0:
            cp('dve', usum, xc)
            cp('pool', qsum, sq)
        else:
            tt('dve', usum, usum, xc, ALU.add)
            tt('pool', qsum, qsum, sq, ALU.add)

    def dense_fm(name, npan, Tt, epi):
        for pi in range(npan):
            slot = wget(name, pi)
            sv = slot.rearrange("p (k n) -> p k n", n=256)
            for j in range(2):
                ps = psum_alloc(1)
                mm_group(ps[:, 0:Tt], [(sv[:, kc, j * 128:(j + 1) * 128], A0v[:, kc, 0:Tt]) for kc in range(16)])
                epi(ps, 2 * pi + j)

    def mlp(l, Tt):
        for pi in range(32):
            slot = wget('up%d' % l, pi)
            sv = slot.rearrange("p (k n) -> p k n", n=256)
            ps = psum_alloc(1)
            for j in range(2):
                mm_group(ps[:, j * 256:j * 256 + Tt], [(sv[:, kc, j * 128:(j + 1) * 128], A0v[:, kc, 0:Tt]) for kc in range(16)])
            rl = lnt[:, (1 + pi % 2) * 2 * TM:(1 + pi % 2) * 2 * TM + 2 * TM].rearrange("p (j t) -> p j t", t=TM)[:, :, 0:Tt]
            p3 = ps.rearrange("p (j t) -> p j t", t=256)[:, :, 0:Tt]
            act(rl, p3, AF.Relu)
            tt('pool' if pi % 2 else 'dve', hidTv[:, 2 * pi:2 * pi + 2, 0:Tt], rl, rl, ALU.mult)
        for ncx in range(16):
            ps = psum_alloc(1)
            for jh in range(2):
                slot = wget('dn%d' % l, ncx * 2 + jh)
                sv = slot.rearrange("p (j n) -> p j n", n=128)
                mm_group(ps[:, 0:Tt], [(sv[:, j, :], hidTv[:, jh * 32 + j, 0:Tt]) for j in range(32)],
                         start=(jh == 0), stop=(jh == 1))
            resid_epilogue(ps[:, 0:Tt], ncx, Tt)
        layer_norm(l, 1, Tt)

    def p_prep(ntb, psrc_of_tb):
        for tb in range(ntb):
            pst_ = hst_r.next()
            pbf_ = tmpb_r.next()
            DMA(pst_[:], psrc_of_tb(tb))
            cp('pool', pbf_[:], pst_[:])
            ps = psum_alloc(1)
            pb_ = psb(ps).rearrange("p (j t) -> p j t", t=128)
            transposes([(pb_[:, j, :], pbf_[:, j * 128:(j + 1) * 128]) for j in range(2)], identb)
            evac(pTv[:, :, tb * 128:(tb + 1) * 128], pb_[:, 0:2, :])

    def gate_ple(l, Tt, ntb, psrc_of_tb, last):
        DMA(plew.rearrange("p k n -> p (k n)"), wsc['ple%d' % l][0, :, :])
        for c8 in range(8):
            slot = wget('gate%d' % l, c8)
            sv = slot.rearrange("p (k n) -> p k n", n=256)
            for j in range(2):
                c = 2 * c8 + j
                ps = psum_alloc(1)
                mm_group(ps[:, 0:Tt], [(sv[:, kc, j * 128:(j + 1) * 128], A0v[:, kc, 0:Tt]) for kc in range(16)])
                mm_group(ps[:, 256:256 + Tt], [(plew[:, kc, c * 128:(c + 1) * 128], pTv[:, kc, 0:Tt]) for kc in range(2)])
                gt = tmpf_r.next()[:, 0:Tt]
                act(gt, ps[:, 0:Tt], AF.Sigmoid, bias=lnp_col(4, l, c), r_extra=[lnp_col(4, l, c)])
                tt('dve', gt, gt, ps[:, 256:256 + Tt], ALU.mult)
                xc = xT32v[:, c, 0:Tt]
                tt('pool', xc, xc, gt, ALU.add)
        if not last:
            for g4 in range(4):
                cp('act' if g4 % 2 else 'dve', A0v[:, g4 * 4:(g4 + 1) * 4, 0:Tt], xT32v[:, g4 * 4:(g4 + 1) * 4, 0:Tt])

    def softmax_tail(nq, den, P, nheads=1):
        rden = stat(nheads)
        OP('dve', lambda e: e.reciprocal(out=rden[0:nq, :], in_=den), r=[den], w=[rden[0:nq, :]])
        Dt = D_r.next()
        if nheads == 1:
            Dv = Dt[0:nq, 0:nq]
            ts('dve', Dv, identb[0:nq, 0:nq], rden[0:nq, 0:1], None, ALU.mult, r_extra=[rden[0:nq, 0:1]])
        else:
            Dv = Dt[0:nq, 0:nheads * nq].rearrange("p (h q) -> p h q", q=nq)
            tt('dve', Dv, cap(identb[0:nq, 0:1], [[0, nheads], [1, nq]]), cap(rden[0:nq, 0:1], [[1, nheads], [0, nq]]), ALU.mult)
        return Dv

    def attn_A(g, qc0, nq, kc0, kbl, maskap):
        nk = sum(k for k, _ in kbl)
        st = {}
        nkb = len(kbl)

        def sA():
            ps = psum_at(pool_next('l0sc', [0, 2]), 2)
            p3 = ps.rearrange("p (h k) -> p h k", k=256)
            for hh in range(4):
                mm_group(p3[0:nq, hh, 0:nk], [(qaTv[:, 4 * g + hh, qc0:qc0 + nq], kaTv[:, g, kc0:kc0 + nk])])
            st['S3'] = p3[0:nq, :, 0:nk]

        def sB():
            S3 = st['S3']
            tt('dve', S3, S3, cap(maskap, [[0, 4], [1, nk]]), ALU.add)
            mx = stat(4)[0:nq, :]
            OP('dve', lambda e: e.tensor_reduce(out=mx, in_=S3, axis=AX.X, op=ALU.max), r=[S3], w=[mx])
            negm = stat(4)[0:nq, :]
            stt(negm, mx, -1.0, nsinks_b[0:nq, 4 * g:4 * g + 4], ALU.mult, ALU.min)
            P = PS_r.next()
            den = stat(4)[0:nq, :]
            for hh in range(4):
                act(P[0:nq, hh * 256:hh * 256 + nk], S3[:, hh, :], AF.Exp, bias=negm[:, hh:hh + 1], scale=1.0,
                    accum_out=den[:, hh:hh + 1], r_extra=[negm[:, hh:hh + 1]])
            tmp = stat(4)[0:nq, :]
            tt('dve', tmp, sinks_b[0:nq, 4 * g:4 * g + 4], negm, ALU.add)
            es_ = stat(4)[0:nq, :]
            act(es_, tmp, AF.Exp)
            st['P'] = P
            st['den'] = den
            st['es'] = es_

        def sC():
            den2 = stat(4)[0:nq, :]
            tt('dve', den2, st['den'], st['es'], ALU.add)
            P = st['P']
            Dv = softmax_tail(nq, den2, P, 4)
            ps = psum_at(4, 2)
            p4 = ps.rearrange("p (h b q) -> p h b q", b=2, q=128)
            off = 0
            for kb, (ks, _) in enumerate(kbl):
                for hh in range(4):
                    mm_group(p4[0:ks, hh, kb, 0:nq], [(P[0:nq, hh * 256 + off:hh * 256 + off + ks], Dv[:, hh, :])])
                off += ks
            PT = PT_r.next()
            PT4 = PT[:].rearrange("p (h b q) -> p h b q", b=2, q=128)
            if all(ks == 128 for ks, _ in kbl):
                cp('act', PT4[:, :, 0:nkb, 0:nq], p4[:, :, 0:nkb, 0:nq])
            else:
                for kb, (ks, _) in enumerate(kbl):
                    cp('act', PT4[0:ks, :, kb, 0:nq], p4[0:ks, :, kb, 0:nq])
            st['PT4'] = PT4

        def sD():
            PT4 = st['PT4']
            po = psum_at(pool_next('l0po', [6, 7]))
            po3 = po.rearrange("p (h q) -> p h q", q=128)
            for hh in range(4):
                mm_group(po3[:, hh, 0:nq], [(vblk[0:ks, g * 128:(g + 1) * 128], PT4[0:ks, hh, kb, 0:nq])
                                           for kb, (ks, vblk) in enumerate(kbl)])
            st['po3'] = po3

        def sE():
            cp('dve', A0v[:, 4 * g:4 * g + 4, qc0:qc0 + nq], st['po3'][:, :, 0:nq])
        return sA, sB, sC, sD, sE

    def attn_B(h, qc0, nq, kc0, kbl, bias_ap):
        nk = sum(k for k, _ in kbl)
        st = {}
        nkb = len(kbl)

        def sA():
            ps = psum_at(pool_next('l0sc', [0, 2]), 2)
            k0 = 0
            while k0 < nk:
                n = min(512, nk - k0)
                mm_group(ps[0:nq, k0:k0 + n], [(qbTv[:, h, qc0:qc0 + nq], kbTv[:, h, kc0 + k0:kc0 + k0 + n])])
                k0 += n
            st['S2'] = ps[0:nq, 0:nk]

        def sB():
            S2 = st['S2']
            tt('dve', S2, S2, bias_ap, ALU.add)
            negm = stat(1)[0:nq, :]
            OP('dve', lambda e: e.tensor_reduce(out=negm, in_=S2, axis=AX.X, op=ALU.max, negate=True), r=[S2], w=[negm])
            P = PS_r.next()
            den = stat(1)[0:nq, :]
            act(P[0:nq, 0:nk], S2, AF.Exp, bias=negm, scale=1.0, accum_out=den, r_extra=[negm])
            st['P'] = P
            st['den'] = den

        def sC():
            P = st['P']
            Dv = softmax_tail(nq, st['den'], P, 1)
            ps = psum_at(4, 2)
            p3 = ps.rearrange("p (b q) -> p b q", q=128)
            off = 0
            for kb, (ks, _) in enumerate(kbl):
                mm_group(p3[0:ks, kb, 0:nq], [(P[0:nq, off:off + ks], Dv)])
                off += ks
            PT = PT_r.next()
            PT3 = PT[:].rearrange("p (b q) -> p b q", q=128)
            if all(ks == 128 for ks, _ in kbl):
                cp('act', PT3[:, 0:nkb, 0:nq], p3[:, 0:nkb, 0:nq])
            else:
                for kb, (ks, _) in enumerate(kbl):
                    cp('act', PT3[0:ks, kb, 0:nq], p3[0:ks, kb, 0:nq])
            st['PT3'] = PT3

        def sD():
            PT3 = st['PT3']
            po = psum_at(pool_next('l0po', [6, 7]))
            mm_group(po[:, 0:nq], [(vblk[0:ks, h * 128:(h + 1) * 128], PT3[0:ks, kb, 0:nq])
                                   for kb, (ks, vblk) in enumerate(kbl)])
            st['po'] = po

        def sE():
            cp('dve', A0v[:, 8 + h, qc0:qc0 + nq], st['po'][:, 0:nq])
        return sA, sB, sC, sD, sE

    def attn_C(hslot, h, qc0, nq, kbl, mask_last):
        nk = sum(kbl)
        st = {}

        def s1():
            nb = (nk + 511) // 512
            ps = (psum_at(pool_next('l1sc', [0, 2]), nb) if nb <= 2 else psum_at(0, nb))
            k0 = 0
            while k0 < nk:
                n = min(512, nk - k0)
                mm_group(ps[0:nq, k0:k0 + n], [(qnT[hslot][:, qc0:qc0 + nq], KhT[hslot][:, k0:k0 + n]),
                                               (qrT[hslot][0:64, qc0:qc0 + nq], krT[0:64, k0:k0 + n])])
                k0 += n
            if mask_last:
                tt('dve', ps[0:nq, nk - 128:nk], ps[0:nq, nk - 128:nk], maskC[0:nq, :], ALU.add)
            negm = stat(1)[0:nq, :]
            pin = ps[0:nq, 0:nk]
            OP('dve', lambda e: e.tensor_reduce(out=negm, in_=pin, axis=AX.X, op=ALU.max, negate=True), r=[pin], w=[negm])
            P = P_r.next()
            den = stat(1)[0:nq, :]
            act(P[0:nq, 0:nk], pin, AF.Exp, bias=negm, scale=1.0, accum_out=den, r_extra=[negm])
            st['P'] = P
            st['den'] = den

        def s2():
            P = st['P']
            Dv = softmax_tail(nq, st['den'], P, 1)
            po = psum_at(7)
            nkb = len(kbl)
            groups = [list(range(g0, min(g0 + 4, nkb))) for g0 in range(0, nkb, 4)]
            pend = None
            for gi, grp in enumerate(groups):
                ps = psum_at(pool_next('l1s', C_SINGLES[0]))
                p3 = ps.rearrange("p (b q) -> p b q", q=128)
                for j, kb in enumerate(grp):
                    ks = kbl[kb]
                    mm_group(p3[0:ks, j, 0:nq], [(P[0:nq, kb * 128:kb * 128 + ks], Dv)])
                PT = PT_r.next()
                PT3 = PT[:, 0:512].rearrange("p (b q) -> p b q", q=128)
                if all(kbl[kb] == 128 for kb in grp):
                    evac(PT3[:, 0:len(grp), 0:nq], p3[:, 0:len(grp), 0:nq])
                else:
                    for j, kb in enumerate(grp):
                        evac(PT3[0:kbl[kb], j, 0:nq], p3[0:kbl[kb], j, 0:nq])
                if pend is not None:
                    pend()

                def pv(grp=grp, PT3=PT3):
                    mm_group(po[:, 0:nq], [(Vh[hslot][0:kbl[kb], kb, :], PT3[0:kbl[kb], j, 0:nq]) for j, kb in enumerate(grp)],
                             start=(grp[0] == 0), stop=(grp[-1] == nkb - 1))
                pend = pv
            pend()
            evac(A0v[:, h, qc0:qc0 + nq], po[:, 0:nq])
        return s1, s2

    def run_pipelined3(items):
        n = len(items)
        for r in range(n + 4):
            for stage, lag in ((2, 2), (0, 0), (3, 3), (1, 1), (4, 4)):
                i = r - lag
                if 0 <= i < n:
                    items[i][stage]()

    def run_pipelined(items, depth=1):
        pend = []
        for s1, s2 in items:
            s1()
            pend.append(s2)
            if len(pend) > depth:
                pend.pop(0)()
        while pend:
            pend.pop(0)()

    def run_tile(tile):
        kind = tile[0]
        if kind == 'prompt':
            _, s, pos0 = tile
            Tt = TM
        else:
            s, pos0, Tt = None, PAST, 128
        ntb = Tt // 128
        first = (kind == 'prompt' and pos0 == 0)
        if kind == 'prompt':
            psrc0 = lambda tb: pp[0, s, pos0 + tb * 128:pos0 + (tb + 1) * 128, :]
            psrc1 = lambda tb: pp[1, s, pos0 + tb * 128:pos0 + (tb + 1) * 128, :]
        else:
            psrc0 = lambda tb: psm[0, :, :]
            psrc1 = lambda tb: psm[1, :, :]

        for tb in range(ntb):
            if kind == 'prompt':
                p = pos0 + tb * 128
                DMA(ropeAv[:, tb, :], c_ropeA[p:p + 128, :])
                DMA(ropeCtv[:, tb, :], c_ropeCt[p:p + 128, :])
            else:
                for s2 in range(2):
                    DMA(ropeAv[s2 * 64:(s2 + 1) * 64, 0, :], c_ropeA[PAST:PAST + 64, :])
                    DMA(ropeCtv[s2 * 64:(s2 + 1) * 64, 0, :], c_ropeCt[PAST:PAST + 64, :])
        if kind == 'prompt':
            DMA(ropeCfv[0:64, :, 0:Tt], c_ropeCf[:, :, pos0:pos0 + Tt])
        else:
            for s2 in range(2):
                DMA(ropeCfv[0:64, :, s2 * 64:(s2 + 1) * 64], c_ropeCf[:, :, PAST:PAST + 64])

        if not enabled('x'):
            return
        phase('x')
        for tb in range(ntb):
            stg = xstage[tb % 2]
            src = xp[s, pos0 + tb * 128:pos0 + (tb + 1) * 128, :] if kind == 'prompt' else xs[:, :]
            DMA(stg[:], src)
            for g in range(4):
                ps = psum_alloc(1)
                transposes([(ps[:, j * 128:(j + 1) * 128], stg[:, (g * 4 + j) * 128:(g * 4 + j + 1) * 128]) for j in range(4)], identf)
                p3 = ps.rearrange("p (j t) -> p j t", t=128)
                cp('act', xT32v[:, g * 4:(g + 1) * 4, tb * 128:(tb + 1) * 128], p3)
                cp('dve', A0v[:, g * 4:(g + 1) * 4, tb * 128:(tb + 1) * 128], p3)

        if not enabled('l0in'):
            return
        phase('l0in')
        pendq = []

        def run_pend(keep):
            while len(pendq) > keep:
                pendq.pop(0)()

        def tr_post(src_bf, dst_fn, scale=None):
            def post():
                pt = psum_alloc(1)
                pb_ = psb(pt).rearrange("p (j t) -> p j t", t=128)
                transposes([(pb_[:, j, :], src_bf[:, j * 128:(j + 1) * 128]) for j in range(2)], identb)
                dst_fn(pb_[:, 0:2, :])
            pendq.append(post)

        for pi in range(18):
            slot = wget('in_ab', pi)
            sv = slot.rearrange("p (k n) -> p k n", n=256)
            for tb in range(ntb):
                tok0 = tb * 128
                ps = psum_alloc(1)
                mm_group(ps[:, 0:256], [(A0v[:, kc, tok0:tok0 + 128], sv[:, kc, :]) for kc in range(16)])
                run_pend(2)
                pin = ps[:, 0:256]
                cosA = ropeAv[:, tb, 0:64]
                sinA = ropeAv[:, tb, 64:128]
                if pi < 4:
                    h_ = hst_r.next()
                    cp('act', h_[:], pin)
                    qrot = tmpb_r.next()
                    q3 = qrot[:].rearrange("p (h c) -> p h c", c=128)
                    rope_tok(h_[:], 2, 64, cosA, sinA, q3[:, :, 0:64], q3[:, :, 64:128])
                    tr_post(qrot, lambda src, pi=pi, tok0=tok0: act(qaTv[:, 2 * pi:2 * pi + 2, tok0:tok0 + 128], src, AF.Copy, scale=SC_AB))
                elif pi == 4:
                    h_ = hst_r.next()
                    cp('act', h_[:], pin)
                    kro = hst_r.next()
                    k3 = kro[:].rearrange("p (h c) -> p h c", c=128)
                    rope_tok(h_[:], 2, 64, cosA, sinA, k3[:, :, 0:64], k3[:, :, 64:128])
                    if kind == 'prompt' and pos0 + tok0 == SEQ - 128:
                        DMA(pak[s, :, :], kro[:])
                    if kind == 'sample':
                        for s2 in range(2):
                            DMA(sak[s2, 64:128, :], kro[s2 * 64:(s2 + 1) * 64, :])
                    kbf = tmpb_r.next()
                    cp('act', kbf[:], kro[:])
                    tr_post(kbf, lambda src, tok0=tok0: evac(kaTv[:, 0:2, 128 + tok0:128 + tok0 + 128], src))
                elif pi == 5:
                    h_ = hst_r.next()
                    cp('act', h_[:], pin)
                    if kind == 'prompt' and pos0 + tok0 == SEQ - 128:
                        DMA(pav[s, :, :], h_[:])
                    if kind == 'sample':
                        for s2 in range(2):
                            DMA(sav[s2, 64:128, :], h_[s2 * 64:(s2 + 1) * 64, :])
                    cp('pool', vav[:, 1 + tb, :], h_[:])
                elif pi < 10:
                    hb = pi - 6
                    qt = tmpb_r.next()
                    act(qt[:], pin, AF.Copy, scale=SC_AB)
                    tr_post(qt, lambda src, hb=hb, tok0=tok0: cp('dve', qbTv[:, 2 * hb:2 * hb + 2, tok0:tok0 + 128], src))
                elif pi < 14:
                    hb = pi - 10
                    h_ = hst_r.next()
                    cp('act', h_[:], pin)
                    if kind == 'prompt' and pos0 + tok0 >= SEQ - 512:
                        r0 = pos0 + tok0 - (SEQ - 512)
                        DMA(pbk[s, r0:r0 + 128, hb * 256:(hb + 1) * 256], h_[:])
                    if kind == 'sample':
                        for s2 in range(2):
                            DMA(sbk[s2, 448:512, hb * 256:(hb + 1) * 256], h_[s2 * 64:(s2 + 1) * 64, :])
                    kt = tmpb_r.next()
                    cp('pool', kt[:], h_[:])
                    tr_post(kt, lambda src, hb=hb, tok0=tok0: cp('dve', kbTv[:, 2 * hb:2 * hb + 2, 512 + tok0:512 + tok0 + 128], src))
                else:
                    hb = pi - 14
                    h_ = hst_r.next()
                    cp('act', h_[:], pin)
                    if kind == 'prompt' and pos0 + tok0 >= SEQ - 512:
                        r0 = pos0 + tok0 - (SEQ - 512)
                        DMA(pbv[s, r0:r0 + 128, hb * 256:(hb + 1) * 256], h_[:])
                    if kind == 'sample':
                        for s2 in range(2):
                            DMA(sbv[s2, 448:512, hb * 256:(hb + 1) * 256], h_[s2 * 64:(s2 + 1) * 64, :])
                    cp('pool', vbv[:, 4 + tb, hb * 256:(hb + 1) * 256], h_[:])
        run_pend(0)

        if not enabled('l0attn'):
            return
        phase('l0attn')
        if kind == 'prompt':
            items = []
            for tb in range(ntb):
                qc0 = tb * 128
                if first and tb == 0:
                    kbl = [(128, vav[:, 1, :])]
                    kc0 = 128
                    mk = maskA[:, 128:256]
                else:
                    kbl = [(128, vav[:, tb, :]), (128, vav[:, tb + 1, :])]
                    kc0 = tb * 128
                    mk = maskA[:, 0:256]
                for g in range(2):
                    items.append(attn_A(g, qc0, 128, kc0, kbl, mk))
            run_pipelined3(items)
            for tb in range(ntb):
                qc0 = tb * 128
                nvalid = min(640, pos0 + tb * 128 + 128)
                nskip = (640 - nvalid) // 128
                kc0 = tb * 128 + nskip * 128
                kblB = [(128, vbv[:, tb + j, :]) for j in range(nskip, 5)]
                items = []
                for h in range(8):
                    items.append(attn_B(h, qc0, 128, kc0, kblB, BMv[:, h, nskip * 128:640]))
                run_pipelined3(items)
        else:
            for s2 in range(2):
                qc0 = s2 * 64
                stg = xstage[0]
                DMA(stg[:, 0:256], cak[s2, :, :])
                kbf = tmpb_r.next()
                cp('pool', kbf[:], stg[:, 0:256])
                pt = psum_alloc(1)
                pb_ = psb(pt).rearrange("p (j t) -> p j t", t=128)
                transposes([(pb_[:, j, :], kbf[:, j * 128:(j + 1) * 128]) for j in range(2)], identb)
                evac(kaTv[:, 0:2, 0:128], pb_[:, 0:2, :])
                DMA(stg[:, 256:512], cav[s2, :, :])
                cp('pool', vav[:, 0, :], stg[:, 256:512])
                for blk in range(4):
                    st2 = xstage[(blk + 1) % 2]
                    DMA(st2[:, 0:1024], cbk[s2, blk * 128:(blk + 1) * 128, :])
                    kt = PT_r.next()
                    cp('dve' if blk % 2 else 'pool', kt[:], st2[:, 0:1024])
                    pt = psum_alloc(1)
                    pb_ = psb(pt).rearrange("p (j t) -> p j t", t=128)
                    transposes([(pb_[:, j, :], kt[:, j * 128:(j + 1) * 128]) for j in range(8)], identb)
                    evac(kbTv[:, :, blk * 128:(blk + 1) * 128], pb_[:, 0:8, :])
                    DMA(st2[:, 1024:2048], cbv[s2, blk * 128:(blk + 1) * 128, :])
                    cp('pool' if blk % 2 else 'dve', vbv[:, blk, :], st2[:, 1024:2048])
                if s2 == 1:
                    cp('pool', kaTv[:, :, 128:192], kaTv[:, :, 192:256])
                    cp('pool', kbTv[:, :, 512:576], kbTv[:, :, 576:640])
                    DMA(vav[0:64, 2, :], vav[64:128, 1, :])
                    DMA(vbv[0:64, 5, :], vbv[64:128, 4, :])
                vnewA = vav[:, 1, :] if s2 == 0 else vav[:, 2, :]
                vnewB = vbv[:, 4, :] if s2 == 0 else vbv[:, 5, :]
                items = []
                kbl = [(128, vav[:, 0, :]), (64, vnewA)]
                for g in range(2):
                    items.append(attn_A(g, qc0, 64, 0, kbl, maskA[0:64, 0:192]))
                run_pipelined3(items)
                kblB = [(128, vbv[:, j, :]) for j in range(4)] + [(64, vnewB)]
                items = []
                for h in range(8):
                    items.append(attn_B(h, qc0, 64, 0, kblB, BMv[0:64, h, 0:576]))
                run_pipelined3(items)
                DMA(sak[s2, 0:64, :], cak[s2, 64:128, :])
                DMA(sav[s2, 0:64, :], cav[s2, 64:128, :])
                DMA(sbk[s2, 0:448, :], cbk[s2, 64:512, :])
                DMA(sbv[s2, 0:448, :], cbv[s2, 64:512, :])

        if kind == 'prompt' and pos0 + Tt < SEQ:
            for i in range(512 // Tt):
                cp('pool', kbTv[:, :, i * Tt:(i + 1) * Tt], kbTv[:, :, (i + 1) * Tt:(i + 2) * Tt])
                cp('pool', vbv[:, i * ntb:(i + 1) * ntb, :], vbv[:, (i + 1) * ntb:(i + 2) * ntb, :])
            cp('pool', kaTv[:, :, 0:128], kaTv[:, :, Tt:Tt + 128])
            cp('pool', vav[:, 0, :], vav[:, ntb, :])

        if not enabled('l0out'):
            return
        phase('l0out')
        dense_fm('out_ab', 8, Tt, lambda ps, c: resid_epilogue(ps[:, 0:Tt], c, Tt))
        if not enabled('l0ln1'):
            return
        phase('l0ln1')
        layer_norm(0, 0, Tt)
        phase('l0mlp')
        if not enabled('l0mlp'):
            return
        p_prep(ntb, psrc0)
        mlp(0, Tt)
        if not enabled('l0gate'):
            return
        phase('l0gate')
        gate_ple(0, Tt, ntb, psrc0, False)

        if not enabled('l1in'):
            return
        phase('l1in')
        for pi in range(6):
            ncl = 256 if pi < 5 else 64
            slot = wget('in_c', pi)
            sv = slot[:, 0:16 * ncl].rearrange("p (k n) -> p k n", n=ncl)
            for tb in range(ntb):
                tok0 = tb * 128
                ps = psum_alloc(1)
                mm_group(ps[:, 0:ncl], [(A0v[:, kc, tok0:tok0 + 128], sv[:, kc, :]) for kc in range(16)])
                evac(hcv[:, tb, pi * 256:pi * 256 + ncl], ps[:, 0:ncl])
        for tb in range(ntb):
            tok0 = tb * 128
            ssq = stat(2)
            act(Sbuf[:, 0:768], hcv[:, tb, 0:768], AF.Square, accum_out=ssq[:, 0:1])
            act(Sbuf[:, 0:512], hcv[:, tb, 768:1280], AF.Square, accum_out=ssq[:, 1:2])
            t2 = stat(2)
            ts('dve', t2[:, 0:1], ssq[:, 0:1], 1.0 / 768, RMS_EPS, ALU.mult, ALU.add)
            ts('dve', t2[:, 1:2], ssq[:, 1:2], 1.0 / 512, RMS_EPS, ALU.mult, ALU.add)
            sd = stat(2)
            act(sd, t2, AF.Sqrt)
            rs = stat(2)
            OP('dve', lambda e, rs=rs, sd=sd: e.reciprocal(out=rs, in_=sd), r=[sd], w=[rs])
            cqn = PT_r.next()
            stt(cqn[:, 0:768], hcv[:, tb, 0:768], rs[:, 0:1], gq_b[:], ALU.mult, ALU.mult, r_extra=[rs[:, 0:1]])
            ckv32 = hcv[:, tb, 768:1280]
            stt(ckv32, ckv32, rs[:, 1:2], gkv_b[:], ALU.mult, ALU.mult, r_extra=[rs[:, 1:2]])
            ckvb = PT_r.next()
            cp('pool', ckvb[:, 0:512], ckv32)
            kro = hst_r.next()
            rope_tok(hcv[:, tb, 1280:1344], 1, 32, ropeCtv[:, tb, 0:32], ropeCtv[:, tb, 32:64],
                     kro[:, 0:32].rearrange("p (h c) -> p h c", c=32), kro[:, 32:64].rearrange("p (h c) -> p h c", c=32))
            krb = tmpb_r.next()
            cp('act', krb[:, 0:64], kro[:, 0:64])
            if kind == 'prompt':
                p = pos0 + tok0
                DMA(pckv[s, p:p + 128, :], ckv32)
                DMA(pckr[s, p:p + 128, :], kro[:, 0:64])
            else:
                DMA(sckv[:, :], ckv32)
                DMA(sckr[:, :], kro[:, 0:64])
            pt = psum_alloc(1)
            pb_ = psb(pt).rearrange("p (j t) -> p j t", t=128)
            transposes([(pb_[:, j, :], cqn[:, j * 128:(j + 1) * 128]) for j in range(6)], identb)
            evac(cqTv[:, 0:6, tok0:tok0 + 128], pb_[:, 0:6, :])
            pt = psum_alloc(1)
            pb_ = psb(pt).rearrange("p (j t) -> p j t", t=128)
            transposes([(pb_[:, j, :], ckvb[:, j * 128:(j + 1) * 128]) for j in range(4)] +
                       [(pb_[0:64, 4, :], krb[:, 0:64])], identb)
            if kind == 'prompt':
                p = pos0 + tok0
                evac(ckvTv[:, 0:4, p:p + 128], pb_[:, 0:4, :])
                evac(krT[0:64, p:p + 128], pb_[0:64, 4, :])
            else:
                evac(cnewv[:, 0:4, :], pb_[:, 0:4, :])
                evac(krnew[0:64, :], pb_[0:64, 4, :])

        if not enabled('l1heads'):
            return
        phase('l1heads')
        def heads_loop(qsl_list, nk_total, kbl_full):
            prev_items = None
            C_SINGLES[0] = [5, 6] if nk_total > 2048 else [4, 5, 6]
            for hp in range(8):
                qslot = wget('q_b', hp)
                qv = qslot[:, 0:3072].rearrange("p (k h c) -> p k h c", h=2, c=256)
                kvslot = wget('kv_b', hp)
                kvv = kvslot[:, 0:2048].rearrange("p (k h c) -> p k h c", h=2, c=256)
                for hh in range(2):
                    h = 2 * hp + hh
                    hs = h % 2
                    ps = psum_at(C_SINGLES[0][0], 2)
                    mm_group(ps[:, 0:Tt], [(qv[:, kc, hh, 0:128], cqTv[:, kc, 0:Tt]) for kc in range(6)])
                    mm_group(ps[0:64, 256:256 + Tt], [(qv[:, kc, hh, 128:192], cqTv[:, kc, 0:Tt]) for kc in range(6)])
                    mm_group(ps[0:64, 512:512 + Tt], [(qv[:, kc, hh, 192:256], cqTv[:, kc, 0:Tt]) for kc in range(6)])
                    act(qnT[hs][:, 0:Tt], ps[:, 0:Tt], AF.Copy, scale=SC_C)
                    t1 = tmpf_r.next()[0:64, 0:Tt]
                    t2_ = tmpf_r.next()[0:64, 0:Tt]
                    tt('dve', t1, ps[0:64, 256:256 + Tt], ropeCfv[0:64, 0, 0:Tt], ALU.mult)
                    tt('dve', t2_, ps[0:64, 512:512 + Tt], ropeCfv[0:64, 1, 0:Tt], ALU.mult)
                    tt('pool', qrT[hs][0:64, 0:Tt], t1, t2_, ALU.add)
                    pit = list(prev_items) if prev_items is not None else []
                    coarse = (nk_total <= 1024 and len(pit) == 2)
                    if coarse:
                        pit[0][0]()
                        pit[1][0]()
                    elif pit:
                        pit[0][0]()
                    k0 = 0
                    while k0 < nk_total:
                        n = min(512, nk_total - k0)
                        ps = psum_at(pool_next('l1s', C_SINGLES[0]))
                        mm_group(ps[:, 0:n], [(kvv[:, kc, hh, 0:128], ckvTv[:, kc, k0:k0 + n]) for kc in range(4)])
                        evac(KhT[hs][:, k0:k0 + n], ps[:, 0:n])
                        k0 += n
                    if pit and not coarse:
                        pit[0][1]()
                        for it in pit[1:2]:
                            it[0]()
                    nkb = len(kbl_full)
                    for g0 in range(0, nkb, 4):
                        grp = list(range(g0, min(g0 + 4, nkb)))
                        ps = psum_at(pool_next('l1s', C_SINGLES[0]))
                        p3 = ps.rearrange("p (b d) -> p b d", d=128)
                        for j, kb in enumerate(grp):
                            ks = kbl_full[kb]
                            mm_group(p3[0:ks, j, :], [(ckvTv[:, kc, kb * 128:kb * 128 + ks], kvv[:, kc, hh, 128:256]) for kc in range(4)])
                        if all(kbl_full[kb] == 128 for kb in grp):
                            evac(Vh[hs][:, g0:g0 + len(grp), :], p3[:, 0:len(grp), :])
                        else:
                            for j, kb in enumerate(grp):
                                evac(Vh[hs][0:kbl_full[kb], kb, :], p3[0:kbl_full[kb], j, :])
                    if coarse:
                        pit[0][1]()
                        pit[1][1]()
                    elif len(pit) > 1:
                        pit[1][1]()
                        run_pipelined(pit[2:])
                    its = []
                    for (qc0, nq, nkv, ml) in qsl_list:
                        kbl = kbl_full[0:(nkv + 127) // 128]
                        its.append(attn_C(hs, h, qc0, nq, kbl, ml))
                    prev_items = its
            run_pipelined(prev_items)

        if kind == 'prompt':
            nk_total = pos0 + Tt
            qsl = [(tb * 128, 128, pos0 + tb * 128 + 128, True) for tb in range(ntb)]
            heads_loop(qsl, nk_total, [128] * (nk_total // 128))
        else:
            for s2 in range(2):
                for blk in range(16):
                    st2 = xstage[blk % 2]
                    DMA(st2[:, 0:512], cckv[s2, blk * 128:(blk + 1) * 128, :])
                    cb = PT_r.next()
                    cp('pool' if blk % 2 else 'dve', cb[:, 0:512], st2[:, 0:512])
                    pt = psum_alloc(1)
                    pb_ = psb(pt).rearrange("p (j t) -> p j t", t=128)
                    transposes([(pb_[:, j, :], cb[:, j * 128:(j + 1) * 128]) for j in range(4)], identb)
                    evac(ckvTv[:, 0:4, blk * 128:(blk + 1) * 128], pb_[:, 0:4, :])
                st2 = xstage[0]
                DMA(st2[:, 1024:2048].rearrange("p (b d) -> p b d", d=64), ckr[s2, :, :].rearrange("(b p) d -> p b d", p=128))
                kb_ = PT_r.next()
                cp('dve', kb_[:, 0:1024], st2[:, 1024:2048])
                for g0 in range(0, 16, 8):
                    pt = psum_alloc(1)
                    pb_ = psb(pt).rearrange("p (j t) -> p j t", t=128)
                    transposes([(pb_[0:64, j, :], kb_[:, (g0 + j) * 64:(g0 + j + 1) * 64]) for j in range(8)], identb)
                    evac(krT[0:64, g0 * 128:(g0 + 8) * 128].rearrange("p (j t) -> p j t", t=128), pb_[0:64, 0:8, :])
                cp('pool', ckvTv[:, :, PAST:PAST + 64], cnewv[:, :, s2 * 64:(s2 + 1) * 64])
                cp('pool', krT[0:64, PAST:PAST + 64], krnew[0:64, s2 * 64:(s2 + 1) * 64])
                heads_loop([(s2 * 64, 64, PAST + 64, False)], PAST + 64, [128] * 16 + [64])

        if not enabled('l1out'):
            return
        phase('l1out')
        dense_fm('out_c', 8, Tt, lambda ps, c: resid_epilogue(ps[:, 0:Tt], c, Tt))
        if not enabled('all'):
            return
        phase('l1ln1')
        layer_norm(1, 0, Tt)
        phase('l1mlp')
        p_prep(ntb, psrc1)
        mlp(1, Tt)
        phase('l1gate')
        gate_ple(1, Tt, ntb, psrc1, True)
        phase('yout')

        for tb in range(ntb):
            stg = xstage[tb % 2]
            for g in range(4):
                ps = psum_alloc(1)
                transposes([(ps[:, j * 128:(j + 1) * 128], xT32v[:, g * 4 + j, tb * 128:(tb + 1) * 128]) for j in range(4)], identf)
                evac(stg[:, g * 512:(g + 1) * 512], ps[:, 0:512])
            if kind == 'prompt':
                DMA(yp[s, pos0 + tb * 128:pos0 + (tb + 1) * 128, :], stg[:])
            else:
                DMA(ys[:, :], stg[:])


    if enabled('x'):
        for t in tiles:
            run_tile(t)
    dump_all()
    phase('end')
    if cfg.get('marks'):
        import json
        json.dump(marks, open(cfg['marks'], 'w'))
    assert UPTO != 'all' or wst['pos'] == len(PSEQ)

    tr.finalize()
    block = es.enter_context(nc.Block())

    @block.tensor
    def _(e):
        tr.emit('pe', e, esem, dsem)

    @block.scalar
    def _(e):
        tr.emit('act', e, esem, dsem)

    @block.vector
    def _(e):
        tr.emit('dve', e, esem, dsem)

    @block.gpsimd
    def _(e):
        tr.emit('pool', e, esem, dsem)

    @block.sync
    def _(e):
        tr.emit('sp', e, esem, dsem)
        nd = tr.ndma
        for k in range(min(NDSEM, nd)):
            cnt = (nd - 1 - k) // NDSEM + 1
            e.wait_ge(dsem[k], 16 * cnt)
        for f in ENGS:
            if f != 'sp' and tr.count[f] > 0:
                e.wait_ge(esem[f], tr.count[f])
    es.close()
    return nc


def _consts():
    c = {}
    c['c_identb'] = np.eye(128, dtype=np.float32).astype(ml_dtypes.bfloat16)
    c['c_identf'] = np.eye(128, dtype=np.float32)
    c['c_antiI'] = np.ascontiguousarray(np.eye(128, dtype=np.float32)[::-1])
    c['c_ones'] = np.full((128, 128), 1.0 / D, dtype=np.float32)
    NEG = np.float32(-1e30)
    mA = np.zeros((128, 256), np.float32)
    mA[0:64, 192:256] = NEG
    mA[64:128, 0:64] = NEG
    c['c_maskA'] = mA
    mB = np.zeros((128, 640), np.float32)
    mB[0:64, 576:640] = NEG
    mB[64:128, 0:64] = NEG
    c['c_maskB'] = mB
    mC = np.zeros((128, 128), np.float32)
    mC[0:64, 64:128] = NEG
    c['c_maskC'] = mC
    pos = np.arange(2112, dtype=np.float32)

    def tab(d):
        half = d // 2
        inv = (np.float32(10000.0) ** (-np.arange(half, dtype=np.float32) * np.float32(2.0 / d))).astype(np.float32)
        ang = (pos[:, None] * inv[None, :]).astype(np.float32)
        return np.cos(ang).astype(np.float32), np.sin(ang).astype(np.float32)
    ca, sa = tab(128)
    c['c_ropeA'] = np.ascontiguousarray(np.concatenate([ca, sa], 1))
    cc, sc = tab(64)
    c['c_ropeCt'] = np.ascontiguousarray(np.concatenate([cc, sc], 1))
    cf = np.zeros((64, 2, 2112), np.float32)
    cf[0:32, 0] = cc.T
    cf[32:64, 0] = cc.T
    cf[0:32, 1] = -sc.T
    cf[32:64, 1] = sc.T
    c['c_ropeCf'] = (cf * np.float32(SC_C)).astype(np.float32)
    return c


def all_tiles(TM=256):
    tiles = []
    for s in range(2):
        for p in range(0, SEQ, TM):
            tiles.append(('prompt', s, p))
    tiles.append(('sample',))
    return tiles


def core_inputs(inp, c, consts):
    b0, b1 = 2 * c, 2 * c + 2
    f = np.ascontiguousarray
    m = {
        'xp': f(inp['x_prompt'][b0:b1]),
        'xs': f(inp['x_sample'][b0:b1].reshape(128, D)),
        'cak': f(inp['cache_a_k'][0, b0:b1].reshape(2, 128, 256)),
        'cav': f(inp['cache_a_v'][0, b0:b1].reshape(2, 128, 256)),
        'cbk': f(inp['cache_b_k'][0, b0:b1].reshape(2, 512, 1024)),
        'cbv': f(inp['cache_b_v'][0, b0:b1].reshape(2, 512, 1024)),
        'cckv': f(inp['cache_c_kv'][0, b0:b1]),
        'ckr': f(inp['cache_c_krope'][0, b0:b1]),
        'pp': f(inp['p_prompt'][:, b0:b1]),
        'psm': f(inp['p_sample'][:, b0:b1].reshape(2, 128, 256)),
        'w_in_ab': inp['w_in_ab'][0], 'sinks': inp['sinks_a'], 'relb': inp['rel_bias_b'][0],
        'w_out_ab': inp['w_out_ab'][0], 'w_in_c': inp['w_in_c'][0], 'g_q': inp['g_q_c'],
        'w_q_b': inp['w_q_b_c'][0], 'g_kv': inp['g_kv_c'], 'w_kv_b': inp['w_kv_b_c'][0],
        'w_out_c': inp['w_out_c'][0], 'ln1_g': inp['ln1_g'], 'ln1_b': inp['ln1_b'],
        'ln2_g': inp['ln2_g'], 'ln2_b': inp['ln2_b'], 'w_up': inp['w_mlp_up'], 'w_down': inp['w_mlp_down'],
        'w_gate': inp['w_ple_gate'], 'b_gate': inp['b_ple_gate'], 'w_ple': inp['w_ple'],
    }
    m = {k: f(np.asarray(v, dtype=np.float32)) for k, v in m.items()}
    m.update(consts)
    return m


def kernel(**inputs):
    inp = {k: np.asarray(v) for k, v in inputs.items()}
    consts = _consts()
    nc = build(all_tiles())
    in_maps = [core_inputs(inp, c, consts) for c in range(NCORES)]
    res = run_bass_kernel_spmd(nc, in_maps, core_ids=list(range(NCORES)))
    R = res.results

    def cat(name, shape_tail):
        return np.concatenate([np.asarray(R[c][name], dtype=np.float32).reshape((2,) + shape_tail) for c in range(NCORES)], 0)
    y_prompt = cat('yp', (SEQ, D))
    y_sample = cat('ys', (DEC, D))
    pa_k = cat('pak', (128, 2, 128))[None]
    pa_v = cat('pav', (128, 2, 128))[None]
    pb_k = cat('pbk', (512, 8, 128))[None]
    pb_v = cat('pbv', (512, 8, 128))[None]
    pc_kv = cat('pckv', (SEQ, 512))[None]
    pc_kr = cat('pckr', (SEQ, 64))[None]
    sa_k = cat('sak', (128, 2, 128))[None]
    sa_v = cat('sav', (128, 2, 128))[None]
    sb_k = cat('sbk', (512, 8, 128))[None]
    sb_v = cat('sbv', (512, 8, 128))[None]
    sc_kv = cat('sckv', (DEC, 512))[None]
    sc_kr = cat('sckr', (DEC, 64))[None]
    return (y_prompt, y_sample, pa_k, pa_v, pb_k, pb_v, pc_kv, pc_kr, sa_k, sa_v, sb_k, sb_v, sc_kv, sc_kr)
```

```python
import numpy as np
import ml_dtypes
from bisect import bisect_right
from contextlib import ExitStack
import concourse.bass as bass
import concourse.mybir as mybir
from concourse.bass_utils import run_bass_kernel_spmd

F32 = mybir.dt.float32
BF16 = mybir.dt.bfloat16
AF = mybir.ActivationFunctionType
ALU = mybir.AluOpType
AX = mybir.AxisListType

D = 2048
SEQ = 2048
DEC = 64
PAST = 2048
ALPHA = float(4 ** 0.25)
LN_EPS = 1e-5
RMS_EPS = 1e-6
SC_AB = float(128 ** -0.5)
SC_C = float(192 ** -0.5)
NCORES = 8
ESZ = {F32: 4, BF16: 2}
ENGS = ['pe', 'act', 'dve', 'pool', 'sp']
NDSEM = 24
NSLOT = 4
LOOK = 2
PE_ELEMS = 4096


def cap(ap, pattern):
    return bass.AP(ap.tensor, ap.offset, [list(ap.ap[0])] + [list(p) for p in pattern])


class Op:
    __slots__ = ('eng', 'fn', 'deps', 'ms', 'dsem', 'dval', 'isdma', 'has_dep')

    def __init__(self, eng, fn, isdma=False):
        self.eng = eng
        self.fn = fn
        self.deps = set()
        self.ms = None
        self.isdma = isdma
        self.dsem = None
        self.dval = None
        self.has_dep = False


class Tracker:
    def __init__(self):
        self.ops = {e: [] for e in ENGS}
        self.seg = {}
        self.rowb = {}
        self.ndma = 0

    def register(self, name, rowbytes):
        self.seg[name] = ([0], [[0, 1 << 62, None, {}]])
        self.rowb[name] = rowbytes

    def _region(self, ap):
        name = ap.tensor.name
        if name not in self.seg:
            return None
        es = ESZ[ap.dtype]
        pat = ap.ap
        rb = self.rowb[name]
        if rb is None:
            off = ap.offset
            ext = 1
            for st, cn in pat:
                ext += (cn - 1) * abs(st)
        else:
            re_ = rb // es
            off = ap.offset % re_
            ext = 1
            for st, cn in pat[1:]:
                ext += (cn - 1) * abs(st)
            if name == 'psum':
                lo = (off * es) // 2048 * 2048
                hi = ((off + ext) * es + 2047) // 2048 * 2048
                return name, lo, hi
        return name, off * es, (off + ext) * es

    def _access(self, reg, write, op):
        name, lo, hi = reg
        starts, segs = self.seg[name]
        i = bisect_right(starts, lo) - 1
        s = segs[i]
        if s[0] < lo:
            new = [lo, s[1], s[2], dict(s[3])]
            s[1] = lo
            segs.insert(i + 1, new)
            starts.insert(i + 1, lo)
            i += 1
        while i < len(segs) and segs[i][0] < hi:
            s = segs[i]
            if s[1] > hi:
                new = [hi, s[1], s[2], dict(s[3])]
                s[1] = hi
                segs.insert(i + 1, new)
                starts.insert(i + 1, hi)
            if s[2] is not None:
                op.deps.add(s[2])
            if write:
                for r in s[3].values():
                    op.deps.add(r)
                s[2] = op
                s[3] = {}
            else:
                key = ('d', id(op)) if op.isdma else op.eng
                s[3][key] = op
            i += 1

    def op(self, eng, fn, r=(), w=()):
        o = Op(eng, fn)
        for ap in r:
            reg = self._region(ap)
            if reg is not None:
                self._access(reg, reg[0] == 'psum', o)
        for ap in w:
            reg = self._region(ap)
            if reg is not None:
                self._access(reg, True, o)
        o.deps.discard(o)
        self.ops[eng].append(o)
        return o

    def dma(self, out, in_):
        o = Op('sp', (lambda e, out=out, in_=in_: e.dma_start(out=out, in_=in_)), isdma=True)
        o.dsem = self.ndma % NDSEM
        o.dval = 16 * (self.ndma // NDSEM + 1)
        self.ndma += 1
        reg = self._region(in_)
        if reg is not None:
            self._access(reg, False, o)
        reg = self._region(out)
        if reg is not None:
            self._access(reg, True, o)
        o.deps.discard(o)
        self.ops['sp'].append(o)
        return o

    def finalize(self):
        for e in ENGS:
            for o in self.ops[e]:
                for d in o.deps:
                    d.has_dep = True
        self.count = {}
        for e in ENGS:
            c = 0
            for o in self.ops[e]:
                if o.has_dep and not o.isdma:
                    c += 1
                    o.ms = c
            self.count[e] = c

    def emit(self, e, engobj, esem, dsem):
        known = {f: 0 for f in ENGS}
        dknown = {}
        for o in self.ops[e]:
            need = {}
            dneed = {}
            for d in o.deps:
                if d.isdma:
                    if dneed.get(d.dsem, 0) < d.dval:
                        dneed[d.dsem] = d.dval
                else:
                    if d.eng == e and e in ('pe', 'sp'):
                        continue
                    if need.get(d.eng, 0) < d.ms:
                        need[d.eng] = d.ms
            if o.isdma and o.dval > 16:
                if dneed.get(o.dsem, 0) < o.dval - 16:
                    dneed[o.dsem] = o.dval - 16
            for f, v in need.items():
                if known[f] < v:
                    engobj.wait_ge(esem[f], v)
                    known[f] = v
            for k, v in dneed.items():
                if dknown.get(k, 0) < v:
                    engobj.wait_ge(dsem[k], v)
                    dknown[k] = v
            ins = o.fn(engobj)
            if o.isdma:
                ins.then_inc(dsem[o.dsem], 16)
            elif o.ms is not None:
                ins.then_inc(esem[e], 1)


class Ring:
    def __init__(self, items):
        self.items = items
        self.i = 0

    def next(self):
        x = self.items[self.i % len(self.items)]
        self.i += 1
        return x


def panel_specs():
    sp = {
        'in_ab': (18, 4096), 'out_ab': (8, 4096), 'in_c': (6, 4096), 'q_b': (8, 3072), 'kv_b': (8, 2048),
        'out_c': (8, 4096),
    }
    for l in range(2):
        sp['up%d' % l] = (32, 4096)
        sp['dn%d' % l] = (32, 4096)
        sp['gate%d' % l] = (8, 4096)
        sp['ple%d' % l] = (1, 4096)
    return sp


def tile_pseq(kind):
    seq = []
    seq += [('in_ab', i) for i in range(18)] + [('out_ab', i) for i in range(8)]
    seq += [('up0', i) for i in range(32)] + [('dn0', i) for i in range(32)] + [('gate0', i) for i in range(8)]
    seq += [('in_c', i) for i in range(6)]
    for _ in range(2 if kind == 'sample' else 1):
        for hp in range(8):
            seq += [('q_b', hp), ('kv_b', hp)]
    seq += [('out_c', i) for i in range(8)]
    seq += [('up1', i) for i in range(32)] + [('dn1', i) for i in range(32)] + [('gate1', i) for i in range(8)]
    return seq


def build(tiles, TM=256, cfg=None):
    cfg = cfg or {}
    UPTO = cfg.get('upto', 'all')
    ORDER = ['setup', 'pre1', 'prepass', 'x', 'l0in', 'l0attn', 'l0out', 'l0ln1', 'l0mlp', 'l0gate', 'l1in', 'l1heads', 'l1out', 'all']
    def enabled(stage):
        return ORDER.index(stage) <= ORDER.index(UPTO)
    NTBM = TM // 128
    nc = bass.Bass("TRN2", target_bir_lowering=False, dynamic_dma_scratch_size=256)
    tr = Tracker()
    es = ExitStack()

    def din(name, shape, dt=F32):
        return nc.dram_tensor(name, list(shape), dt, kind="ExternalInput").ap()

    def dout(name, shape):
        return nc.dram_tensor(name, list(shape), F32, kind="ExternalOutput").ap()

    xp = din('xp', [2, SEQ, D])
    xs = din('xs', [128, D])
    cak = din('cak', [2, 128, 256])
    cav = din('cav', [2, 128, 256])
    cbk = din('cbk', [2, 512, 1024])
    cbv = din('cbv', [2, 512, 1024])
    cckv = din('cckv', [2, PAST, 512])
    ckr = din('ckr', [2, PAST, 64])
    pp = din('pp', [2, 2, SEQ, 256])
    psm = din('psm', [2, 128, 256])
    w_in_ab = din('w_in_ab', [D, 4608])
    sinks = din('sinks', [1, 8])
    relb = din('relb', [8, 257])
    w_out_ab = din('w_out_ab', [D, D])
    w_in_c = din('w_in_c', [D, 1344])
    g_q = din('g_q', [1, 768])
    w_q_b = din('w_q_b', [768, 3072])
    g_kv = din('g_kv', [1, 512])
    w_kv_b = din('w_kv_b', [512, 4096])
    w_out_c = din('w_out_c', [D, D])
    ln1_g = din('ln1_g', [2, D])
    ln1_b = din('ln1_b', [2, D])
    ln2_g = din('ln2_g', [2, D])
    ln2_b = din('ln2_b', [2, D])
    w_up = din('w_up', [2, D, 8192])
    w_down = din('w_down', [2, 8192, D])
    w_gate = din('w_gate', [2, D, D])
    b_gate = din('b_gate', [2, D])
    w_ple = din('w_ple', [2, 256, D])
    c_identb = din('c_identb', [128, 128], BF16)
    c_identf = din('c_identf', [128, 128])
    c_antiI = din('c_antiI', [128, 128])
    c_ones = din('c_ones', [128, 128])
    c_maskA = din('c_maskA', [128, 256])
    c_maskB = din('c_maskB', [128, 640])
    c_maskC = din('c_maskC', [128, 128])
    c_ropeA = din('c_ropeA', [2112, 128])
    c_ropeCt = din('c_ropeCt', [2112, 64])
    c_ropeCf = din('c_ropeCf', [64, 2, 2112])

    yp = dout('yp', [2, SEQ, D])
    ys = dout('ys', [128, D])
    pak = dout('pak', [2, 128, 256])
    pav = dout('pav', [2, 128, 256])
    pbk = dout('pbk', [2, 512, 1024])
    pbv = dout('pbv', [2, 512, 1024])
    pckv = dout('pckv', [2, SEQ, 512])
    pckr = dout('pckr', [2, SEQ, 64])
    sak = dout('sak', [2, 128, 256])
    sav = dout('sav', [2, 128, 256])
    sbk = dout('sbk', [2, 512, 1024])
    sbv = dout('sbv', [2, 512, 1024])
    sckv = dout('sckv', [128, 512])
    sckr = dout('sckr', [128, 64])

    PSPEC = panel_specs()
    wsc = {}
    for name, (npan, el) in PSPEC.items():
        t = nc.dram_tensor('wsc_' + name, [npan, 128, el], BF16, kind="Internal").ap()
        wsc[name] = t
        tr.register(t.tensor.name, None)
    hh = nc.dram_tensor('hh_scr', [8, 768], F32, kind="Internal").ap()
    tr.register(hh.tensor.name, None)

    def sb(name, n, dt):
        t = es.enter_context(nc.sbuf_tensor(name, [128, n], dt))
        tr.register(name, n * ESZ[dt])
        return t

    identb = sb('identb', 128, BF16)
    identf = sb('identf', 128, F32)
    antiI = sb('antiI', 128, F32)
    onesm = sb('onesm', 128, F32)
    maskA = sb('maskA', 256, F32)
    maskC = sb('maskC', 128, F32)
    BM = sb('BM', 8 * 640, F32)
    lnp = sb('lnp', 160, F32)
    gq_b = sb('gq_b', 768, F32)
    gkv_b = sb('gkv_b', 512, F32)
    sinks_b = sb('sinks_b', 8, F32)
    nsinks_b = sb('nsinks_b', 8, F32)
    ropeA = sb('ropeA', NTBM * 128, F32)
    ropeCt = sb('ropeCt', NTBM * 64, F32)
    ropeCf = sb('ropeCf', 2 * TM, F32)
    xT32 = sb('xT32', 16 * TM, F32)
    A0 = sb('A0', 16 * TM, BF16)
    KB_W = 512 + TM
    kbT = sb('kbT', 8 * KB_W, BF16)
    NVB = 4 + NTBM
    vb = sb('vb', NVB * 1024, BF16)
    KA_W = 128 + TM
    kaT = sb('kaT', 2 * KA_W, BF16)
    NVA = 1 + NTBM
    va = sb('va', NVA * 256, BF16)
    ckvT = sb('ckvT', 4 * 2112, BF16)
    krT = sb('krT', 2112, BF16)
    arena = sb('arena', 16384, BF16)
    wring = sb('wring', NSLOT * PE_ELEMS, BF16)
    Pbuf = [sb('Pbuf%d' % i, 2112, BF16) for i in range(2)]
    PTb = [sb('PTb%d' % i, 1024, BF16) for i in range(4)]
    Dbuf = [sb('Dbuf%d' % i, 512, BF16) for i in range(4)]
    stats = sb('stats', 256, F32)
    hst = [sb('hst%d' % i, 256, F32) for i in range(4)]
    tmpf = [sb('tmpf%d' % i, 256, F32) for i in range(3)]
    tmpb = [sb('tmpb%d' % i, 256, BF16) for i in range(5)]
    xstage_all = sb('xstage', 4096, F32)
    xstage = [xstage_all[:, 0:2048], xstage_all[:, 2048:4096]]
    hc = xstage_all[:, 0:NTBM * 1344]
    lnt = sb('lnt', 6 * TM, F32)
    Sbuf = lnt[:, 0:1024]
    maskB = xstage_all[:, 256:896]
    pT = sb('pT', 2 * TM, BF16)
    cnew = sb('cnew', 4 * 128 + 128, BF16)

    psum = es.enter_context(nc.psum_tensor('psum', [128, 4096], F32))
    tr.register('psum', 4096 * 4)

    esem = {e: es.enter_context(nc.semaphore('es_' + e)) for e in ENGS}
    dsem = [es.enter_context(nc.semaphore('ds%d' % i)) for i in range(NDSEM)]

    def OP(eng, fn, r=(), w=()):
        return tr.op(eng, fn, r, w)

    def DMA(out, in_):
        return tr.dma(out, in_)

    ps_cur = [0]

    def psum_alloc(nb=1):
        if ps_cur[0] + nb > 8:
            ps_cur[0] = 0
        b = ps_cur[0]
        ps_cur[0] += nb
        return psum[:, b * 512:(b + nb) * 512]

    def psum_at(b, nb=1):
        return psum[:, b * 512:(b + nb) * 512]

    pools = {}
    C_SINGLES = [[4, 5, 6]]

    def pool_next(name, choices):
        i = pools.get(name, 0)
        pools[name] = i + 1
        return choices[i % len(choices)]

    pe_cnt = [0]
    marks = []

    def phase(label):
        marks.append((label, pe_cnt[0]))

    def mm_group(out, pairs, start=True, stop=True):
        n = len(pairs)
        pe_cnt[0] += n

        def fn(e, out=out, pairs=pairs):
            ins = None
            for i, (l, r) in enumerate(pairs):
                ins = e.matmul(out, lhsT=l, rhs=r, start=(start and i == 0), stop=(stop and i == n - 1))
            return ins
        rr = []
        for l, r in pairs:
            rr.append(l)
            rr.append(r)
        return OP('pe', fn, r=rr, w=[out])

    def transposes(outs_ins, ident):
        pe_cnt[0] += len(outs_ins)
        def fn(e):
            ins = None
            for o, i in outs_ins:
                k = i.shape[0]
                ins = e.transpose(out=o, in_=i, identity=ident[0:k, 0:k])
            return ins
        return OP('pe', fn, r=[i for _, i in outs_ins] + [ident[:]], w=[o for o, _ in outs_ins])

    def act(out, in_, func, r_extra=(), **kw):
        return OP('act', lambda e: e.activation(out=out, in_=in_, func=func, **kw), r=[in_] + list(r_extra),
                  w=[out] + ([kw['accum_out']] if 'accum_out' in kw else []))

    def tt(eng, out, in0, in1, op):
        return OP(eng, lambda e: e.tensor_tensor(out=out, in0=in0, in1=in1, op=op), r=[in0, in1], w=[out])

    def ts(eng, out, in0, s1, s2, op0, op1=None, r_extra=()):
        if op1 is None:
            return OP(eng, lambda e: e.tensor_scalar(out=out, in0=in0, scalar1=s1, scalar2=None, op0=op0),
                      r=[in0] + list(r_extra), w=[out])
        return OP(eng, lambda e: e.tensor_scalar(out=out, in0=in0, scalar1=s1, scalar2=s2, op0=op0, op1=op1),
                  r=[in0] + list(r_extra), w=[out])

    def stt(out, in0, scalar, in1, op0, op1, r_extra=()):
        return OP('dve', lambda e: e.scalar_tensor_tensor(out=out, in0=in0, scalar=scalar, in1=in1, op0=op0, op1=op1),
                  r=[in0, in1] + list(r_extra), w=[out])

    def cp(eng, out, in_):
        if eng == 'act':
            return OP('act', lambda e: e.activation(out=out, in_=in_, func=AF.Copy), r=[in_], w=[out])
        return OP(eng, lambda e: e.tensor_copy(out=out, in_=in_), r=[in_], w=[out])

    evac_rr = [0]

    def evac(out, in_):
        evac_rr[0] += 1
        return cp('act' if evac_rr[0] % 2 else 'dve', out, in_)

    xT32v = xT32[:].rearrange("p (c t) -> p c t", t=TM)
    A0v = A0[:].rearrange("p (c t) -> p c t", t=TM)
    kbTv = kbT[:].rearrange("p (h k) -> p h k", k=KB_W)
    vbv = vb[:].rearrange("p (b c) -> p b c", c=1024)
    kaTv = kaT[:].rearrange("p (h k) -> p h k", k=KA_W)
    vav = va[:].rearrange("p (b c) -> p b c", c=256)
    ckvTv = ckvT[:].rearrange("p (c k) -> p c k", k=2112)
    BMv = BM[:].rearrange("p (h k) -> p h k", k=640)
    hcv = hc.rearrange("p (b c) -> p b c", c=1344)
    ropeAv = ropeA[:].rearrange("p (b c) -> p b c", c=128)
    ropeCtv = ropeCt[:].rearrange("p (b c) -> p b c", c=64)
    ropeCfv = ropeCf[:].rearrange("p (a t) -> p a t", t=TM)
    pTv = pT[:].rearrange("p (a t) -> p a t", t=TM)
    qaTv = arena[:, 0:8 * TM].rearrange("p (h t) -> p h t", t=TM)
    qbTv = arena[:, 8 * TM:16 * TM].rearrange("p (h t) -> p h t", t=TM)
    hidTv = arena[:, 0:64 * TM].rearrange("p (c t) -> p c t", t=TM)
    plew = arena[:, 12288:16384].rearrange("p (k n) -> p k n", n=2048)
    cqTv = arena[:, 0:6 * TM].rearrange("p (c t) -> p c t", t=TM)
    a0 = 6 * TM
    KhT = [arena[:, a0 + i * 2112: a0 + (i + 1) * 2112] for i in range(2)]
    a0 += 2 * 2112
    Vh = [arena[:, a0 + i * 2176: a0 + (i + 1) * 2176].rearrange("p (b d) -> p b d", d=128) for i in range(2)]
    a0 += 2 * 2176
    qnT = [arena[:, a0 + i * TM: a0 + (i + 1) * TM] for i in range(2)]
    a0 += 2 * TM
    qrT = [arena[:, a0 + i * TM: a0 + (i + 1) * TM] for i in range(2)]
    a0 += 2 * TM
    assert a0 <= 16384
    stage32 = [arena[:, i * 8192:(i + 1) * 8192].bitcast(F32) for i in range(2)]
    cnewv = cnew[:, 0:512].rearrange("p (c t) -> p c t", t=128)
    krnew = cnew[:, 512:640]

    hst_r = Ring(hst)
    tmpf_r = Ring(tmpf)
    tmpb_r = Ring(tmpb)
    P_r = Ring(Pbuf)
    PS_r = Ring([Pbuf[0][:, 0:1024], Pbuf[0][:, 1056:2080], Pbuf[1][:, 0:1024], Pbuf[1][:, 1056:2080]])
    PT_r = Ring(PTb)
    D_r = Ring(Dbuf)
    st_i = [0]

    def stat(n):
        if st_i[0] + n > 256:
            st_i[0] = 0
        a = stats[:, st_i[0]:st_i[0] + n]
        st_i[0] += n
        return a

    def psb(ps):
        return ps.bitcast(BF16)

    def dump_all():
        if UPTO == 'all' and not cfg.get('dump'):
            return
        for nm, t in [('xT32', xT32), ('A0', A0), ('arena', arena), ('kbT', kbT), ('vb', vb), ('kaT', kaT), ('va', va),
                      ('xstage', xstage_all), ('lnp', lnp), ('BM', BM), ('ropeA', ropeA), ('ckvT', ckvT), ('krT', krT),
                      ('lnt', lnt), ('ropeCf', ropeCf), ('ropeCt', ropeCt), ('sinks_b', sinks_b)]:
            ap = t[:]
            d = nc.dram_tensor('dbg_' + nm, list(ap.shape), ap.dtype, kind="ExternalOutput").ap()
            DMA(d[:, :], ap)

    DMA(identb[:], c_identb[:, :])
    DMA(identf[:], c_identf[:, :])
    DMA(antiI[:], c_antiI[:, :])
    DMA(onesm[:], c_ones[:, :])
    DMA(maskA[:], c_maskA[:, :])
    DMA(maskB, c_maskB[:, :])
    DMA(maskC[:], c_maskC[:, :])
    DMA(sinks_b[:], sinks[0:1, :].partition_broadcast(128))
    DMA(gq_b[:], g_q[0:1, :].partition_broadcast(128))
    DMA(gkv_b[:], g_kv[0:1, :].partition_broadcast(128))
    ts('dve', nsinks_b[:], sinks_b[:], -1.0, None, ALU.mult)
    for hb_ in (kbT, vb, kaT, va):
        OP('pool', (lambda e, hb_=hb_: e.memset(hb_[:], 0.0)), w=[hb_[:]])
    vst0 = xstage[0][:, 0:128]
    vst1 = xstage[0][0:32, 128:256]
    for v, src in enumerate([ln1_g, ln1_b, ln2_g, ln2_b]):
        DMA(xstage[0][v * 32:(v + 1) * 32, 0:128], src.rearrange("l (c p) -> (l c) p", p=128))
    DMA(vst1, b_gate.rearrange("l (c p) -> (l c) p", p=128))
    ps = psum_alloc(1)
    transposes([(ps[:, 0:128], vst0), (ps[:, 128:160], vst1)], identf)
    cp('act', lnp[:], ps[:, 0:160])

    def lnp_col(v, l, c):
        i = v * 32 + l * 16 + c
        return lnp[:, i:i + 1]

    tb8 = Sbuf[0:8, 0:257]
    hhs = xstage[1][0:8, 0:768]
    DMA(tb8, relb[:, :])
    cp('dve', hhs[:, 0:256], tb8[:, 1:257])
    cp('dve', hhs[:, 256:768], cap(tb8[:, 256:257], [[0, 512]]))
    DMA(hh[:, :], hhs)
    for h in range(8):
        hk = xstage[h % 2][:, 1024:1664].rearrange("p (c r) -> p c r", r=128)
        DMA(hk, bass.AP(hh.tensor, h * 768, [[1, 128], [128, 5], [1, 128]]))
        ps = psum_alloc(2)
        for c in range(5):
            mm_group(ps[:, (4 - c) * 128:(5 - c) * 128], [(hk[:, c, :], antiI[:])])
        tt('dve', BMv[:, h, :], ps[:, 0:640], maskB, ALU.add)

    cast_rr = [0]

    def cast(out, in_):
        cast_rr[0] += 1
        k = cast_rr[0] % 5
        eng = 'act' if k in (0, 2) else ('dve' if k in (1, 3) else 'pool')
        return cp(eng, out, in_)

    pre_i = [0]

    def cast2(out, in_):
        cast_rr[0] += 1
        return cp('act' if cast_rr[0] % 2 else 'dve', out, in_)

    def prepass_load(i, name, pi, src_ap, el, shape3, qb=False):
        stg = stage32[i % 2]
        if not qb:
            a, b = shape3
            DMA(stg[:, 0:el].rearrange("p (a b) -> p a b", b=b), src_ap)
        else:
            DMA(stg[:, 0:6 * 384].rearrange("p (a b) -> p a b", b=384), src_ap)

    def prepass_cast_store(i, name, pi, src_ap, el, shape3, qb=False):
        stg = stage32[i % 2]
        slot = wring[:, (i % NSLOT) * PE_ELEMS:(i % NSLOT) * PE_ELEMS + el]
        if not qb:
            cast2(slot, stg[:, 0:el])
        else:
            s4 = stg[:, 0:6 * 384].rearrange("p (a h c) -> p a h c", h=2, c=192)
            o4 = slot.rearrange("p (a h c) -> p a h c", h=2, c=256)
            cast2(o4[:, :, :, 0:192], s4)
            cast2(o4[:, :, :, 192:224], s4[:, :, :, 160:192])
            cast2(o4[:, :, :, 224:256], s4[:, :, :, 128:160])
        DMA(wsc[name][pi, :, 0:el], slot)

    def colpanel(w2d, pi, ncols=256, c0=None):
        c0 = pi * 256 if c0 is None else c0
        return w2d[:, c0:c0 + ncols].rearrange("(kc p) n -> p kc n", p=128)

    jobs = []
    for pi in range(18):
        jobs.append(('in_ab', pi, colpanel(w_in_ab, pi), 4096, (16, 256), False))
    for pi in range(8):
        jobs.append(('out_ab', pi, colpanel(w_out_ab, pi), 4096, (16, 256), False))
    for l in range(2):
        if l == 1:
            for pi in range(6):
                ncl = 256 if pi < 5 else 64
                jobs.append(('in_c', pi, colpanel(w_in_c, pi, ncl), 16 * ncl, (16, ncl), False))
            for pi in range(8):
                jobs.append(('q_b', pi, colpanel(w_q_b, pi, 384, pi * 384), 3072, None, True))
            for pi in range(8):
                jobs.append(('kv_b', pi, colpanel(w_kv_b, pi, 512, pi * 512), 2048, (4, 512), False))
            for pi in range(8):
                jobs.append(('out_c', pi, colpanel(w_out_c, pi), 4096, (16, 256), False))
        for pi in range(32):
            jobs.append(('up%d' % l, pi, colpanel(w_up[l], pi), 4096, (16, 256), False))
        for pi in range(32):
            ncx, jh = pi // 2, pi % 2
            src = w_down[l][jh * 4096:(jh + 1) * 4096, ncx * 128:(ncx + 1) * 128].rearrange("(j p) n -> p j n", p=128)
            jobs.append(('dn%d' % l, pi, src, 4096, (32, 128), False))
        for pi in range(8):
            jobs.append(('gate%d' % l, pi, colpanel(w_gate[l], pi), 4096, (16, 256), False))
        jobs.append(('ple%d' % l, 0, w_ple[l].rearrange("(kc p) n -> p kc n", p=128), 4096, (2, 2048), False))
    if 'prejobs' in cfg:
        jobs = [j for j in jobs if j[0] in cfg['prejobs']]
    if not enabled('pre1'):
        jobs = []
    elif not enabled('prepass'):
        jobs = jobs[0:2]
    for i, (nm, pi, src, el, shp, qb) in enumerate(jobs):
        if i == 0:
            prepass_load(0, nm, pi, src, el, shp, qb=qb)
        if i + 1 < len(jobs):
            nm2, pi2, src2, el2, shp2, qb2 = jobs[i + 1]
            prepass_load(i + 1, nm2, pi2, src2, el2, shp2, qb=qb2)
        prepass_cast_store(i, nm, pi, src, el, shp, qb=qb)
    pre_i[0] = len(jobs)

    PSEQ = []
    for t in tiles:
        PSEQ += tile_pseq(t[0])
    wst = {'pos': 0, 'issued': 0}

    def issue_load(j):
        name, pi = PSEQ[j]
        el = PSPEC[name][1]
        if name == 'in_c' and pi == 5:
            el = 1024
        s = (pre_i[0] + j) % NSLOT
        DMA(wring[:, s * PE_ELEMS:s * PE_ELEMS + el], wsc[name][pi, :, 0:el])

    def wget(name, pi):
        i = wst['pos']
        assert PSEQ[i] == (name, pi), (PSEQ[i], name, pi)
        wst['pos'] += 1
        while wst['issued'] <= min(i + LOOK, len(PSEQ) - 1):
            issue_load(wst['issued'])
            wst['issued'] += 1
        s = (pre_i[0] + i) % NSLOT
        return wring[:, s * PE_ELEMS:(s + 1) * PE_ELEMS]

    def rope_tok(src, nh, half, cos, sin, out1_of, out2_of):
        g = nh * 2
        s3 = src.rearrange("p (g c) -> p g c", c=half)
        tA = tmpf_r.next()[:, 0:g * half]
        tB = tmpf_r.next()[:, 0:g * half]
        tt('dve', tA.rearrange("p (g c) -> p g c", c=half), s3, cap(cos, [[0, g], [1, half]]), ALU.mult)
        tt('pool', tB.rearrange("p (g c) -> p g c", c=half), s3, cap(sin, [[0, g], [1, half]]), ALU.mult)
        a4 = tA.rearrange("p (h t c) -> p h t c", t=2, c=half)
        b4 = tB.rearrange("p (h t c) -> p h t c", t=2, c=half)
        tt('dve', out1_of, a4[:, :, 0, :], b4[:, :, 1, :], ALU.subtract)
        tt('pool', out2_of, a4[:, :, 1, :], b4[:, :, 0, :], ALU.add)

    def layer_norm(l, which, Tt):
        usum = lnt[:, 0:Tt]
        qsum = lnt[:, TM:TM + Tt]
        mean = lnt[:, 2 * TM:2 * TM + Tt]
        var = lnt[:, 3 * TM:3 * TM + Tt]
        rstd = lnt[:, 4 * TM:4 * TM + Tt]
        nmr = lnt[:, 5 * TM:5 * TM + Tt]
        ps = psum_alloc(1)
        mm_group(ps[:, 0:Tt], [(onesm[:], usum)])
        mm_group(ps[:, 256:256 + Tt], [(onesm[:], qsum)])
        cp('act', mean, ps[:, 0:Tt])
        tt('dve', usum, mean, mean, ALU.mult)
        stt(var, ps[:, 256:256 + Tt], LN_EPS, usum, ALU.add, ALU.subtract)
        act(var, var, AF.Sqrt)
        OP('dve', lambda e: e.reciprocal(out=rstd, in_=var), r=[var], w=[rstd])
        stt(nmr, mean, -1.0, rstd, ALU.mult, ALU.mult)
        gv, bv = (0, 1) if which == 0 else (2, 3)
        pend_cast = None
        for g4 in range(4):
            xs4 = xT32v[:, g4 * 4:(g4 + 1) * 4, 0:Tt]
            tt('dve', xs4, xs4, cap(rstd, [[0, 4], [1, Tt]]), ALU.mult)
            tt('dve', xs4, xs4, cap(nmr, [[0, 4], [1, Tt]]), ALU.add)
            if pend_cast is not None:
                pend_cast()
            for c in range(g4 * 4, g4 * 4 + 4):
                xc = xT32v[:, c, 0:Tt]
                act(xc, xc, AF.Identity, scale=lnp_col(gv, l, c), bias=lnp_col(bv, l, c),
                    r_extra=[lnp_col(gv, l, c), lnp_col(bv, l, c)])
            pend_cast = (lambda g4=g4, xs4=xs4: cp('dve', A0v[:, g4 * 4:(g4 + 1) * 4, 0:Tt], xs4))
        pend_cast()

    def resid_epilogue(ps_ap, c, Tt):
        xc = xT32v[:, c, 0:Tt]
        usum = lnt[:, 0:Tt]
        qsum = lnt[:, TM:TM + Tt]
        stt(xc, xc, ALPHA, ps_ap, ALU.mult, ALU.add)
        sq = tmpf_r.next()[:, 0:Tt]
        act(sq, xc, AF.Square)
        if c == 0:
            cp('dve', usum, xc)
            cp('pool', qsum, sq)
        else:
            tt('dve', usum, usum, xc, ALU.add)
            tt('pool', qsum, qsum, sq, ALU.add)

    def dense_fm(name, npan, Tt, epi):
        for pi in range(npan):
            slot = wget(name, pi)
            sv = slot.rearrange("p (k n) -> p k n", n=256)
            for j in range(2):
                ps = psum_alloc(1)
                mm_group(ps[:, 0:Tt], [(sv[:, kc, j * 128:(j + 1) * 128], A0v[:, kc, 0:Tt]) for kc in range(16)])
                epi(ps, 2 * pi + j)

    def mlp(l, Tt):
        for pi in range(32):
            slot = wget('up%d' % l, pi)
            sv = slot.rearrange("p (k n) -> p k n", n=256)
            ps = psum_alloc(1)
            for j in range(2):
                mm_group(ps[:, j * 256:j * 256 + Tt], [(sv[:, kc, j * 128:(j + 1) * 128], A0v[:, kc, 0:Tt]) for kc in range(16)])
            rl = lnt[:, (1 + pi % 2) * 2 * TM:(1 + pi % 2) * 2 * TM + 2 * TM].rearrange("p (j t) -> p j t", t=TM)[:, :, 0:Tt]
            p3 = ps.rearrange("p (j t) -> p j t", t=256)[:, :, 0:Tt]
            act(rl, p3, AF.Relu)
            tt('pool' if pi % 2 else 'dve', hidTv[:, 2 * pi:2 * pi + 2, 0:Tt], rl, rl, ALU.mult)
        for ncx in range(16):
            ps = psum_alloc(1)
            for jh in range(2):
                slot = wget('dn%d' % l, ncx * 2 + jh)
                sv = slot.rearrange("p (j n) -> p j n", n=128)
                mm_group(ps[:, 0:Tt], [(sv[:, j, :], hidTv[:, jh * 32 + j, 0:Tt]) for j in range(32)],
                         start=(jh == 0), stop=(jh == 1))
            resid_epilogue(ps[:, 0:Tt], ncx, Tt)
        layer_norm(l, 1, Tt)

    def p_prep(ntb, psrc_of_tb):
        for tb in range(ntb):
            pst_ = hst_r.next()
            pbf_ = tmpb_r.next()
            DMA(pst_[:], psrc_of_tb(tb))
            cp('pool', pbf_[:], pst_[:])
            ps = psum_alloc(1)
            pb_ = psb(ps).rearrange("p (j t) -> p j t", t=128)
            transposes([(pb_[:, j, :], pbf_[:, j * 128:(j + 1) * 128]) for j in range(2)], identb)
            evac(pTv[:, :, tb * 128:(tb + 1) * 128], pb_[:, 0:2, :])

    def gate_ple(l, Tt, ntb, psrc_of_tb, last):
        DMA(plew.rearrange("p k n -> p (k n)"), wsc['ple%d' % l][0, :, :])
        for c8 in range(8):
            slot = wget('gate%d' % l, c8)
            sv = slot.rearrange("p (k n) -> p k n", n=256)
            for j in range(2):
                c = 2 * c8 + j
                ps = psum_alloc(1)
                mm_group(ps[:, 0:Tt], [(sv[:, kc, j * 128:(j + 1) * 128], A0v[:, kc, 0:Tt]) for kc in range(16)])
                mm_group(ps[:, 256:256 + Tt], [(plew[:, kc, c * 128:(c + 1) * 128], pTv[:, kc, 0:Tt]) for kc in range(2)])
                gt = tmpf_r.next()[:, 0:Tt]
                act(gt, ps[:, 0:Tt], AF.Sigmoid, bias=lnp_col(4, l, c), r_extra=[lnp_col(4, l, c)])
                tt('dve', gt, gt, ps[:, 256:256 + Tt], ALU.mult)
                xc = xT32v[:, c, 0:Tt]
                tt('pool', xc, xc, gt, ALU.add)
        if not last:
            for g4 in range(4):
                cp('act' if g4 % 2 else 'dve', A0v[:, g4 * 4:(g4 + 1) * 4, 0:Tt], xT32v[:, g4 * 4:(g4 + 1) * 4, 0:Tt])

    def softmax_tail(nq, den, P, nheads=1):
        rden = stat(nheads)
        OP('dve', lambda e: e.reciprocal(out=rden[0:nq, :], in_=den), r=[den], w=[rden[0:nq, :]])
        Dt = D_r.next()
        if nheads == 1:
            Dv = Dt[0:nq, 0:nq]
            ts('dve', Dv, identb[0:nq, 0:nq], rden[0:nq, 0:1], None, ALU.mult, r_extra=[rden[0:nq, 0:1]])
        else:
            Dv = Dt[0:nq, 0:nheads * nq].rearrange("p (h q) -> p h q", q=nq)
            tt('dve', Dv, cap(identb[0:nq, 0:1], [[0, nheads], [1, nq]]), cap(rden[0:nq, 0:1], [[1, nheads], [0, nq]]), ALU.mult)
        return Dv

    def attn_A(g, qc0, nq, kc0, kbl, maskap):
        nk = sum(k for k, _ in kbl)
        st = {}
        nkb = len(kbl)

        def sA():
            ps = psum_at(pool_next('l0sc', [0, 2]), 2)
            p3 = ps.rearrange("p (h k) -> p h k", k=256)
            for hh in range(4):
                mm_group(p3[0:nq, hh, 0:nk], [(qaTv[:, 4 * g + hh, qc0:qc0 + nq], kaTv[:, g, kc0:kc0 + nk])])
            st['S3'] = p3[0:nq, :, 0:nk]

        def sB():
            S3 = st['S3']
            tt('dve', S3, S3, cap(maskap, [[0, 4], [1, nk]]), ALU.add)
            mx = stat(4)[0:nq, :]
            OP('dve', lambda e: e.tensor_reduce(out=mx, in_=S3, axis=AX.X, op=ALU.max), r=[S3], w=[mx])
            negm = stat(4)[0:nq, :]
            stt(negm, mx, -1.0, nsinks_b[0:nq, 4 * g:4 * g + 4], ALU.mult, ALU.min)
            P = PS_r.next()
            den = stat(4)[0:nq, :]
            for hh in range(4):
                act(P[0:nq, hh * 256:hh * 256 + nk], S3[:, hh, :], AF.Exp, bias=negm[:, hh:hh + 1], scale=1.0,
                    accum_out=den[:, hh:hh + 1], r_extra=[negm[:, hh:hh + 1]])
            tmp = stat(4)[0:nq, :]
            tt('dve', tmp, sinks_b[0:nq, 4 * g:4 * g + 4], negm, ALU.add)
            es_ = stat(4)[0:nq, :]
            act(es_, tmp, AF.Exp)
            st['P'] = P
            st['den'] = den
            st['es'] = es_

        def sC():
            den2 = stat(4)[0:nq, :]
            tt('dve', den2, st['den'], st['es'], ALU.add)
            P = st['P']
            Dv = softmax_tail(nq, den2, P, 4)
            ps = psum_at(4, 2)
            p4 = ps.rearrange("p (h b q) -> p h b q", b=2, q=128)
            off = 0
            for kb, (ks, _) in enumerate(kbl):
                for hh in range(4):
                    mm_group(p4[0:ks, hh, kb, 0:nq], [(P[0:nq, hh * 256 + off:hh * 256 + off + ks], Dv[:, hh, :])])
                off += ks
            PT = PT_r.next()
            PT4 = PT[:].rearrange("p (h b q) -> p h b q", b=2, q=128)
            if all(ks == 128 for ks, _ in kbl):
                cp('act', PT4[:, :, 0:nkb, 0:nq], p4[:, :, 0:nkb, 0:nq])
            else:
                for kb, (ks, _) in enumerate(kbl):
                    cp('act', PT4[0:ks, :, kb, 0:nq], p4[0:ks, :, kb, 0:nq])
            st['PT4'] = PT4

        def sD():
            PT4 = st['PT4']
            po = psum_at(pool_next('l0po', [6, 7]))
            po3 = po.rearrange("p (h q) -> p h q", q=128)
            for hh in range(4):
                mm_group(po3[:, hh, 0:nq], [(vblk[0:ks, g * 128:(g + 1) * 128], PT4[0:ks, hh, kb, 0:nq])
                                           for kb, (ks, vblk) in enumerate(kbl)])
            st['po3'] = po3

        def sE():
            cp('dve', A0v[:, 4 * g:4 * g + 4, qc0:qc0 + nq], st['po3'][:, :, 0:nq])
        return sA, sB, sC, sD, sE

    def attn_B(h, qc0, nq, kc0, kbl, bias_ap):
        nk = sum(k for k, _ in kbl)
        st = {}
        nkb = len(kbl)

        def sA():
            ps = psum_at(pool_next('l0sc', [0, 2]), 2)
            k0 = 0
            while k0 < nk:
                n = min(512, nk - k0)
                mm_group(ps[0:nq, k0:k0 + n], [(qbTv[:, h, qc0:qc0 + nq], kbTv[:, h, kc0 + k0:kc0 + k0 + n])])
                k0 += n
            st['S2'] = ps[0:nq, 0:nk]

        def sB():
            S2 = st['S2']
            tt('dve', S2, S2, bias_ap, ALU.add)
            negm = stat(1)[0:nq, :]
            OP('dve', lambda e: e.tensor_reduce(out=negm, in_=S2, axis=AX.X, op=ALU.max, negate=True), r=[S2], w=[negm])
            P = PS_r.next()
            den = stat(1)[0:nq, :]
            act(P[0:nq, 0:nk], S2, AF.Exp, bias=negm, scale=1.0, accum_out=den, r_extra=[negm])
            st['P'] = P
            st['den'] = den

        def sC():
            P = st['P']
            Dv = softmax_tail(nq, st['den'], P, 1)
            ps = psum_at(4, 2)
            p3 = ps.rearrange("p (b q) -> p b q", q=128)
            off = 0
            for kb, (ks, _) in enumerate(kbl):
                mm_group(p3[0:ks, kb, 0:nq], [(P[0:nq, off:off + ks], Dv)])
                off += ks
            PT = PT_r.next()
            PT3 = PT[:].rearrange("p (b q) -> p b q", q=128)
            if all(ks == 128 for ks, _ in kbl):
                cp('act', PT3[:, 0:nkb, 0:nq], p3[:, 0:nkb, 0:nq])
            else:
                for kb, (ks, _) in enumerate(kbl):
                    cp('act', PT3[0:ks, kb, 0:nq], p3[0:ks, kb, 0:nq])
            st['PT3'] = PT3

        def sD():
            PT3 = st['PT3']
            po = psum_at(pool_next('l0po', [6, 7]))
            mm_group(po[:, 0:nq], [(vblk[0:ks, h * 128:(h + 1) * 128], PT3[0:ks, kb, 0:nq])
                                   for kb, (ks, vblk) in enumerate(kbl)])
            st['po'] = po

        def sE():
            cp('dve', A0v[:, 8 + h, qc0:qc0 + nq], st['po'][:, 0:nq])
        return sA, sB, sC, sD, sE

    def attn_C(hslot, h, qc0, nq, kbl, mask_last):
        nk = sum(kbl)
        st = {}

        def s1():
            nb = (nk + 511) // 512
            ps = (psum_at(pool_next('l1sc', [0, 2]), nb) if nb <= 2 else psum_at(0, nb))
            k0 = 0
            while k0 < nk:
                n = min(512, nk - k0)
                mm_group(ps[0:nq, k0:k0 + n], [(qnT[hslot][:, qc0:qc0 + nq], KhT[hslot][:, k0:k0 + n]),
                                               (qrT[hslot][0:64, qc0:qc0 + nq], krT[0:64, k0:k0 + n])])
                k0 += n
            if mask_last:
                tt('dve', ps[0:nq, nk - 128:nk], ps[0:nq, nk - 128:nk], maskC[0:nq, :], ALU.add)
            negm = stat(1)[0:nq, :]
            pin = ps[0:nq, 0:nk]
            OP('dve', lambda e: e.tensor_reduce(out=negm, in_=pin, axis=AX.X, op=ALU.max, negate=True), r=[pin], w=[negm])
            P = P_r.next()
            den = stat(1)[0:nq, :]
            act(P[0:nq, 0:nk], pin, AF.Exp, bias=negm, scale=1.0, accum_out=den, r_extra=[negm])
            st['P'] = P
            st['den'] = den

        def s2():
            P = st['P']
            Dv = softmax_tail(nq, st['den'], P, 1)
            po = psum_at(7)
            nkb = len(kbl)
            groups = [list(range(g0, min(g0 + 4, nkb))) for g0 in range(0, nkb, 4)]
            pend = None
            for gi, grp in enumerate(groups):
                ps = psum_at(pool_next('l1s', C_SINGLES[0]))
                p3 = ps.rearrange("p (b q) -> p b q", q=128)
                for j, kb in enumerate(grp):
                    ks = kbl[kb]
                    mm_group(p3[0:ks, j, 0:nq], [(P[0:nq, kb * 128:kb * 128 + ks], Dv)])
                PT = PT_r.next()
                PT3 = PT[:, 0:512].rearrange("p (b q) -> p b q", q=128)
                if all(kbl[kb] == 128 for kb in grp):
                    evac(PT3[:, 0:len(grp), 0:nq], p3[:, 0:len(grp), 0:nq])
                else:
                    for j, kb in enumerate(grp):
                        evac(PT3[0:kbl[kb], j, 0:nq], p3[0:kbl[kb], j, 0:nq])
                if pend is not None:
                    pend()

                def pv(grp=grp, PT3=PT3):
                    mm_group(po[:, 0:nq], [(Vh[hslot][0:kbl[kb], kb, :], PT3[0:kbl[kb], j, 0:nq]) for j, kb in enumerate(grp)],
                             start=(grp[0] == 0), stop=(grp[-1] == nkb - 1))
                pend = pv
            pend()
            evac(A0v[:, h, qc0:qc0 + nq], po[:, 0:nq])
        return s1, s2

    def run_pipelined3(items):
        n = len(items)
        for r in range(n + 5):
            for stage, lag in ((2, 3), (0, 0), (3, 4), (1, 1), (4, 5)):
                i = r - lag
                if 0 <= i < n:
                    items[i][stage]()

    def run_pipelined(items, depth=1):
        pend = []
        for s1, s2 in items:
            s1()
            pend.append(s2)
            if len(pend) > depth:
                pend.pop(0)()
        while pend:
            pend.pop(0)()

    def run_tile(tile):
        kind = tile[0]
        if kind == 'prompt':
            _, s, pos0 = tile
            Tt = TM
        else:
            s, pos0, Tt = None, PAST, 128
        ntb = Tt // 128
        first = (kind == 'prompt' and pos0 == 0)
        if kind == 'prompt':
            psrc0 = lambda tb: pp[0, s, pos0 + tb * 128:pos0 + (tb + 1) * 128, :]
            psrc1 = lambda tb: pp[1, s, pos0 + tb * 128:pos0 + (tb + 1) * 128, :]
        else:
            psrc0 = lambda tb: psm[0, :, :]
            psrc1 = lambda tb: psm[1, :, :]

        for tb in range(ntb):
            if kind == 'prompt':
                p = pos0 + tb * 128
                DMA(ropeAv[:, tb, :], c_ropeA[p:p + 128, :])
                DMA(ropeCtv[:, tb, :], c_ropeCt[p:p + 128, :])
            else:
                for s2 in range(2):
                    DMA(ropeAv[s2 * 64:(s2 + 1) * 64, 0, :], c_ropeA[PAST:PAST + 64, :])
                    DMA(ropeCtv[s2 * 64:(s2 + 1) * 64, 0, :], c_ropeCt[PAST:PAST + 64, :])
        if kind == 'prompt':
            DMA(ropeCfv[0:64, :, 0:Tt], c_ropeCf[:, :, pos0:pos0 + Tt])
        else:
            for s2 in range(2):
                DMA(ropeCfv[0:64, :, s2 * 64:(s2 + 1) * 64], c_ropeCf[:, :, PAST:PAST + 64])

        if not enabled('x'):
            return
        phase('x')
        for tb in range(ntb):
            stg = xstage[tb % 2]
            src = xp[s, pos0 + tb * 128:pos0 + (tb + 1) * 128, :] if kind == 'prompt' else xs[:, :]
            DMA(stg[:], src)
            for g in range(4):
                ps = psum_alloc(1)
                transposes([(ps[:, j * 128:(j + 1) * 128], stg[:, (g * 4 + j) * 128:(g * 4 + j + 1) * 128]) for j in range(4)], identf)
                p3 = ps.rearrange("p (j t) -> p j t", t=128)
                cp('act', xT32v[:, g * 4:(g + 1) * 4, tb * 128:(tb + 1) * 128], p3)
                cp('dve', A0v[:, g * 4:(g + 1) * 4, tb * 128:(tb + 1) * 128], p3)

        if not enabled('l0in'):
            return
        phase('l0in')
        pendq = []

        def run_pend(keep):
            while len(pendq) > keep:
                pendq.pop(0)()

        def tr_post(src_bf, dst_fn, scale=None):
            def post():
                pt = psum_alloc(1)
                pb_ = psb(pt).rearrange("p (j t) -> p j t", t=128)
                transposes([(pb_[:, j, :], src_bf[:, j * 128:(j + 1) * 128]) for j in range(2)], identb)
                dst_fn(pb_[:, 0:2, :])
            pendq.append(post)

        for pi in range(18):
            slot = wget('in_ab', pi)
            sv = slot.rearrange("p (k n) -> p k n", n=256)
            for tb in range(ntb):
                tok0 = tb * 128
                ps = psum_alloc(1)
                mm_group(ps[:, 0:256], [(A0v[:, kc, tok0:tok0 + 128], sv[:, kc, :]) for kc in range(16)])
                run_pend(2)
                pin = ps[:, 0:256]
                cosA = ropeAv[:, tb, 0:64]
                sinA = ropeAv[:, tb, 64:128]
                if pi < 4:
                    h_ = hst_r.next()
                    cp('act', h_[:], pin)
                    qrot = tmpb_r.next()
                    q3 = qrot[:].rearrange("p (h c) -> p h c", c=128)
                    rope_tok(h_[:], 2, 64, cosA, sinA, q3[:, :, 0:64], q3[:, :, 64:128])
                    tr_post(qrot, lambda src, pi=pi, tok0=tok0: act(qaTv[:, 2 * pi:2 * pi + 2, tok0:tok0 + 128], src, AF.Copy, scale=SC_AB))
                elif pi == 4:
                    h_ = hst_r.next()
                    cp('act', h_[:], pin)
                    kro = hst_r.next()
                    k3 = kro[:].rearrange("p (h c) -> p h c", c=128)
                    rope_tok(h_[:], 2, 64, cosA, sinA, k3[:, :, 0:64], k3[:, :, 64:128])
                    if kind == 'prompt' and pos0 + tok0 == SEQ - 128:
                        DMA(pak[s, :, :], kro[:])
                    if kind == 'sample':
                        for s2 in range(2):
                            DMA(sak[s2, 64:128, :], kro[s2 * 64:(s2 + 1) * 64, :])
                    kbf = tmpb_r.next()
                    cp('act', kbf[:], kro[:])
                    tr_post(kbf, lambda src, tok0=tok0: evac(kaTv[:, 0:2, 128 + tok0:128 + tok0 + 128], src))
                elif pi == 5:
                    h_ = hst_r.next()
                    cp('act', h_[:], pin)
                    if kind == 'prompt' and pos0 + tok0 == SEQ - 128:
                        DMA(pav[s, :, :], h_[:])
                    if kind == 'sample':
                        for s2 in range(2):
                            DMA(sav[s2, 64:128, :], h_[s2 * 64:(s2 + 1) * 64, :])
                    cp('pool', vav[:, 1 + tb, :], h_[:])
                elif pi < 10:
                    hb = pi - 6
                    qt = tmpb_r.next()
                    act(qt[:], pin, AF.Copy, scale=SC_AB)
                    tr_post(qt, lambda src, hb=hb, tok0=tok0: cp('dve', qbTv[:, 2 * hb:2 * hb + 2, tok0:tok0 + 128], src))
                elif pi < 14:
                    hb = pi - 10
                    h_ = hst_r.next()
                    cp('act', h_[:], pin)
                    if kind == 'prompt' and pos0 + tok0 >= SEQ - 512:
                        r0 = pos0 + tok0 - (SEQ - 512)
                        DMA(pbk[s, r0:r0 + 128, hb * 256:(hb + 1) * 256], h_[:])
                    if kind == 'sample':
                        for s2 in range(2):
                            DMA(sbk[s2, 448:512, hb * 256:(hb + 1) * 256], h_[s2 * 64:(s2 + 1) * 64, :])
                    kt = tmpb_r.next()
                    cp('pool', kt[:], h_[:])
                    tr_post(kt, lambda src, hb=hb, tok0=tok0: cp('dve', kbTv[:, 2 * hb:2 * hb + 2, 512 + tok0:512 + tok0 + 128], src))
                else:
                    hb = pi - 14
                    h_ = hst_r.next()
                    cp('act', h_[:], pin)
                    if kind == 'prompt' and pos0 + tok0 >= SEQ - 512:
                        r0 = pos0 + tok0 - (SEQ - 512)
                        DMA(pbv[s, r0:r0 + 128, hb * 256:(hb + 1) * 256], h_[:])
                    if kind == 'sample':
                        for s2 in range(2):
                            DMA(sbv[s2, 448:512, hb * 256:(hb + 1) * 256], h_[s2 * 64:(s2 + 1) * 64, :])
                    cp('pool', vbv[:, 4 + tb, hb * 256:(hb + 1) * 256], h_[:])
        run_pend(0)

        if not enabled('l0attn'):
            return
        phase('l0attn')
        if kind == 'prompt':
            items = []
            for tb in range(ntb):
                qc0 = tb * 128
                if first and tb == 0:
                    kbl = [(128, vav[:, 1, :])]
                    kc0 = 128
                    mk = maskA[:, 128:256]
                else:
                    kbl = [(128, vav[:, tb, :]), (128, vav[:, tb + 1, :])]
                    kc0 = tb * 128
                    mk = maskA[:, 0:256]
                for g in range(2):
                    items.append(attn_A(g, qc0, 128, kc0, kbl, mk))
            for tb in range(ntb):
                qc0 = tb * 128
                nvalid = min(640, pos0 + tb * 128 + 128)
                nskip = (640 - nvalid) // 128
                kc0 = tb * 128 + nskip * 128
                kblB = [(128, vbv[:, tb + j, :]) for j in range(nskip, 5)]
                for h in range(8):
                    items.append(attn_B(h, qc0, 128, kc0, kblB, BMv[:, h, nskip * 128:640]))
            run_pipelined3(items)
        else:
            for s2 in range(2):
                qc0 = s2 * 64
                stg = xstage[0]
                DMA(stg[:, 0:256], cak[s2, :, :])
                kbf = tmpb_r.next()
                cp('pool', kbf[:], stg[:, 0:256])
                pt = psum_alloc(1)
                pb_ = psb(pt).rearrange("p (j t) -> p j t", t=128)
                transposes([(pb_[:, j, :], kbf[:, j * 128:(j + 1) * 128]) for j in range(2)], identb)
                evac(kaTv[:, 0:2, 0:128], pb_[:, 0:2, :])
                DMA(stg[:, 256:512], cav[s2, :, :])
                cp('pool', vav[:, 0, :], stg[:, 256:512])
                for blk in range(4):
                    st2 = xstage[(blk + 1) % 2]
                    DMA(st2[:, 0:1024], cbk[s2, blk * 128:(blk + 1) * 128, :])
                    kt = PT_r.next()
                    cp('dve' if blk % 2 else 'pool', kt[:], st2[:, 0:1024])
                    pt = psum_alloc(1)
                    pb_ = psb(pt).rearrange("p (j t) -> p j t", t=128)
                    transposes([(pb_[:, j, :], kt[:, j * 128:(j + 1) * 128]) for j in range(8)], identb)
                    evac(kbTv[:, :, blk * 128:(blk + 1) * 128], pb_[:, 0:8, :])
                    DMA(st2[:, 1024:2048], cbv[s2, blk * 128:(blk + 1) * 128, :])
                    cp('pool' if blk % 2 else 'dve', vbv[:, blk, :], st2[:, 1024:2048])
                if s2 == 1:
                    cp('pool', kaTv[:, :, 128:192], kaTv[:, :, 192:256])
                    cp('pool', kbTv[:, :, 512:576], kbTv[:, :, 576:640])
                    DMA(vav[0:64, 2, :], vav[64:128, 1, :])
                    DMA(vbv[0:64, 5, :], vbv[64:128, 4, :])
                vnewA = vav[:, 1, :] if s2 == 0 else vav[:, 2, :]
                vnewB = vbv[:, 4, :] if s2 == 0 else vbv[:, 5, :]
                items = []
                kbl = [(128, vav[:, 0, :]), (64, vnewA)]
                for g in range(2):
                    items.append(attn_A(g, qc0, 64, 0, kbl, maskA[0:64, 0:192]))
                kblB = [(128, vbv[:, j, :]) for j in range(4)] + [(64, vnewB)]
                for h in range(8):
                    items.append(attn_B(h, qc0, 64, 0, kblB, BMv[0:64, h, 0:576]))
                run_pipelined3(items)
                DMA(sak[s2, 0:64, :], cak[s2, 64:128, :])
                DMA(sav[s2, 0:64, :], cav[s2, 64:128, :])
                DMA(sbk[s2, 0:448, :], cbk[s2, 64:512, :])
                DMA(sbv[s2, 0:448, :], cbv[s2, 64:512, :])

        if kind == 'prompt' and pos0 + Tt < SEQ:
            for i in range(512 // Tt):
                cp('pool', kbTv[:, :, i * Tt:(i + 1) * Tt], kbTv[:, :, (i + 1) * Tt:(i + 2) * Tt])
                cp('pool', vbv[:, i * ntb:(i + 1) * ntb, :], vbv[:, (i + 1) * ntb:(i + 2) * ntb, :])
            cp('pool', kaTv[:, :, 0:128], kaTv[:, :, Tt:Tt + 128])
            cp('pool', vav[:, 0, :], vav[:, ntb, :])

        if not enabled('l0out'):
            return
        phase('l0out')
        dense_fm('out_ab', 8, Tt, lambda ps, c: resid_epilogue(ps[:, 0:Tt], c, Tt))
        if not enabled('l0ln1'):
            return
        phase('l0ln1')
        layer_norm(0, 0, Tt)
        phase('l0mlp')
        if not enabled('l0mlp'):
            return
        p_prep(ntb, psrc0)
        mlp(0, Tt)
        if not enabled('l0gate'):
            return
        phase('l0gate')
        gate_ple(0, Tt, ntb, psrc0, False)

        if not enabled('l1in'):
            return
        phase('l1in')
        for pi in range(6):
            ncl = 256 if pi < 5 else 64
            slot = wget('in_c', pi)
            sv = slot[:, 0:16 * ncl].rearrange("p (k n) -> p k n", n=ncl)
            for tb in range(ntb):
                tok0 = tb * 128
                ps = psum_alloc(1)
                mm_group(ps[:, 0:ncl], [(A0v[:, kc, tok0:tok0 + 128], sv[:, kc, :]) for kc in range(16)])
                evac(hcv[:, tb, pi * 256:pi * 256 + ncl], ps[:, 0:ncl])
        for tb in range(ntb):
            tok0 = tb * 128
            ssq = stat(2)
            act(Sbuf[:, 0:768], hcv[:, tb, 0:768], AF.Square, accum_out=ssq[:, 0:1])
            act(Sbuf[:, 0:512], hcv[:, tb, 768:1280], AF.Square, accum_out=ssq[:, 1:2])
            t2 = stat(2)
            ts('dve', t2[:, 0:1], ssq[:, 0:1], 1.0 / 768, RMS_EPS, ALU.mult, ALU.add)
            ts('dve', t2[:, 1:2], ssq[:, 1:2], 1.0 / 512, RMS_EPS, ALU.mult, ALU.add)
            sd = stat(2)
            act(sd, t2, AF.Sqrt)
            rs = stat(2)
            OP('dve', lambda e, rs=rs, sd=sd: e.reciprocal(out=rs, in_=sd), r=[sd], w=[rs])
            cqn = PT_r.next()
            stt(cqn[:, 0:768], hcv[:, tb, 0:768], rs[:, 0:1], gq_b[:], ALU.mult, ALU.mult, r_extra=[rs[:, 0:1]])
            ckv32 = hcv[:, tb, 768:1280]
            stt(ckv32, ckv32, rs[:, 1:2], gkv_b[:], ALU.mult, ALU.mult, r_extra=[rs[:, 1:2]])
            ckvb = PT_r.next()
            cp('pool', ckvb[:, 0:512], ckv32)
            kro = hst_r.next()
            rope_tok(hcv[:, tb, 1280:1344], 1, 32, ropeCtv[:, tb, 0:32], ropeCtv[:, tb, 32:64],
                     kro[:, 0:32].rearrange("p (h c) -> p h c", c=32), kro[:, 32:64].rearrange("p (h c) -> p h c", c=32))
            krb = tmpb_r.next()
            cp('act', krb[:, 0:64], kro[:, 0:64])
            if kind == 'prompt':
                p = pos0 + tok0
                DMA(pckv[s, p:p + 128, :], ckv32)
                DMA(pckr[s, p:p + 128, :], kro[:, 0:64])
            else:
                DMA(sckv[:, :], ckv32)
                DMA(sckr[:, :], kro[:, 0:64])
            pt = psum_alloc(1)
            pb_ = psb(pt).rearrange("p (j t) -> p j t", t=128)
            transposes([(pb_[:, j, :], cqn[:, j * 128:(j + 1) * 128]) for j in range(6)], identb)
            evac(cqTv[:, 0:6, tok0:tok0 + 128], pb_[:, 0:6, :])
            pt = psum_alloc(1)
            pb_ = psb(pt).rearrange("p (j t) -> p j t", t=128)
            transposes([(pb_[:, j, :], ckvb[:, j * 128:(j + 1) * 128]) for j in range(4)] +
                       [(pb_[0:64, 4, :], krb[:, 0:64])], identb)
            if kind == 'prompt':
                p = pos0 + tok0
                evac(ckvTv[:, 0:4, p:p + 128], pb_[:, 0:4, :])
                evac(krT[0:64, p:p + 128], pb_[0:64, 4, :])
            else:
                evac(cnewv[:, 0:4, :], pb_[:, 0:4, :])
                evac(krnew[0:64, :], pb_[0:64, 4, :])

        if not enabled('l1heads'):
            return
        phase('l1heads')
        def heads_loop(qsl_list, nk_total, kbl_full):
            prev_items = None
            C_SINGLES[0] = [5, 6] if nk_total > 2048 else [4, 5, 6]
            for hp in range(8):
                qslot = wget('q_b', hp)
                qv = qslot[:, 0:3072].rearrange("p (k h c) -> p k h c", h=2, c=256)
                kvslot = wget('kv_b', hp)
                kvv = kvslot[:, 0:2048].rearrange("p (k h c) -> p k h c", h=2, c=256)
                for hh in range(2):
                    h = 2 * hp + hh
                    hs = h % 2
                    ps = psum_at(C_SINGLES[0][0], 2)
                    mm_group(ps[:, 0:Tt], [(qv[:, kc, hh, 0:128], cqTv[:, kc, 0:Tt]) for kc in range(6)])
                    mm_group(ps[0:64, 256:256 + Tt], [(qv[:, kc, hh, 128:192], cqTv[:, kc, 0:Tt]) for kc in range(6)])
                    mm_group(ps[0:64, 512:512 + Tt], [(qv[:, kc, hh, 192:256], cqTv[:, kc, 0:Tt]) for kc in range(6)])
                    act(qnT[hs][:, 0:Tt], ps[:, 0:Tt], AF.Copy, scale=SC_C)
                    t1 = tmpf_r.next()[0:64, 0:Tt]
                    t2_ = tmpf_r.next()[0:64, 0:Tt]
                    tt('dve', t1, ps[0:64, 256:256 + Tt], ropeCfv[0:64, 0, 0:Tt], ALU.mult)
                    tt('dve', t2_, ps[0:64, 512:512 + Tt], ropeCfv[0:64, 1, 0:Tt], ALU.mult)
                    tt('pool', qrT[hs][0:64, 0:Tt], t1, t2_, ALU.add)
                    pit = list(prev_items) if prev_items is not None else []
                    coarse = (nk_total <= 1024 and len(pit) == 2)
                    if coarse:
                        pit[0][0]()
                        pit[1][0]()
                    elif pit:
                        pit[0][0]()
                    k0 = 0
                    while k0 < nk_total:
                        n = min(512, nk_total - k0)
                        ps = psum_at(pool_next('l1s', C_SINGLES[0]))
                        mm_group(ps[:, 0:n], [(kvv[:, kc, hh, 0:128], ckvTv[:, kc, k0:k0 + n]) for kc in range(4)])
                        evac(KhT[hs][:, k0:k0 + n], ps[:, 0:n])
                        k0 += n
                    if pit and not coarse:
                        pit[0][1]()
                        for it in pit[1:2]:
                            it[0]()
                    nkb = len(kbl_full)
                    for g0 in range(0, nkb, 4):
                        grp = list(range(g0, min(g0 + 4, nkb)))
                        ps = psum_at(pool_next('l1s', C_SINGLES[0]))
                        p3 = ps.rearrange("p (b d) -> p b d", d=128)
                        for j, kb in enumerate(grp):
                            ks = kbl_full[kb]
                            mm_group(p3[0:ks, j, :], [(ckvTv[:, kc, kb * 128:kb * 128 + ks], kvv[:, kc, hh, 128:256]) for kc in range(4)])
                        if all(kbl_full[kb] == 128 for kb in grp):
                            evac(Vh[hs][:, g0:g0 + len(grp), :], p3[:, 0:len(grp), :])
                        else:
                            for j, kb in enumerate(grp):
                                evac(Vh[hs][0:kbl_full[kb], kb, :], p3[0:kbl_full[kb], j, :])
                    if coarse:
                        pit[0][1]()
                        pit[1][1]()
                    elif len(pit) > 1:
                        pit[1][1]()
                        run_pipelined(pit[2:])
                    its = []
                    for (qc0, nq, nkv, ml) in qsl_list:
                        kbl = kbl_full[0:(nkv + 127) // 128]
                        its.append(attn_C(hs, h, qc0, nq, kbl, ml))
                    prev_items = its
            run_pipelined(prev_items)

        if kind == 'prompt':
            nk_total = pos0 + Tt
            qsl = [(tb * 128, 128, pos0 + tb * 128 + 128, True) for tb in range(ntb)]
            heads_loop(qsl, nk_total, [128] * (nk_total // 128))
        else:
            for s2 in range(2):
                for blk in range(16):
                    st2 = xstage[blk % 2]
                    DMA(st2[:, 0:512], cckv[s2, blk * 128:(blk + 1) * 128, :])
                    cb = PT_r.next()
                    cp('pool' if blk % 2 else 'dve', cb[:, 0:512], st2[:, 0:512])
                    pt = psum_alloc(1)
                    pb_ = psb(pt).rearrange("p (j t) -> p j t", t=128)
                    transposes([(pb_[:, j, :], cb[:, j * 128:(j + 1) * 128]) for j in range(4)], identb)
                    evac(ckvTv[:, 0:4, blk * 128:(blk + 1) * 128], pb_[:, 0:4, :])
                st2 = xstage[0]
                DMA(st2[:, 1024:2048].rearrange("p (b d) -> p b d", d=64), ckr[s2, :, :].rearrange("(b p) d -> p b d", p=128))
                kb_ = PT_r.next()
                cp('dve', kb_[:, 0:1024], st2[:, 1024:2048])
                for g0 in range(0, 16, 8):
                    pt = psum_alloc(1)
                    pb_ = psb(pt).rearrange("p (j t) -> p j t", t=128)
                    transposes([(pb_[0:64, j, :], kb_[:, (g0 + j) * 64:(g0 + j + 1) * 64]) for j in range(8)], identb)
                    evac(krT[0:64, g0 * 128:(g0 + 8) * 128].rearrange("p (j t) -> p j t", t=128), pb_[0:64, 0:8, :])
                cp('pool', ckvTv[:, :, PAST:PAST + 64], cnewv[:, :, s2 * 64:(s2 + 1) * 64])
                cp('pool', krT[0:64, PAST:PAST + 64], krnew[0:64, s2 * 64:(s2 + 1) * 64])
                heads_loop([(s2 * 64, 64, PAST + 64, False)], PAST + 64, [128] * 16 + [64])

        if not enabled('l1out'):
            return
        phase('l1out')
        dense_fm('out_c', 8, Tt, lambda ps, c: resid_epilogue(ps[:, 0:Tt], c, Tt))
        if not enabled('all'):
            return
        phase('l1ln1')
        layer_norm(1, 0, Tt)
        phase('l1mlp')
        p_prep(ntb, psrc1)
        mlp(1, Tt)
        phase('l1gate')
        gate_ple(1, Tt, ntb, psrc1, True)
        phase('yout')

        for tb in range(ntb):
            stg = xstage[tb % 2]
            for g in range(4):
                ps = psum_alloc(1)
                transposes([(ps[:, j * 128:(j + 1) * 128], xT32v[:, g * 4 + j, tb * 128:(tb + 1) * 128]) for j in range(4)], identf)
                evac(stg[:, g * 512:(g + 1) * 512], ps[:, 0:512])
            if kind == 'prompt':
                DMA(yp[s, pos0 + tb * 128:pos0 + (tb + 1) * 128, :], stg[:])
            else:
                DMA(ys[:, :], stg[:])


    if enabled('x'):
        for t in tiles:
            run_tile(t)
    dump_all()
    phase('end')
    if cfg.get('marks'):
        import json
        json.dump(marks, open(cfg['marks'], 'w'))
    assert UPTO != 'all' or wst['pos'] == len(PSEQ)

    tr.finalize()
    block = es.enter_context(nc.Block())

    @block.tensor
    def _(e):
        tr.emit('pe', e, esem, dsem)

    @block.scalar
    def _(e):
        tr.emit('act', e, esem, dsem)

    @block.vector
    def _(e):
        tr.emit('dve', e, esem, dsem)

    @block.gpsimd
    def _(e):
        tr.emit('pool', e, esem, dsem)

    @block.sync
    def _(e):
        tr.emit('sp', e, esem, dsem)
        nd = tr.ndma
        for k in range(min(NDSEM, nd)):
            cnt = (nd - 1 - k) // NDSEM + 1
            e.wait_ge(dsem[k], 16 * cnt)
        for f in ENGS:
            if f != 'sp' and tr.count[f] > 0:
                e.wait_ge(esem[f], tr.count[f])
    es.close()
    return nc


def _consts():
    c = {}
    c['c_identb'] = np.eye(128, dtype=np.float32).astype(ml_dtypes.bfloat16)
    c['c_identf'] = np.eye(128, dtype=np.float32)
    c['c_antiI'] = np.ascontiguousarray(np.eye(128, dtype=np.float32)[::-1])
    c['c_ones'] = np.full((128, 128), 1.0 / D, dtype=np.float32)
    NEG = np.float32(-1e30)
    mA = np.zeros((128, 256), np.float32)
    mA[0:64, 192:256] = NEG
    mA[64:128, 0:64] = NEG
    c['c_maskA'] = mA
    mB = np.zeros((128, 640), np.float32)
    mB[0:64, 576:640] = NEG
    mB[64:128, 0:64] = NEG
    c['c_maskB'] = mB
    mC = np.zeros((128, 128), np.float32)
    mC[0:64, 64:128] = NEG
    c['c_maskC'] = mC
    pos = np.arange(2112, dtype=np.float32)

    def tab(d):
        half = d // 2
        inv = (np.float32(10000.0) ** (-np.arange(half, dtype=np.float32) * np.float32(2.0 / d))).astype(np.float32)
        ang = (pos[:, None] * inv[None, :]).astype(np.float32)
        return np.cos(ang).astype(np.float32), np.sin(ang).astype(np.float32)
    ca, sa = tab(128)
    c['c_ropeA'] = np.ascontiguousarray(np.concatenate([ca, sa], 1))
    cc, sc = tab(64)
    c['c_ropeCt'] = np.ascontiguousarray(np.concatenate([cc, sc], 1))
    cf = np.zeros((64, 2, 2112), np.float32)
    cf[0:32, 0] = cc.T
    cf[32:64, 0] = cc.T
    cf[0:32, 1] = -sc.T
    cf[32:64, 1] = sc.T
    c['c_ropeCf'] = (cf * np.float32(SC_C)).astype(np.float32)
    return c


def all_tiles(TM=256):
    tiles = []
    for s in range(2):
        for p in range(0, SEQ, TM):
            tiles.append(('prompt', s, p))
    tiles.append(('sample',))
    return tiles


def core_inputs(inp, c, consts):
    b0, b1 = 2 * c, 2 * c + 2
    f = np.ascontiguousarray
    m = {
        'xp': f(inp['x_prompt'][b0:b1]),
        'xs': f(inp['x_sample'][b0:b1].reshape(128, D)),
        'cak': f(inp['cache_a_k'][0, b0:b1].reshape(2, 128, 256)),
        'cav': f(inp['cache_a_v'][0, b0:b1].reshape(2, 128, 256)),
        'cbk': f(inp['cache_b_k'][0, b0:b1].reshape(2, 512, 1024)),
        'cbv': f(inp['cache_b_v'][0, b0:b1].reshape(2, 512, 1024)),
        'cckv': f(inp['cache_c_kv'][0, b0:b1]),
        'ckr': f(inp['cache_c_krope'][0, b0:b1]),
        'pp': f(inp['p_prompt'][:, b0:b1]),
        'psm': f(inp['p_sample'][:, b0:b1].reshape(2, 128, 256)),
        'w_in_ab': inp['w_in_ab'][0], 'sinks': inp['sinks_a'], 'relb': inp['rel_bias_b'][0],
        'w_out_ab': inp['w_out_ab'][0], 'w_in_c': inp['w_in_c'][0], 'g_q': inp['g_q_c'],
        'w_q_b': inp['w_q_b_c'][0], 'g_kv': inp['g_kv_c'], 'w_kv_b': inp['w_kv_b_c'][0],
        'w_out_c': inp['w_out_c'][0], 'ln1_g': inp['ln1_g'], 'ln1_b': inp['ln1_b'],
        'ln2_g': inp['ln2_g'], 'ln2_b': inp['ln2_b'], 'w_up': inp['w_mlp_up'], 'w_down': inp['w_mlp_down'],
        'w_gate': inp['w_ple_gate'], 'b_gate': inp['b_ple_gate'], 'w_ple': inp['w_ple'],
    }
    m = {k: f(np.asarray(v, dtype=np.float32)) for k, v in m.items()}
    m.update(consts)
    return m


def kernel(**inputs):
    inp = {k: np.asarray(v) for k, v in inputs.items()}
    consts = _consts()
    nc = build(all_tiles())
    in_maps = [core_inputs(inp, c, consts) for c in range(NCORES)]
    res = run_bass_kernel_spmd(nc, in_maps, core_ids=list(range(NCORES)))
    R = res.results

    def cat(name, shape_tail):
        return np.concatenate([np.asarray(R[c][name], dtype=np.float32).reshape((2,) + shape_tail) for c in range(NCORES)], 0)
    y_prompt = cat('yp', (SEQ, D))
    y_sample = cat('ys', (DEC, D))
    pa_k = cat('pak', (128, 2, 128))[None]
    pa_v = cat('pav', (128, 2, 128))[None]
    pb_k = cat('pbk', (512, 8, 128))[None]
    pb_v = cat('pbv', (512, 8, 128))[None]
    pc_kv = cat('pckv', (SEQ, 512))[None]
    pc_kr = cat('pckr', (SEQ, 64))[None]
    sa_k = cat('sak', (128, 2, 128))[None]
    sa_v = cat('sav', (128, 2, 128))[None]
    sb_k = cat('sbk', (512, 8, 128))[None]
    sb_v = cat('sbv', (512, 8, 128))[None]
    sc_kv = cat('sckv', (DEC, 512))[None]
    sc_kr = cat('sckr', (DEC, 64))[None]
    return (y_prompt, y_sample, pa_k, pa_v, pb_k, pb_v, pc_kv, pc_kr, sa_k, sa_v, sb_k, sb_v, sc_kv, sc_kr)
```
